# Optimizing a Trainium2 kernel written in Bass

```python
import jax, jax.numpy as jnp
from jax import lax
import numpy as np

D_MODEL = 1024
BATCH = 8
SEQ = 2048
DEPTH = 4

N_A = DEPTH // 2
N_B = DEPTH - N_A
HEAD_DIM = 64
MEM_LEN = 256
N_MEM_HEADS = 4
MEM_WIDTH = N_MEM_HEADS * HEAD_DIM
MIX_WIDTH = D_MODEL - MEM_WIDTH
CONV_CH = MIX_WIDTH
CONV_WIDTH = 31
N_FOX_HEADS = MIX_WIDTH // HEAD_DIM
D_FF = 2816
FFN_CONV_WIDTH = 3
BLOCK_Q = 128
RMS_EPS = 1e-6
LN_EPS = 1e-5

kernel_name = "yoco_conformer_fox_hybrid"


def rmsnorm(x, g):
    xf = x.astype(jnp.float32)
    y = xf * lax.rsqrt(jnp.mean(xf * xf, axis=-1, keepdims=True) + RMS_EPS)
    return (y * g.astype(jnp.float32)).astype(x.dtype)


def layernorm(x, g, b):
    xf = x.astype(jnp.float32)
    mu = jnp.mean(xf, axis=-1, keepdims=True)
    var = jnp.mean(jnp.square(xf - mu), axis=-1, keepdims=True)
    y = (xf - mu) * lax.rsqrt(var + LN_EPS)
    return (y * g.astype(jnp.float32) + b.astype(jnp.float32)).astype(x.dtype)


def causal_dwconv(x, w, b):
    width, ch = w.shape
    y = lax.conv_general_dilated(
        x, w[:, None, :].astype(x.dtype), window_strides=(1,),
        padding=((width - 1, 0),), dimension_numbers=("NWC", "WIO", "NWC"),
        feature_group_count=ch)
    return y + b


def conformer_conv(u, b_glu, w_dw, b_dw, ln_g, ln_b):
    u = u + b_glu
    a, gate = jnp.split(u, 2, axis=-1)
    v = a * jax.nn.sigmoid(gate)
    v = causal_dwconv(v, w_dw, b_dw)
    v = layernorm(v, ln_g, ln_b)
    return jax.nn.silu(v)


def memory_attention(q, mem_k, mem_v):
    b, s, _ = q.shape
    qh = q.reshape(b, s, N_MEM_HEADS, HEAD_DIM)
    kh = mem_k.reshape(b, -1, N_MEM_HEADS, HEAD_DIM)
    vh = mem_v.reshape(b, -1, N_MEM_HEADS, HEAD_DIM)
    logits = jnp.einsum("bshd,bmhd->bhsm", qh, kh).astype(jnp.float32) * (HEAD_DIM ** -0.5)
    p = jax.nn.softmax(logits, axis=-1).astype(vh.dtype)
    o = jnp.einsum("bhsm,bmhd->bshd", p, vh)
    return o.reshape(b, s, MEM_WIDTH)


def forgetting_attention(q, k, v, cum_logf):
    b, s, h, dh = q.shape
    scale = dh ** -0.5
    outs = []
    for i in range(s // BLOCK_Q):
        q0 = i * BLOCK_Q
        kend = q0 + BLOCK_Q
        qb = q[:, q0:kend]
        kb = k[:, :kend]
        vb = v[:, :kend]
        logits = jnp.einsum("bqhd,bkhd->bhqk", qb, kb).astype(jnp.float32) * scale
        logits = logits + cum_logf[:, :, q0:kend, None] - cum_logf[:, :, None, :kend]
        qpos = q0 + jnp.arange(BLOCK_Q)
        kpos = jnp.arange(kend)
        logits = jnp.where(kpos[None, :] <= qpos[:, None], logits, -jnp.inf)
        p = jax.nn.softmax(logits, axis=-1).astype(vb.dtype)
        outs.append(jnp.einsum("bhqk,bkhd->bqhd", p, vb))
    return jnp.concatenate(outs, axis=1)


def conv_ffn(h, w_up, w_dw, b_dw, w_down):
    u = h @ w_up
    u = causal_dwconv(u, w_dw, b_dw)
    gate, val = jnp.split(u, 2, axis=-1)
    return (jax.nn.silu(gate) * val) @ w_down


def setup_inputs(seed: int = 0) -> dict:
    key = jax.random.key(seed)
    ks = jax.random.split(key, 24)
    f32 = jnp.float32
    nrm = lambda k, shape, scale: jax.random.normal(k, shape, f32) * scale
    gain = lambda k, shape: 1.0 + 0.05 * jax.random.normal(k, shape, f32)
    d = D_MODEL
    return {
        "x": jax.random.normal(ks[0], (BATCH, SEQ, d), f32),
        "mem": jax.random.normal(ks[1], (BATCH, MEM_LEN, d), f32),
        "g_mix": gain(ks[2], (DEPTH, d)),
        "w_in_a": nrm(ks[3], (N_A, d, 2 * CONV_CH + MEM_WIDTH), d ** -0.5),
        "b_glu": nrm(ks[4], (N_A, 2 * CONV_CH), 0.02),
        "w_dw_a": nrm(ks[5], (N_A, CONV_WIDTH, CONV_CH), CONV_WIDTH ** -0.5),
        "b_dw_a": nrm(ks[6], (N_A, CONV_CH), 0.02),
        "ln_g": gain(ks[7], (N_A, CONV_CH)),
        "ln_b": nrm(ks[8], (N_A, CONV_CH), 0.02),
        "g_kv": gain(ks[9], (d,)),
        "w_kvf": nrm(ks[10], (d, 2 * MIX_WIDTH + N_FOX_HEADS), d ** -0.5),
        "b_f": 2.0 + 0.5 * jax.random.normal(ks[11], (N_FOX_HEADS,), f32),
        "w_in_b": nrm(ks[12], (N_B, d, MIX_WIDTH + MEM_WIDTH), d ** -0.5),
        "g_mem": gain(ks[13], (d,)),
        "w_mem_kv": nrm(ks[14], (DEPTH, d, 2 * MEM_WIDTH), d ** -0.5),
        "w_out": nrm(ks[15], (DEPTH, d, d), d ** -0.5),
        "g_ffn": gain(ks[16], (DEPTH, d)),
        "w_up": nrm(ks[17], (DEPTH, d, 2 * D_FF), d ** -0.5),
        "w_dw_f": nrm(ks[18], (DEPTH, FFN_CONV_WIDTH, 2 * D_FF), FFN_CONV_WIDTH ** -0.5),
        "b_dw_f": nrm(ks[19], (DEPTH, 2 * D_FF), 0.02),
        "w_down": nrm(ks[20], (DEPTH, D_FF, d), D_FF ** -0.5),
        "g_final": gain(ks[21], (d,)),
    }


def reference(x, mem, g_mix, w_in_a, b_glu, w_dw_a, b_dw_a, ln_g, ln_b, g_kv, w_kvf, b_f,
              w_in_b, g_mem, w_mem_kv, w_out, g_ffn, w_up, w_dw_f, b_dw_f, w_down, g_final):
    bsz, seq, _ = x.shape
    mem_n = rmsnorm(mem, g_mem)
    k_sh = v_sh = cum_logf = None
    for l in range(DEPTH):
        h = rmsnorm(x, g_mix[l])
        mem_k, mem_v = jnp.split(mem_n @ w_mem_kv[l], 2, axis=-1)
        if l < N_A:
            p = h @ w_in_a[l]
            u, q_mem = p[..., :2 * CONV_CH], p[..., 2 * CONV_CH:]
            mix = conformer_conv(u, b_glu[l], w_dw_a[l], b_dw_a[l], ln_g[l], ln_b[l])
        else:
            if l == N_A:
                hk = rmsnorm(x, g_kv)
                kvf = hk @ w_kvf
                k_sh = kvf[..., :MIX_WIDTH].reshape(bsz, seq, N_FOX_HEADS, HEAD_DIM)
                v_sh = kvf[..., MIX_WIDTH:2 * MIX_WIDTH].reshape(bsz, seq, N_FOX_HEADS, HEAD_DIM)
                f_logit = (kvf[..., 2 * MIX_WIDTH:] + b_f).astype(jnp.float32)
                cum_logf = jnp.cumsum(jax.nn.log_sigmoid(f_logit), axis=1).transpose(0, 2, 1)
            p = h @ w_in_b[l - N_A]
            q, q_mem = p[..., :MIX_WIDTH], p[..., MIX_WIDTH:]
            q = q.reshape(bsz, seq, N_FOX_HEADS, HEAD_DIM)
            mix = forgetting_attention(q, k_sh, v_sh, cum_logf).reshape(bsz, seq, MIX_WIDTH)
        mem_o = memory_attention(q_mem, mem_k, mem_v)
        x = x + jnp.concatenate([mix, mem_o], axis=-1) @ w_out[l]
        x = x + conv_ffn(rmsnorm(x, g_ffn[l]), w_up[l], w_dw_f[l], b_dw_f[l], w_down[l])
    return rmsnorm(x, g_final)
```

```python
import os
import numpy as np
import concourse.bass as bass
import concourse.mybir as mybir
from concourse.bass_utils import run_bass_kernel_spmd

F32 = mybir.dt.float32
BF16 = mybir.dt.bfloat16
U8 = mybir.dt.uint8
AF = mybir.ActivationFunctionType
ALU = mybir.AluOpType

D = 1024
S = 2048
NT = 4
TW = 512
KC = 8
DFF = 2816
NFF = 22
NH = 12
MEM = 256
ARENA = 212832
SAME_ENG_SYNC = True
ENGS = ['pe', 'act', 'dve', 'pool', 'sp']


class Unit:
    __slots__ = ('name', 'lo', 'hi', 'lastw', 'readers', 'alias', 'ap', 'gen')

    def __init__(self, name, lo=None, hi=None, ap=None):
        self.name = name
        self.lo = lo
        self.hi = hi
        self.ap = ap
        self.lastw = None
        self.readers = []
        self.alias = []
        self.gen = 0


class Op:
    __slots__ = ('eng', 'fn', 'idx', 'waits', 'dwaits', 'is_dma', 'signal', 'sigval',
                 'dslot', 'dval', 'desc')

    def __init__(self, eng, fn, is_dma):
        self.eng = eng
        self.fn = fn
        self.is_dma = is_dma
        self.waits = []
        self.dwaits = []
        self.signal = False
        self.sigval = 0
        self.dslot = 0
        self.dval = 0


class Prog:
    R = 8

    def __init__(self):
        self.streams = {e: [] for e in ENGS}
        self.sb_units = []
        self.known = {e: {f: -1 for f in ENGS} for e in ENGS}
        self.kdma = {e: {} for e in ENGS}
        self.dmaops = {e: [] for e in ENGS}

    def unit(self, name, lo=None, hi=None, ap=None):
        u = Unit(name, lo, hi, ap)
        if lo is not None:
            for v in self.sb_units:
                if v.lo < hi and lo < v.hi:
                    v.alias.append(u)
                    u.alias.append(v)
            self.sb_units.append(u)
        return u

    def op(self, eng, fn, r=(), w=(), dma=False):
        o = Op(eng, fn, dma)
        o.idx = len(self.streams[eng])
        o.desc = ([u.name for u in r], [u.name for u in w])
        need = {}
        dneed = []
        seen = set()

        def add(d, kind):
            if d is None or id(d) in seen:
                return
            if d.is_dma:
                seen.add(id(d))
                dneed.append(d)
                return
            if d.eng == eng:
                if eng == 'pe' or kind == 'war' or not SAME_ENG_SYNC:
                    return
            if need.get(d.eng, -1) < d.idx:
                need[d.eng] = d.idx

        for u in r:
            add(u.lastw, 'raw')
            for v in u.alias:
                add(v.lastw, 'raw')
        for u in w:
            for v in [u] + u.alias:
                add(v.lastw, 'waw')
                for rd in v.readers:
                    add(rd, 'war')
        if dma:
            k = len(self.dmaops[eng])
            o.dslot = k % self.R
            o.dval = 16 * (k // self.R + 1)
            if k >= self.R:
                dneed.append(self.dmaops[eng][k - self.R])
            self.dmaops[eng].append(o)
        for f, idx in need.items():
            if idx > self.known[eng][f]:
                self.known[eng][f] = idx
                dep = self.streams[f][idx]
                dep.signal = True
                o.waits.append(dep)
        for d in dneed:
            key = (d.eng, d.dslot)
            if self.kdma[eng].get(key, 0) < d.dval:
                self.kdma[eng][key] = d.dval
                o.dwaits.append(d)
        for u in r:
            u.readers.append(o)
        for u in w:
            u.lastw = o
            u.readers = []
        self.streams[eng].append(o)
        return o

    def finalize(self):
        for e in ENGS:
            c = 0
            for o in self.streams[e]:
                if o.signal and not o.is_dma:
                    c += 1
                    o.sigval = c

    def emit(self, e, h, sems, dsems):
        for o in self.streams[e]:
            for d in o.waits:
                h.wait_ge(sems[d.eng], d.sigval)
            for d in o.dwaits:
                h.wait_ge(dsems[d.eng][d.dslot], d.dval)
            if o.fn is None:
                continue
            ins = o.fn(h)
            if o.is_dma:
                ins.then_inc(dsems[e][o.dslot], 16)
            elif o.signal:
                ins.then_inc(sems[e], 1)


class Ring:
    def __init__(self, units):
        self.units = units
        self.i = 0

    def next(self):
        u = self.units[self.i % len(self.units)]
        self.i += 1
        u.gen += 1
        return u


def _vec_layout(inp):
    cols = []
    index = {}

    def add(name, arr):
        arr = np.asarray(arr, np.float32)
        index[name] = sum(c.shape[1] for c in cols)
        cols.append(arr)

    def chunked(v, n):
        return np.ascontiguousarray(np.asarray(v, np.float32).reshape(n, 128).T)

    for l in range(4):
        add(('g_mix', l), chunked(inp['g_mix'][l], 8))
        add(('g_ffn', l), chunked(inp['g_ffn'][l], 8))
        wf = np.asarray(inp['w_dw_f'][l], np.float32)
        add(('w_dw_f', l), np.ascontiguousarray(
            wf.T.reshape(44, 128, 3).transpose(1, 0, 2).reshape(128, 132)))
        add(('b_dw_f', l), chunked(inp['b_dw_f'][l], 44))
    for l in range(2):
        add(('b_glu', l), chunked(inp['b_glu'][l], 12))
        wa = np.asarray(inp['w_dw_a'][l], np.float32)
        add(('w_dw_a', l), np.ascontiguousarray(
            wa.T.reshape(6, 128, 31).transpose(1, 0, 2).reshape(128, 186)))
        add(('b_dw_a', l), chunked(inp['b_dw_a'][l], 6))
        add(('ln_g', l), chunked(inp['ln_g'][l], 6))
        add(('ln_b', l), chunked(inp['ln_b'][l], 6))
    add('g_kv', chunked(inp['g_kv'], 8))
    add('g_mem', chunked(inp['g_mem'], 8))
    add('g_final', chunked(inp['g_final'], 8))
    bf = np.zeros((128, 1), np.float32)
    bf[:12, 0] = np.asarray(inp['b_f'], np.float32)
    add('b_f', bf)
    add('eps_rms', np.full((128, 1), 1e-6, np.float32))
    add('eps_ln', np.full((128, 1), 1e-5, np.float32))
    vec = np.concatenate(cols, axis=1)
    return np.ascontiguousarray(vec), index


def _consts():
    k = np.arange(128)[:, None]
    q = np.arange(128)[None, :]
    mask = np.where(k <= q, 0.0, -30000.0).astype(np.float32)
    ident = np.eye(128, dtype=np.float32)
    id12 = np.zeros((128, 12), np.float32)
    id12[:12, :12] = np.eye(12, dtype=np.float32)
    sel = np.zeros((128, 12, 128), np.float32)
    for pc in range(3):
        for h in range(12):
            sel[12 * pc + h, h, :] = 1.0
    cst = np.concatenate([mask, ident, id12, sel.reshape(128, 12 * 128)], axis=1)
    return np.ascontiguousarray(cst)


_VEC_INDEX_CACHE = {}


def build(vec_index, nv, stop=None):
    nc = bass.Bass("TRN2", target_bir_lowering=False)
    dr = {}

    def din(name, shape):
        dr[name] = nc.dram_tensor(name, list(shape), F32, kind="ExternalInput").ap()
        return dr[name]

    xT_d = din("xT", [D, S])
    memT_d = din("memT", [D, MEM])
    vec_d = din("vec", [128, nv])
    NCST = 128 + 128 + 12 + 12 * 128
    cst_d = din("cst", [128, NCST])
    w_in_a_d = din("w_in_a", [2, D, 1792])
    w_kvf_d = din("w_kvf", [D, 1548])
    w_in_b_d = din("w_in_b", [2, D, D])
    w_mem_kv_d = din("w_mem_kv", [4, D, 512])
    w_out_d = din("w_out", [4, D, D])
    w_up_d = din("w_up", [4, D, 2 * DFF])
    w_down_d = din("w_down", [4, DFF, D])
    outT_d = nc.dram_tensor("outT", [D, S], F32, kind="ExternalOutput").ap()
    wsrc = dict(w_in_a=w_in_a_d, w_kvf=w_kvf_d, w_in_b=w_in_b_d, w_mem_kv=w_mem_kv_d, w_out=w_out_d,
                w_up=w_up_d, w_down=w_down_d)
    wbf = {k: nc.dram_tensor(k + "_bf", list(v.shape), BF16).ap() for k, v in wsrc.items()}

    p = Prog()
    WU = {}

    def wsel(d, name, l):
        return d[name] if l is None else d[name][l]

    cast_chain = [None]

    def cast_w(name, l=None, after=()):
        u = p.unit(f"WU_{name}_{l}")
        WU[(name, l)] = u
        src = wsel(wsrc, name, l)
        dst = wsel(wbf, name, l)
        nrow, ncol = src.shape[-2], src.shape[-1]
        step = 2048
        first = True
        for c0 in range(0, ncol, step):
            c1 = min(ncol, c0 + step)
            rows = 128 if (c1 - c0) > 1024 else 256
            for r0 in range(0, nrow, rows):
                r1 = min(nrow, r0 + rows)
                rd = list(after) if first else []
                if first and cast_chain[0] is not None:
                    rd.append(cast_chain[0])
                first = False
                p.op('pool', lambda e, o=dst[r0:r1, c0:c1], i=src[r0:r1, c0:c1]: e.dma_start(out=o, in_=i, max_dma_last_dim=2048),
                     r=rd, w=[u], dma=True)
        cast_chain[0] = u

    with nc.sbuf_tensor("arena", [128, ARENA], U8) as arena_t:
        arena = arena_t[:]

        def view(off, nbytes, dt, pat=None, **kw):
            a = arena[:, off:off + nbytes].bitcast(dt)
            if pat is not None:
                a = a.rearrange(pat, **kw)
            return a

        OX = 0
        OB1 = OX + 65536
        OB2 = OB1 + 32768
        OPH = OB2 + 32768
        OMISC = OPH + 57344
        assert OMISC == 188416

        def mk(name, off, nbytes, dt, pat=None, **kw):
            return p.unit(name, off, off + nbytes, view(off, nbytes, dt, pat, **kw))

        X = [[mk(f"X{c}_{n}", OX + c * 8192 + n * 2048, 2048, F32) for n in range(NT)]
             for c in range(KC)]
        Xc = [view(OX + c * 8192, 8192, F32) for c in range(KC)]
        B1 = [[mk(f"B1_{c}_{n}", OB1 + c * 4096 + n * 1024, 1024, BF16) for n in range(NT)]
              for c in range(KC)]
        B2 = [[mk(f"B2_{c}_{n}", OB2 + c * 4096 + n * 1024, 1024, BF16) for n in range(NT)]
              for c in range(KC)]
        B2c = [view(OB2 + c * 4096, 4096, BF16) for c in range(KC)]
        OU = OB2 + 16384
        SQ = [mk(f"SQ{i}", OU + i * 1024, 1024, BF16) for i in range(2)]
        SD = mk("SD", OU + 2048, 2048, F32)
        RSTD = mk("RSTD", OU + 4096, 2048, F32)
        SQ4 = [mk(f"SQ4_{i}", OU + 4096 + i * 1024, 1024, BF16) for i in range(4)]
        RS4 = [mk(f"RS4_{i}", OU + 8192 + i * 2048, 2048, F32) for i in range(4)]
        TG = [mk(f"TG{i}", OU + i * 2048, 2048, F32) for i in range(3)]
        TV = [mk(f"TV{i}", OU + 6144 + i * 2048, 2048, F32) for i in range(3)]
        HG = [mk(f"HG{i}", OU + 12288 + i * 8, 8, F32) for i in range(2)]
        HV = [mk(f"HV{i}", OU + 12304 + i * 8, 8, F32) for i in range(2)]
        ACTB = [[B2[g][n] for n in range(NT)] for g in range(4)]
        QTA = [mk(f"QTA{j}", OB2 + j * 2048, 1024, BF16) for j in range(6)]
        QTB = [mk(f"QTB{j}", OB2 + j * 2048 + 1024, 1024, BF16) for j in range(6)]
        QTM = [mk(f"QTM{j}", OB2 + 12288 + j * 1024, 1024, BF16) for j in range(2)]
        PTB = [mk(f"PTB{i}", OU + 6144 + i * 1024, 1024, BF16) for i in range(4)]
        RDENB = mk("RDENB", OU + 10240, 2048, F32)
        DGB = [mk(f"DGB{i}", OU + 12288 + i * 512, 512, F32) for i in range(2)]
        LF = mk("LF", OB2, 8192, F32)
        HI = mk("HIp", OB2, 4096, BF16)
        MID = mk("MIDp", OB2 + 4096, 4096, BF16)
        ONESF = mk("ONESF", OB2 + 8192, 8192, F32)
        CC = mk("CC", OB2 + 16384, 8192, F32)
        LO = mk("LOp", OB2 + 24576, 4096, BF16)
        DIAG = [mk(f"DIAG{i}", OPH + i * 7936, 7936, BF16, "p (k m) -> p k m", k=31)
                for i in range(2)]
        oa = OPH + 15872
        Y = [mk(f"Y{j}", oa + j * 2048, 2048, F32) for j in range(6)]
        YB = [mk(f"YB{j}", oa + 12288 + j * 1024, 1024, BF16) for j in range(6)]
        YSQ = [mk(f"YSQ{j}", oa + 18432 + j * 1024, 1024, BF16) for j in range(6)]
        oa += 24576
        MU = mk("MU", oa, 2048, F32)
        T2 = mk("T2", oa + 2048, 2048, F32)
        RSTL = mk("RSTL", oa + 4096, 2048, F32)
        SG = [mk(f"SG{i}", oa + 6144 + i * 2048, 2048, F32) for i in range(2)]
        PTA = [mk(f"PTA{i}", oa + 10240 + i * 1024, 1024, BF16) for i in range(4)]
        RDENA = mk("RDENA", oa + 14336, 2048, F32)
        IDENT = mk("IDENT", oa + 16384, 512, F32)
        assert oa + 16896 <= OMISC
        MEMX = mk("MEMX", OPH, 8192, F32, "p (c m) -> p c m", c=8)
        KT = [[mk(f"KT{j}_{n}", OPH + j * 4096 + n * 1024, 1024, BF16) for n in range(NT)]
              for j in range(6)]
        KTc = [view(OPH + j * 4096, 4096, BF16) for j in range(6)]
        OV = OPH + 24576
        V = [mk(f"V{kb}", OV + kb * 1536, 1536, BF16) for kb in range(16)]
        Vall = view(OV, 24576, BF16, "p (kb c) -> p kb c", kb=16)
        CQ = mk("CQ", OV + 24576, 4096, BF16)
        SEL = mk("SEL", OV + 28672, 3072, BF16, "p (h m) -> p h m", h=12)
        NEGCK = mk("NEGCK", OV + 31744, 768, F32)
        MASK = mk("MASK", OV + 32512, 256, BF16)
        assert OV + 32768 <= OMISC
        om = OMISC
        VEC = mk("VEC", om, nv * 4, F32)
        om += ((nv * 4 + 31) // 32) * 32
        MEMN = mk("MEMN", om, 4096, BF16, "p (c m) -> p c m", c=8)
        om += 4096
        MEMKT = mk("MEMKT", om, 1024, BF16, "p (c m) -> p c m", c=2)
        om += 1024
        MEMV = mk("MEMV", om, 1024, BF16, "p (b c) -> p b c", b=2)
        om += 1024
        ONES = mk("ONES", om, 256, BF16)
        om += 256
        ID12 = mk("ID12", om, 64, F32)
        om += 64
        ONESH = [mk(f"ONESH{i}", om + i * 256, 256, BF16) for i in range(2)]
        om += 512
        NSLOT = (ARENA - om) // 2048
        assert NSLOT >= 6, NSLOT
        slot_units = [mk(f"WS{i}", om + i * 2048, 2048, BF16) for i in range(NSLOT)]
        slots = Ring(slot_units)

        pst = [nc.psum_tensor("psbig", [128, 4096], F32)]
        psbig = pst[0].__enter__()[:]
        PSU = [p.unit(f"PS{i}", ap=psbig[:, i * 512:(i + 1) * 512]) for i in range(8)]
        psg = Ring(PSU[0:4])
        pss3 = Ring(PSU[0:3])
        psacc = Ring(PSU[3:8])
        psall = Ring(PSU)
        ps6 = Ring(PSU[0:6])

        def vcol(key, i=0, n=1, rows=128):
            b = vec_index[key] + i
            return VEC.ap[0:rows, b:b + n]

        def load_w(dram_ap, shape_pat=None, **kw):
            s = slots.next()
            dst = s.ap if shape_pat is None else s.ap.rearrange(shape_pat, **kw)
            return s, dst

        def wjob_k(wd, c0, m):
            s = slots.next()
            dst = s.ap.rearrange("p (k m) -> p k m", k=8)
            if m < 64:
                src = wsel(wsrc, *wd).rearrange("(k p) n -> p k n", p=128)[:, :, c0:c0 + m]
                p.op('pool', lambda e, d=dst[:, :, 0:m], s_=src: e.dma_start(out=d, in_=s_),
                     w=[s], dma=True)
                return s, dst
            src = wsel(wbf, *wd).rearrange("(k p) n -> p k n", p=128)[:, :, c0:c0 + m]
            p.op('sp', lambda e, d=dst[:, :, 0:m], s_=src: e.dma_start(out=d, in_=s_),
                 r=[WU[wd]], w=[s], dma=True)
            return s, dst

        def linear(groups, rhs_fn, tiles, cb, N=TW, ring=None):
            for gi, grp in enumerate(groups):
                sl = [wjob_k(wd, c0, m) + (m,) for (wd, c0, m) in grp]
                for n in tiles:
                    pss = []
                    for (su, sap, m) in sl:
                        ps = (ring or psg).next()
                        for kc in range(KC):
                            rap, ru = rhs_fn(kc, n)
                            p.op('pe', lambda e, o=ps.ap[0:m, 0:N], a=sap[:, kc, 0:m], b=rap,
                                 st=(kc == 0), sp=(kc == KC - 1):
                                 e.matmul(o, lhsT=a, rhs=b, start=st, stop=sp),
                                 r=[su, ru], w=[ps])
                        pss.append(ps)
                    cb(gi, n, pss)

        def rms_tile(srcs, dsts, gkey, N, eps_key='eps_rms', inv=1.0 / D):
            ps = psg.next()
            for c in range(KC):
                sap, su = srcs[c]
                sq = SQ[c % 2]
                p.op('act', lambda e, o=sq.ap[:, 0:N], i=sap: e.activation(out=o, in_=i, func=AF.Square),
                     r=[su], w=[sq])
                p.op('pe', lambda e, o=ps.ap[:, 0:N], b=sq.ap[:, 0:N], st=(c == 0), sp=(c == KC - 1):
                     e.matmul(o, lhsT=ONES.ap, rhs=b, start=st, stop=sp), r=[sq, ONES], w=[ps])
            p.op('act', lambda e, o=SD.ap[:, 0:N], i=ps.ap[:, 0:N]:
                 e.activation(out=o, in_=i, func=AF.Ln, bias=vcol(eps_key), scale=inv),
                 r=[ps, VEC], w=[SD])
            p.op('act', lambda e, o=RSTD.ap[:, 0:N], i=SD.ap[:, 0:N]:
                 e.activation(out=o, in_=i, func=AF.Exp, scale=-0.5),
                 r=[SD], w=[RSTD])
            for c in range(KC):
                sap, su = srcs[c]
                dap, du = dsts[c]
                p.op('dve', lambda e, o=dap, i=sap, g=vcol(gkey, c), rs=RSTD.ap[:, 0:N]:
                     e.scalar_tensor_tensor(out=o, in0=i, scalar=g, in1=rs, op0=ALU.mult, op1=ALU.mult),
                     r=[su, RSTD, VEC], w=[du])

        fused_stats = [False]

        def rmsnorm_x(dst, gkey, c_outer=False):
            if fused_stats[0]:
                fused_stats[0] = False
                rms_apply(dst, gkey, c_outer)
                return
            rms_stats()
            rms_apply(dst, gkey, c_outer)

        def rms_lnexp(n, ps):
            p.op('act', lambda e, o=RS4[n].ap, i=ps.ap:
                 e.activation(out=o, in_=i, func=AF.Ln, bias=vcol('eps_rms'), scale=1.0 / D),
                 r=[ps, VEC], w=[RS4[n]])
            p.op('act', lambda e, o=RS4[n].ap: e.activation(out=o, in_=o, func=AF.Exp, scale=-0.5),
                 r=[RS4[n]], w=[RS4[n]])

        def rms_apply(dst, gkey, c_outer):
            order = [(n, c) for n in range(NT) for c in range(KC)]
            if c_outer:
                order = [(n, c) for c in range(KC) for n in range(NT)]
            for (n, c) in order:
                p.op('dve', lambda e, o=dst[c][n].ap, i=X[c][n].ap, g=vcol(gkey, c), rs=RS4[n].ap:
                     e.scalar_tensor_tensor(out=o, in0=i, scalar=g, in1=rs, op0=ALU.mult, op1=ALU.mult),
                     r=[X[c][n], RS4[n], VEC], w=[dst[c][n]])

        def rms_stats():
            pss = []
            k = 0
            for n in range(NT):
                ps = psg.next()
                pss.append(ps)
                for c in range(KC):
                    sq = SQ4[k % 4]
                    k += 1
                    if c % 2 == 0:
                        p.op('act', lambda e, o=sq.ap, i=X[c][n].ap: e.activation(out=o, in_=i, func=AF.Square),
                             r=[X[c][n]], w=[sq])
                    else:
                        p.op('dve', lambda e, o=sq.ap, i=X[c][n].ap: e.tensor_tensor(out=o, in0=i, in1=i, op=ALU.mult),
                             r=[X[c][n]], w=[sq])
                    p.op('pe', lambda e, o=ps.ap, b=sq.ap, st=(c == 0), sp=(c == KC - 1):
                         e.matmul(o, lhsT=ONES.ap, rhs=b, start=st, stop=sp), r=[sq, ONES], w=[ps])
            for n in range(NT):
                rms_lnexp(n, pss[n])

        def attn_tile(n, pairs, PT, RDEN, DG, depth=2):
            items = []
            for pi, pd in enumerate(pairs):
                nb = len(pd['blocks'])
                for hh in range(2):
                    for bi, blk in enumerate(pd['blocks']):
                        items.append((pi, hh, bi, blk, bi == 0, bi == nb - 1))
            acc = {}

            def scores(it):
                pi, hh, bi, (kb, c0, diag), first, last = it
                pd = pairs[pi]
                pr = slice(64 * hh, 64 * hh + 64)
                sps = pss3.next()
                kap, ku = pd['kt'](hh, kb)
                if pd['h0'] is None:
                    qt_unit = pd['q']
                    p.op('pe', lambda e, o=sps.ap[:, c0:TW], a=kap, b=qt_unit.ap[pr, c0:TW]:
                         e.matmul(o, lhsT=a, rhs=b, start=True, stop=True),
                         r=[ku, qt_unit], w=[sps])
                    bias = None
                else:
                    h = pd['h0'] + hh
                    qt_unit = pd['q'][hh]
                    p.op('pe', lambda e, o=sps.ap[:, c0:TW], a=kap, b=qt_unit.ap[:, c0:TW]:
                         e.matmul(o, lhsT=a, rhs=b, start=True, stop=False),
                         r=[ku, qt_unit], w=[sps])
                    p.op('pe', lambda e, o=sps.ap[:, c0:TW], a=SEL.ap[:, h, :],
                         b=CQ.ap[:, n * TW + c0:(n + 1) * TW]:
                         e.matmul(o, lhsT=a, rhs=b, start=False, stop=True),
                         r=[SEL, CQ], w=[sps])
                    bias = NEGCK.ap[:, kb * 12 + h:kb * 12 + h + 1]
                pt = PT.next()
                if diag:
                    dg = DG.next()
                    p.op('dve', lambda e, o=dg.ap, a=sps.ap[:, c0:c0 + 128]:
                         e.tensor_tensor(out=o, in0=a, in1=MASK.ap, op=ALU.add),
                         r=[sps, MASK], w=[dg])
                    p.op('act', lambda e, o=pt.ap[:, c0:c0 + 128], i=dg.ap, b=bias:
                         e.activation(out=o, in_=i, func=AF.Exp, bias=b),
                         r=[dg, NEGCK], w=[pt])
                    if c0 + 128 < TW:
                        p.op('act', lambda e, o=pt.ap[:, c0 + 128:TW], i=sps.ap[:, c0 + 128:TW], b=bias:
                             e.activation(out=o, in_=i, func=AF.Exp, bias=b),
                             r=[sps, NEGCK], w=[pt])
                elif bias is not None:
                    p.op('act', lambda e, o=pt.ap[:, c0:TW], i=sps.ap[:, c0:TW], b=bias:
                         e.activation(out=o, in_=i, func=AF.Exp, bias=b),
                         r=[sps, NEGCK], w=[pt])
                else:
                    p.op('act', lambda e, o=pt.ap[:, c0:TW], i=sps.ap[:, c0:TW]:
                         e.activation(out=o, in_=i, func=AF.Exp),
                         r=[sps], w=[pt])
                return pt

            def pv(it, pt):
                pi, hh, bi, (kb, c0, diag), first, last = it
                pd = pairs[pi]
                pr = slice(64 * hh, 64 * hh + 64)
                fox = pd['h0'] is not None
                if pi not in acc:
                    acc[pi] = dict(den=psacc.next())
                a_ = acc[pi]
                den = a_['den']
                vap, vu = pd['v'](hh, kb)
                out_unit = pd['out']
                if fox:
                    if hh not in a_:
                        a_[hh] = psacc.next()
                    num = a_[hh]
                    p.op('pe', lambda e, o=num.ap[:, c0:TW], a=vap, b=pt.ap[:, c0:TW], st=first, sp=last:
                         e.matmul(o, lhsT=a, rhs=b, start=st, stop=sp), r=[vu, pt], w=[num])
                    p.op('pe', lambda e, o=den.ap[:, c0:TW], a=ONESH[hh].ap, b=pt.ap[:, c0:TW],
                         st=(first and hh == 0), sp=(last and hh == 1):
                         e.matmul(o, lhsT=a, rhs=b, start=st, stop=sp), r=[ONESH[hh], pt], w=[den])
                    if hh == 1 and last:
                        na, nb_ = a_[0], a_[1]
                        p.op('dve', lambda e, d=den: e.reciprocal(out=RDEN.ap, in_=d.ap), r=[den], w=[RDEN])
                        p.op('dve', lambda e, o=out_unit.ap[0:64, :], nm=na: e.tensor_tensor(out=o, in0=nm.ap[0:64, :], in1=RDEN.ap[0:64, :], op=ALU.mult),
                             r=[na, RDEN], w=[out_unit])
                        p.op('dve', lambda e, o=out_unit.ap[64:128, :], nm=nb_: e.tensor_tensor(out=o, in0=nm.ap[64:128, :], in1=RDEN.ap[64:128, :], op=ALU.mult),
                             r=[nb_, RDEN], w=[out_unit])
                else:
                    if 'num' not in a_:
                        a_['num'] = psacc.next()
                    num = a_['num']
                    p.op('pe', lambda e, o=num.ap[pr, c0:TW], a=vap, b=pt.ap[:, c0:TW], st=first, sp=last:
                         e.matmul(o, lhsT=a, rhs=b, start=st, stop=sp), r=[vu, pt], w=[num])
                    p.op('pe', lambda e, o=den.ap[pr, c0:TW], a=ONES.ap[:, 0:64], b=pt.ap[:, c0:TW], st=first, sp=last:
                         e.matmul(o, lhsT=a, rhs=b, start=st, stop=sp), r=[ONES, pt], w=[den])
                    if hh == 1 and last:
                        p.op('dve', lambda e, d=den: e.reciprocal(out=RDEN.ap, in_=d.ap), r=[den], w=[RDEN])
                        p.op('dve', lambda e, o=out_unit.ap, nm=num: e.tensor_tensor(out=o, in0=nm.ap, in1=RDEN.ap, op=ALU.mult),
                             r=[num, RDEN], w=[out_unit])

            pend = []
            for it in items:
                pend.append((it, scores(it)))
                if len(pend) > depth:
                    pv(*pend.pop(0))
            while pend:
                pv(*pend.pop(0))

        def mem_pairs(qsrc, outs):
            return [dict(q=qsrc[jp],
                         kt=(lambda hh, kb, jp=jp: (MEMKT.ap[64 * hh:64 * hh + 64, jp, kb * 128:(kb + 1) * 128], MEMKT)),
                         v=(lambda hh, kb, jp=jp: (MEMV.ap[:, kb, (2 * jp + hh) * 64:(2 * jp + hh + 1) * 64], MEMV)),
                         blocks=[(0, 0, False), (1, 0, False)], h0=None, out=outs[jp]) for jp in range(2)]

        def mem_kv(l):
            wd = ('w_mem_kv', l)

            def cbk(gi, n, pss):
                p.op('act', lambda e, o=MEMKT.ap[:, gi, :], i=pss[0].ap[:, 0:MEM]:
                     e.activation(out=o, in_=i, func=AF.Copy), r=[pss[0]], w=[MEMKT])
            linear([[(wd, j * 128, 128)] for j in range(2)],
                   lambda kc, n: (MEMN.ap[:, kc, :], MEMN), [0], cbk, N=MEM)
            for cj in range(2):
                su, sap = wjob_k(wd, 256 + cj * 128, 128)
                ps = psg.next()
                for mb in range(2):
                    for kc in range(KC):
                        p.op('pe', lambda e, o=ps.ap[:, mb * 128:(mb + 1) * 128],
                             a=MEMN.ap[:, kc, mb * 128:(mb + 1) * 128], b=sap[:, kc, :],
                             st=(kc == 0), sp=(kc == KC - 1):
                             e.matmul(o, lhsT=a, rhs=b, start=st, stop=sp), r=[su, MEMN], w=[ps])
                p.op('act', lambda e, o=MEMV.ap[:, :, cj * 128:(cj + 1) * 128],
                     i=ps.ap[:, 0:256].rearrange("p (b c) -> p b c", b=2):
                     e.activation(out=o, in_=i, func=AF.Copy), r=[ps], w=[MEMV])

        def wout_residual(l, src):
            pend_q = []
            sqk = [0]

            def cb(gi, n, pss):
                p.op('dve', lambda e, o=X[gi][n].ap, i=pss[0].ap:
                     e.tensor_tensor(out=o, in0=o, in1=i, op=ALU.add), r=[pss[0], X[gi][n]], w=[X[gi][n]])
                sq = SQ4[sqk[0] % 4]
                sqk[0] += 1
                p.op('act', lambda e, o=sq.ap, i=X[gi][n].ap: e.activation(out=o, in_=i, func=AF.Square),
                     r=[X[gi][n]], w=[sq])
                stat = PSU[4 + n]

                def pend_stat(sq=sq, dc=gi, stat=stat):
                    p.op('pe', lambda e, o=stat.ap, b=sq.ap, st=(dc == 0), sp=(dc == KC - 1):
                         e.matmul(o, lhsT=ONES.ap, rhs=b, start=st, stop=sp), r=[sq, ONES], w=[stat])
                pend_q.append(pend_stat)
                while len(pend_q) > 2:
                    pend_q.pop(0)()
            linear([[(('w_out', l), dc * 128, 128)] for dc in range(KC)],
                   lambda kc, n: (src[kc][n].ap, src[kc][n]), range(NT), cb)
            while pend_q:
                pend_q.pop(0)()
            for n in range(NT):
                rms_lnexp(n, PSU[4 + n])
            fused_stats[0] = True

        def ffn(l):
            rmsnorm_x(B1, ('g_ffn', l))
            wu = ('w_up', l)
            wdn = wbf['w_down'][l]
            groups = [list(range(g, min(g + 4, NFF))) for g in range(0, NFF, 4)]
            cnt = [0]
            pend = [None]
            for grp in groups:
                for gi, f in enumerate(grp):
                    sg_u, sg_ap = wjob_k(wu, f * 128, 128)
                    sv_u, sv_ap = wjob_k(wu, DFF + f * 128, 128)
                    wg = vec_index[('w_dw_f', l)] + f * 3
                    wv = vec_index[('w_dw_f', l)] + (NFF + f) * 3
                    bg = vec_index[('b_dw_f', l)] + f
                    bv = vec_index[('b_dw_f', l)] + NFF + f
                    for n in range(NT):
                        banks = (n, 4 + n)
                        for (su, sap, bk) in ((sg_u, sg_ap, banks[0]), (sv_u, sv_ap, banks[1])):
                            for kc in range(KC):
                                p.op('pe', lambda e, o=PSU[bk].ap, a=sap[:, kc, :], b=B1[kc][n].ap,
                                     st=(kc == 0), sp=(kc == KC - 1):
                                     e.matmul(o, lhsT=a, rhs=b, start=st, stop=sp),
                                     r=[su, B1[kc][n]], w=[PSU[bk]])
                        k = cnt[0]
                        cnt[0] += 1
                        tg = TG[k % 3]
                        tv = TV[k % 3]
                        paths = ((banks[0], wg, bg, tg), (banks[1], wv, bv, tv))
                        for (bk, wb, bb, t) in paths:
                            p.op('act', lambda e, o=t.ap, i=PSU[bk].ap, s_=VEC.ap[:, wb + 2:wb + 3], b_=VEC.ap[:, bb:bb + 1]:
                                 e.activation(out=o, in_=i, func=AF.Identity, bias=b_, scale=s_),
                                 r=[PSU[bk], VEC], w=[t])
                        if pend[0] is not None:
                            pend[0]()
                        for tap, sh in ((1, 1), (0, 2)):
                            for (bk, wb, bb, t) in paths:
                                if n == 0:
                                    p.op('dve', lambda e, o=t.ap[:, sh:TW], i=PSU[bk].ap[:, 0:TW - sh], s_=VEC.ap[:, wb + tap:wb + tap + 1]:
                                         e.scalar_tensor_tensor(out=o, in0=i, scalar=s_, in1=o, op0=ALU.mult, op1=ALU.add),
                                         r=[PSU[bk], t, VEC], w=[t])
                                else:
                                    p.op('dve', lambda e, o=t.ap, i=psbig[:, bk * 512 - sh:bk * 512 - sh + TW], s_=VEC.ap[:, wb + tap:wb + tap + 1]:
                                         e.scalar_tensor_tensor(out=o, in0=i, scalar=s_, in1=o, op0=ALU.mult, op1=ALU.add),
                                         r=[PSU[bk], PSU[bk - 1], t, VEC], w=[t])

                        def tail(tg=tg, tv=tv, dst=ACTB[gi][n]):
                            p.op('act', lambda e, o=tg.ap: e.activation(out=o, in_=o, func=AF.Silu), r=[tg], w=[tg])
                            p.op('dve', lambda e, o=dst.ap, a=tg.ap, b=tv.ap:
                                 e.tensor_tensor(out=o, in0=a, in1=b, op=ALU.mult), r=[tg, tv], w=[dst])
                        pend[0] = tail
                pend[0]()
                pend[0] = None
                dsl = []
                for f in grp:
                    s = slots.next()
                    p.op('sp', lambda e, d=s.ap, s_=wdn[f * 128:(f + 1) * 128, :]: e.dma_start(out=d, in_=s_),
                         r=[WU[('w_down', l)]], w=[s], dma=True)
                    dsl.append(s)
                last_grp = (grp is groups[-1])
                ring = ps6 if last_grp else psall
                sqk = 0
                for n in range(NT):
                    stat = PSU[6 + n % 2]
                    pend_q = []
                    for dc in range(KC):
                        ps = ring.next()
                        for gi, s in enumerate(dsl):
                            p.op('pe', lambda e, o=ps.ap, a=s.ap[:, dc * 128:(dc + 1) * 128], b=ACTB[gi][n].ap,
                                 st=(gi == 0), sp=(gi == len(dsl) - 1):
                                 e.matmul(o, lhsT=a, rhs=b, start=st, stop=sp), r=[s, ACTB[gi][n]], w=[ps])
                        while len(pend_q) > 1:
                            pend_q.pop(0)()
                        p.op('dve', lambda e, o=X[dc][n].ap, i=ps.ap:
                             e.tensor_tensor(out=o, in0=o, in1=i, op=ALU.add), r=[ps, X[dc][n]], w=[X[dc][n]])
                        if last_grp:
                            sq = SQ4[sqk % 4]
                            sqk += 1
                            p.op('act', lambda e, o=sq.ap, i=X[dc][n].ap: e.activation(out=o, in_=i, func=AF.Square),
                                 r=[X[dc][n]], w=[sq])

                            def pend_stat(sq=sq, dc=dc, stat=stat):
                                p.op('pe', lambda e, o=stat.ap, b=sq.ap, st=(dc == 0), sp=(dc == KC - 1):
                                     e.matmul(o, lhsT=ONES.ap, rhs=b, start=st, stop=sp), r=[sq, ONES], w=[stat])
                            pend_q.append(pend_stat)
                    while pend_q:
                        pend_q.pop(0)()
                    if last_grp:
                        rms_lnexp(n, stat)
                if last_grp:
                    fused_stats[0] = True

        def mixer_a(l, hook):
            rmsnorm_x(B1, ('g_mix', l))
            wd = ('w_in_a', l)
            bgl = vec_index[('b_glu', l)]
            sgc = [0]

            def cb_glu(gi, n, pss):
                sg = SG[sgc[0] % 2]
                sgc[0] += 1
                p.op('act', lambda e, o=sg.ap, i=pss[1].ap, b=VEC.ap[:, bgl + 6 + gi:bgl + 7 + gi]:
                     e.activation(out=o, in_=i, func=AF.Sigmoid, bias=b), r=[pss[1], VEC], w=[sg])
                p.op('dve', lambda e, o=B2[gi][n].ap, i=pss[0].ap, s_=VEC.ap[:, bgl + gi:bgl + gi + 1], b=sg.ap:
                     e.scalar_tensor_tensor(out=o, in0=i, scalar=s_, in1=b, op0=ALU.add, op1=ALU.mult),
                     r=[pss[0], sg, VEC], w=[B2[gi][n]])
            linear([[(wd, j * 128, 128), (wd, 768 + j * 128, 128)] for j in range(6)],
                   lambda kc, n: (B1[kc][n].ap, B1[kc][n]), range(NT), cb_glu)
            hook([B2[5][NT - 1]])
            mem_kv(l)

            def cb_q(gi, n, pss):
                p.op('act', lambda e, o=B2[6 + gi][n].ap, i=pss[0].ap:
                     e.activation(out=o, in_=i, func=AF.Copy, scale=0.125), r=[pss[0]], w=[B2[6 + gi][n]])
            linear([[(wd, 1536 + j * 128, 128)] for j in range(2)],
                   lambda kc, n: (B1[kc][n].ap, B1[kc][n]), range(NT), cb_q)

            PT = Ring(PTA)
            wb = vec_index[('w_dw_a', l)]
            dgc = [0]
            allp = []
            for n in range(NT):
                allp += mem_pairs([B2[6][n], B2[7][n]], [B1[6][n], B1[7][n]])
            attn_tile(0, allp, PT, RDENA, None)
            def build_diag(j_, slot):
                dgu = DIAG[slot % 2]
                p.op('dve', lambda e, o=dgu.ap, i0=IDENT.ap.unsqueeze(1).to_broadcast([128, 31, 128]),
                     i1=VEC.ap[:, wb + j_ * 31:wb + j_ * 31 + 31].unsqueeze(2).to_broadcast([128, 31, 128]):
                     e.tensor_tensor(out=o, in0=i0, in1=i1, op=ALU.mult),
                     r=[IDENT, VEC], w=[dgu])
            build_diag(0, 0)
            for n in range(NT):
                for j in range(6):
                    dg = DIAG[dgc[0] % 2]
                    dgc[0] += 1
                    nxt_emitted = [False]

                    def emit_next(j=j):
                        if not nxt_emitted[0] and not (n == NT - 1 and j == 5):
                            build_diag((j + 1) % 6, dgc[0])
                        nxt_emitted[0] = True
                    ps = psg.next()
                    order = [30] + list(range(30))
                    for ki, k in enumerate(order):
                        off = n * TW - 30 + k
                        o0 = 0
                        if off < 0:
                            o0 = -off
                            off = 0
                        ln_ = TW - o0
                        ru = [B2[j][n]] + ([B2[j][n - 1]] if n > 0 else [])
                        p.op('pe', lambda e, o=ps.ap[:, o0:TW], a=dg.ap[:, k, :], b=B2c[j][:, off:off + ln_],
                             st=(ki == 0), sp=(ki == 30):
                             e.matmul(o, lhsT=a, rhs=b, start=st, stop=sp), r=[dg] + ru, w=[ps])
                    emit_next()
                    bd = vcol(('b_dw_a', l), j)
                    p.op('act', lambda e, o=Y[j].ap, i=ps.ap, b=bd: e.activation(out=o, in_=i, func=AF.Identity, bias=b),
                         r=[ps, VEC], w=[Y[j]])
                    p.op('act', lambda e, o=YB[j].ap, i=ps.ap, b=bd: e.activation(out=o, in_=i, func=AF.Identity, bias=b),
                         r=[ps, VEC], w=[YB[j]])
                    p.op('act', lambda e, o=YSQ[j].ap, i=ps.ap, b=bd: e.activation(out=o, in_=i, func=AF.Square, bias=b),
                         r=[ps, VEC], w=[YSQ[j]])
                s1 = psg.next()
                s2 = psg.next()
                for j in range(6):
                    p.op('pe', lambda e, b=YB[j].ap, st=(j == 0), sp=(j == 5):
                         e.matmul(s1.ap, lhsT=ONES.ap, rhs=b, start=st, stop=sp), r=[YB[j], ONES], w=[s1])
                for j in range(6):
                    p.op('pe', lambda e, b=YSQ[j].ap, st=(j == 0), sp=(j == 5):
                         e.matmul(s2.ap, lhsT=ONES.ap, rhs=b, start=st, stop=sp), r=[YSQ[j], ONES], w=[s2])
                p.op('dve', lambda e: e.tensor_scalar(out=MU.ap, in0=s1.ap, scalar1=1.0 / 768, scalar2=None, op0=ALU.mult),
                     r=[s1], w=[MU])
                p.op('dve', lambda e: e.tensor_tensor(out=T2.ap, in0=MU.ap, in1=MU.ap, op=ALU.mult), r=[MU], w=[T2])
                p.op('dve', lambda e: e.scalar_tensor_tensor(out=T2.ap, in0=s2.ap, scalar=1.0 / 768, in1=T2.ap,
                                                            op0=ALU.mult, op1=ALU.subtract), r=[s2, T2], w=[T2])
                p.op('act', lambda e: e.activation(out=T2.ap, in_=T2.ap, func=AF.Ln, bias=vcol('eps_ln')),
                     r=[T2, VEC], w=[T2])
                p.op('act', lambda e: e.activation(out=RSTL.ap, in_=T2.ap, func=AF.Exp, scale=-0.5), r=[T2], w=[RSTL])
                for j in range(6):
                    p.op('dve', lambda e, o=Y[j].ap: e.tensor_tensor(out=o, in0=o, in1=MU.ap, op=ALU.subtract),
                         r=[Y[j], MU], w=[Y[j]])
                    p.op('dve', lambda e, o=Y[j].ap: e.tensor_tensor(out=o, in0=o, in1=RSTL.ap, op=ALU.mult),
                         r=[Y[j], RSTL], w=[Y[j]])
                    p.op('act', lambda e, o=B1[j][n].ap, i=Y[j].ap, s_=vcol(('ln_g', l), j), b=vcol(('ln_b', l), j):
                         e.activation(out=o, in_=i, func=AF.Silu, bias=b, scale=s_),
                         r=[Y[j], VEC], w=[B1[j][n]])
            wout_residual(l, B1)

        def kv_pre():
            p.op('pool', lambda e: e.dma_start(out=SEL.ap, in_=cst_d[:, 268:268 + 1536].rearrange("p (h m) -> p h m", h=12)),
                 w=[SEL], dma=True)
            p.op('pool', lambda e: e.dma_start(out=MASK.ap, in_=cst_d[:, 0:128]), w=[MASK], dma=True)
            rmsnorm_x(B1, 'g_kv')

            def cb_f(gi, n, pss):
                p.op('act', lambda e, o=LF.ap[0:12, n * TW:(n + 1) * TW], i=pss[0].ap[0:12, :]:
                     e.activation(out=o, in_=i, func=AF.Sigmoid, bias=vcol('b_f', rows=12)),
                     r=[pss[0], VEC], w=[LF])
                p.op('act', lambda e, o=LF.ap[0:12, n * TW:(n + 1) * TW]:
                     e.activation(out=o, in_=o, func=AF.Ln), r=[LF], w=[LF])
            linear([[(('w_kvf', None), 1536, 12)]], lambda kc, n: (B1[kc][n].ap, B1[kc][n]), range(NT), cb_f)
            p.op('dve', lambda e: e.memset(ONESF.ap[0:12, :], 1.0), w=[ONESF])
            p.op('dve', lambda e: e.tensor_tensor_scan(out=CC.ap[0:12, :], data0=ONESF.ap[0:12, :], data1=LF.ap[0:12, :],
                                                       initial=0.0, op0=ALU.mult, op1=ALU.add),
                 r=[ONESF, LF], w=[CC])
            ps = psg.next()
            for kb in range(16):
                p.op('pe', lambda e, o=ps.ap[:, kb * 12:(kb + 1) * 12], i=CC.ap[0:12, kb * 128:(kb + 1) * 128]:
                     e.transpose(o, i, ID12.ap[0:12, 0:12]), r=[CC, ID12], w=[ps])
            p.op('act', lambda e: e.activation(out=NEGCK.ap, in_=ps.ap[:, 0:192], func=AF.Copy, scale=-1.0),
                 r=[ps], w=[NEGCK])
            R1 = ONESF
            p.op('dve', lambda e: e.tensor_copy(out=HI.ap[0:12, :], in_=CC.ap[0:12, :]), r=[CC], w=[HI])
            p.op('dve', lambda e: e.tensor_tensor(out=R1.ap[0:12, :], in0=CC.ap[0:12, :], in1=HI.ap[0:12, :], op=ALU.subtract),
                 r=[CC, HI], w=[R1])
            p.op('dve', lambda e: e.tensor_copy(out=MID.ap[0:12, :], in_=R1.ap[0:12, :]), r=[R1], w=[MID])
            p.op('dve', lambda e: e.tensor_tensor(out=R1.ap[0:12, :], in0=R1.ap[0:12, :], in1=MID.ap[0:12, :], op=ALU.subtract),
                 r=[R1, MID], w=[R1])
            p.op('dve', lambda e: e.tensor_copy(out=LO.ap[0:12, :], in_=R1.ap[0:12, :]), r=[R1], w=[LO])
            p.op('dve', lambda e: e.memset(CQ.ap, 0.0), w=[CQ])
            for pc, pu in enumerate((HI, MID, LO)):
                p.op('sp', lambda e, o=CQ.ap[12 * pc:12 * pc + 12, :], i=pu.ap[0:12, :]:
                     e.dma_start(out=o, in_=i), r=[pu], w=[CQ], dma=True)
            def cb_k(gi, n, pss):
                p.op('act', lambda e, o=KT[gi][n].ap, i=pss[0].ap: e.activation(out=o, in_=i, func=AF.Copy),
                     r=[pss[0]], w=[KT[gi][n]])
            linear([[(('w_kvf', None), j * 128, 128)] for j in range(6)],
                   lambda kc, n: (B1[kc][n].ap, B1[kc][n]), range(NT), cb_k)
            for cj in range(6):
                su, sap = wjob_k(('w_kvf', None), 768 + cj * 128, 128)
                for n in range(NT):
                    ps = psg.next()
                    for tb in range(4):
                        for kc in range(KC):
                            p.op('pe', lambda e, o=ps.ap[:, tb * 128:(tb + 1) * 128],
                                 a=B1[kc][n].ap[:, tb * 128:(tb + 1) * 128], b=sap[:, kc, :],
                                 st=(kc == 0), sp=(kc == KC - 1):
                                 e.matmul(o, lhsT=a, rhs=b, start=st, stop=sp), r=[su, B1[kc][n]], w=[ps])
                    p.op('act', lambda e, o=Vall[:, 4 * n:4 * n + 4, cj * 128:(cj + 1) * 128],
                         i=ps.ap.rearrange("p (b c) -> p b c", b=4):
                         e.activation(out=o, in_=i, func=AF.Copy), r=[ps], w=[V[4 * n + t] for t in range(4)])


        def mixer_b(l, hook):
            rmsnorm_x(B1, ('g_mix', l))
            mem_kv(l)
            wd = ('w_in_b', l - 2)
            PT = Ring(PTB)
            DG = Ring(DGB)
            for j in range(6):
                p.op('dve', lambda e, j=j: e.memset(QTA[j].ap, 0.0), w=[QTA[j]])
                p.op('dve', lambda e, j=j: e.memset(QTB[j].ap, 0.0), w=[QTB[j]])
            for n in range(NT):
                def cb_q(gi, n_, pss):
                    if gi < 6:
                        p.op('act', lambda e, o=QTA[gi].ap[0:64, :], i=pss[0].ap[0:64, :]:
                             e.activation(out=o, in_=i, func=AF.Copy, scale=0.125), r=[pss[0]], w=[QTA[gi]])
                        p.op('act', lambda e, o=QTB[gi].ap[64:128, :], i=pss[0].ap[64:128, :]:
                             e.activation(out=o, in_=i, func=AF.Copy, scale=0.125), r=[pss[0]], w=[QTB[gi]])
                    else:
                        p.op('act', lambda e, o=QTM[gi - 6].ap, i=pss[0].ap:
                             e.activation(out=o, in_=i, func=AF.Copy, scale=0.125), r=[pss[0]], w=[QTM[gi - 6]])
                linear([[(wd, c * 128, 128)] for c in range(KC)],
                       lambda kc, n_: (B1[kc][n_].ap, B1[kc][n_]), [n], cb_q, ring=pss3)
                if n == 0:
                    hook([QTM[1]])

                def blocks(n=n):
                    out = []
                    for kb in range(4 * n + 4):
                        i = kb - 4 * n
                        out.append((kb, max(0, i) * 128, i >= 0))
                    return out
                pairs = [dict(q=(QTA[j], QTB[j]),
                              kt=(lambda hh, kb, j=j: (KTc[j][:, kb * 128:(kb + 1) * 128], KT[j][kb // 4])),
                              v=(lambda hh, kb, j=j: (Vall[:, kb, j * 128:(j + 1) * 128], V[kb])),
                              blocks=blocks(), h0=2 * j, out=B1[j][n]) for j in range(6)]
                pairs += mem_pairs([QTM[0], QTM[1]], [B1[6][n], B1[7][n]])
                attn_tile(n, pairs, PT, RDENB, DG)
            wout_residual(l, B1)

        p.op('sp', lambda e: e.dma_start(out=VEC.ap, in_=vec_d), w=[VEC], dma=True)
        p.op('sp', lambda e: e.dma_start(out=MEMX.ap, in_=memT_d.rearrange("(c p) m -> p c m", p=128)),
             w=[MEMX], dma=True)
        Xall = view(OX, 65536, F32, "p (c t) -> p c t", c=KC)
        for n in range(NT):
            p.op('sp', lambda e, o=Xall[:, :, n * TW:(n + 1) * TW],
                 i=xT_d.rearrange("(c p) t -> p c t", p=128)[:, :, n * TW:(n + 1) * TW]: e.dma_start(out=o, in_=i),
                 w=[X[c][n] for c in range(KC)], dma=True)
        p.op('sp', lambda e: e.dma_start(out=IDENT.ap, in_=cst_d[:, 128:256]), w=[IDENT], dma=True)
        p.op('sp', lambda e: e.dma_start(out=ID12.ap[:, 0:12], in_=cst_d[:, 256:268]), w=[ID12], dma=True)
        p.op('dve', lambda e: e.memset(ONES.ap, 1.0), w=[ONES])
        for i in range(2):
            p.op('dve', lambda e, i=i: e.memset(ONESH[i].ap[:, 64 * i:64 * i + 64], 1.0), w=[ONESH[i]])
            p.op('dve', lambda e, i=i: e.memset(ONESH[i].ap[:, 64 * (1 - i):64 * (1 - i) + 64], 0.0), w=[ONESH[i]])
        rms_tile([(MEMX.ap[:, c, :], MEMX) for c in range(KC)],
                 [(MEMN.ap[:, c, :], MEMN) for c in range(KC)], 'g_mem', MEM)

        precast = {
            'A0': [('w_out', 0), ('w_up', 0), ('w_down', 0), ('w_mem_kv', 1), ('w_in_a', 1), ('w_out', 1)],
            'A1': [('w_up', 1), ('w_down', 1), ('w_kvf', None), ('w_mem_kv', 2), ('w_in_b', 0), ('w_out', 2)],
            'B2': [('w_up', 2), ('w_down', 2), ('w_mem_kv', 3), ('w_in_b', 1), ('w_out', 3)],
            'B3': [('w_up', 3), ('w_down', 3)],
        }

        def mk_hook(name):
            def hook(after):
                for i, wn in enumerate(precast.get(name, [])):
                    cast_w(*wn, after=(after if i == 0 else ()))
            return hook
        for wn in [('w_in_a', 0), ('w_mem_kv', 0)]:
            cast_w(*wn)
        stages = []
        for l in range(2):
            stages.append((f"A{l}", lambda l=l: mixer_a(l, mk_hook(f"A{l}"))))
            stages.append((f"F{l}", lambda l=l: ffn(l)))
        stages.append(("KV", kv_pre))
        for l in range(2, 4):
            stages.append((f"B{l}", lambda l=l: mixer_b(l, mk_hook(f"B{l}"))))
            stages.append((f"F{l}", lambda l=l: ffn(l)))
        final_norm = True
        for name, fn in stages:
            if stop is not None and stop == 'init':
                final_norm = False
                break
            fn()
            if stop is not None and name == stop:
                final_norm = False
                break
        if final_norm:
            rmsnorm_x(X, 'g_final', c_outer=True)
        OUT = [p.unit(f"OUT{c}") for c in range(KC)]
        for c in range(KC):
            p.op('sp', lambda e, i=Xc[c], o=outT_d[c * 128:(c + 1) * 128, :]: e.dma_start(out=o, in_=i),
                 r=X[c], w=[OUT[c]], dma=True)
        p.op('sp', None, r=OUT)
        p.finalize()

        import contextlib
        with contextlib.ExitStack() as es:
            sems = {e: es.enter_context(nc.semaphore(f"s_{e}")) for e in ENGS}
            dsems = {e: [es.enter_context(nc.semaphore(f"d_{e}{i}")) for i in range(Prog.R)]
                     for e in ('sp', 'pool')}
            block = es.enter_context(nc.Block())

            @block.tensor
            def _(h):
                p.emit('pe', h, sems, dsems)

            @block.scalar
            def _(h):
                p.emit('act', h, sems, dsems)

            @block.vector
            def _(h):
                p.emit('dve', h, sems, dsems)

            @block.gpsimd
            def _(h):
                p.emit('pool', h, sems, dsems)

            @block.sync
            def _(h):
                p.emit('sp', h, sems, dsems)
        for t in reversed(pst):
            t.__exit__(None, None, None)
    return nc, p


_CACHE = {}


def kernel(x, mem, g_mix, w_in_a, b_glu, w_dw_a, b_dw_a, ln_g, ln_b, g_kv, w_kvf, b_f,
           w_in_b, g_mem, w_mem_kv, w_out, g_ffn, w_up, w_dw_f, b_dw_f, w_down, g_final,
           _stop=None, _ncores=8):
    inp = dict(g_mix=g_mix, b_glu=b_glu, w_dw_a=w_dw_a, b_dw_a=b_dw_a, ln_g=ln_g, ln_b=ln_b,
               g_kv=g_kv, b_f=b_f, g_mem=g_mem, g_ffn=g_ffn, w_dw_f=w_dw_f, b_dw_f=b_dw_f,
               g_final=g_final)
    inp = {k: np.asarray(v, np.float32) for k, v in inp.items()}
    vec, vindex = _vec_layout(inp)
    cst = _consts()
    key = (_stop,)
    if key not in _CACHE:
        _CACHE[key] = build(vindex, vec.shape[1], stop=_stop)[0]
    nc = _CACHE[key]
    x = np.asarray(x, np.float32)
    mem = np.asarray(mem, np.float32)
    shared = dict(
        vec=vec, cst=cst,
        w_in_a=np.ascontiguousarray(np.asarray(w_in_a, np.float32)),
        w_kvf=np.ascontiguousarray(np.asarray(w_kvf, np.float32)),
        w_in_b=np.ascontiguousarray(np.asarray(w_in_b, np.float32)),
        w_mem_kv=np.ascontiguousarray(np.asarray(w_mem_kv, np.float32)),
        w_out=np.ascontiguousarray(np.asarray(w_out, np.float32)),
        w_up=np.ascontiguousarray(np.asarray(w_up, np.float32)),
        w_down=np.ascontiguousarray(np.asarray(w_down, np.float32)),
    )
    in_maps = []
    for b in range(_ncores):
        m = dict(shared)
        m["xT"] = np.ascontiguousarray(x[b].T)
        m["memT"] = np.ascontiguousarray(mem[b].T)
        in_maps.append(m)
    res = run_bass_kernel_spmd(nc, in_maps, core_ids=list(range(_ncores)))
    out = np.stack([np.ascontiguousarray(np.asarray(r["outT"], np.float32).T) for r in res.results], axis=0)
    return out
```

```python
import os
import numpy as np
import concourse.bass as bass
import concourse.mybir as mybir
from concourse.bass_utils import run_bass_kernel_spmd

F32 = mybir.dt.float32
BF16 = mybir.dt.bfloat16
U8 = mybir.dt.uint8
AF = mybir.ActivationFunctionType
ALU = mybir.AluOpType

D = 1024
S = 2048
NT = 4
TW = 512
KC = 8
DFF = 2816
NFF = 22
NH = 12
MEM = 256
ARENA = 212832
SAME_ENG_SYNC = True
ENGS = ['pe', 'act', 'dve', 'pool', 'sp']


class Unit:
    __slots__ = ('name', 'lo', 'hi', 'lastw', 'readers', 'alias', 'ap', 'gen')

    def __init__(self, name, lo=None, hi=None, ap=None):
        self.name = name
        self.lo = lo
        self.hi = hi
        self.ap = ap
        self.lastw = None
        self.readers = []
        self.alias = []
        self.gen = 0


class Op:
    __slots__ = ('eng', 'fn', 'idx', 'waits', 'dwaits', 'is_dma', 'signal', 'sigval',
                 'dslot', 'dval', 'desc')

    def __init__(self, eng, fn, is_dma):
        self.eng = eng
        self.fn = fn
        self.is_dma = is_dma
        self.waits = []
        self.dwaits = []
        self.signal = False
        self.sigval = 0
        self.dslot = 0
        self.dval = 0


class Prog:
    R = 8

    def __init__(self):
        self.streams = {e: [] for e in ENGS}
        self.sb_units = []
        self.known = {e: {f: -1 for f in ENGS} for e in ENGS}
        self.kdma = {e: {} for e in ENGS}
        self.dmaops = {e: [] for e in ENGS}

    def unit(self, name, lo=None, hi=None, ap=None):
        u = Unit(name, lo, hi, ap)
        if lo is not None:
            for v in self.sb_units:
                if v.lo < hi and lo < v.hi:
                    v.alias.append(u)
                    u.alias.append(v)
            self.sb_units.append(u)
        return u

    def op(self, eng, fn, r=(), w=(), dma=False):
        o = Op(eng, fn, dma)
        o.idx = len(self.streams[eng])
        o.desc = ([u.name for u in r], [u.name for u in w])
        need = {}
        dneed = []
        seen = set()

        def add(d, kind):
            if d is None or id(d) in seen:
                return
            if d.is_dma:
                seen.add(id(d))
                dneed.append(d)
                return
            if d.eng == eng:
                if eng == 'pe' or kind == 'war' or not SAME_ENG_SYNC:
                    return
            if need.get(d.eng, -1) < d.idx:
                need[d.eng] = d.idx

        for u in r:
            add(u.lastw, 'raw')
            for v in u.alias:
                add(v.lastw, 'raw')
        for u in w:
            for v in [u] + u.alias:
                add(v.lastw, 'waw')
                for rd in v.readers:
                    add(rd, 'war')
        if dma:
            k = len(self.dmaops[eng])
            o.dslot = k % self.R
            o.dval = 16 * (k // self.R + 1)
            if k >= self.R:
                dneed.append(self.dmaops[eng][k - self.R])
            self.dmaops[eng].append(o)
        for f, idx in need.items():
            if idx > self.known[eng][f]:
                self.known[eng][f] = idx
                dep = self.streams[f][idx]
                dep.signal = True
                o.waits.append(dep)
        for d in dneed:
            key = (d.eng, d.dslot)
            if self.kdma[eng].get(key, 0) < d.dval:
                self.kdma[eng][key] = d.dval
                o.dwaits.append(d)
        for u in r:
            u.readers.append(o)
        for u in w:
            u.lastw = o
            u.readers = []
        self.streams[eng].append(o)
        return o

    def finalize(self):
        for e in ENGS:
            c = 0
            for o in self.streams[e]:
                if o.signal and not o.is_dma:
                    c += 1
                    o.sigval = c

    def emit(self, e, h, sems, dsems):
        for o in self.streams[e]:
            for d in o.waits:
                h.wait_ge(sems[d.eng], d.sigval)
            for d in o.dwaits:
                h.wait_ge(dsems[d.eng][d.dslot], d.dval)
            if o.fn is None:
                continue
            ins = o.fn(h)
            if o.is_dma:
                ins.then_inc(dsems[e][o.dslot], 16)
            elif o.signal:
                ins.then_inc(sems[e], 1)


class Ring:
    def __init__(self, units):
        self.units = units
        self.i = 0

    def next(self):
        u = self.units[self.i % len(self.units)]
        self.i += 1
        u.gen += 1
        return u


def _vec_layout(inp):
    cols = []
    index = {}

    def add(name, arr):
        arr = np.asarray(arr, np.float32)
        index[name] = sum(c.shape[1] for c in cols)
        cols.append(arr)

    def chunked(v, n):
        return np.ascontiguousarray(np.asarray(v, np.float32).reshape(n, 128).T)

    for l in range(4):
        add(('g_mix', l), chunked(inp['g_mix'][l], 8))
        add(('g_ffn', l), chunked(inp['g_ffn'][l], 8))
        wf = np.asarray(inp['w_dw_f'][l], np.float32)
        add(('w_dw_f', l), np.ascontiguousarray(
            wf.T.reshape(44, 128, 3).transpose(1, 0, 2).reshape(128, 132)))
        add(('b_dw_f', l), chunked(inp['b_dw_f'][l], 44))
    for l in range(2):
        add(('b_glu', l), chunked(inp['b_glu'][l], 12))
        wa = np.asarray(inp['w_dw_a'][l], np.float32)
        add(('w_dw_a', l), np.ascontiguousarray(
            wa.T.reshape(6, 128, 31).transpose(1, 0, 2).reshape(128, 186)))
        add(('b_dw_a', l), chunked(inp['b_dw_a'][l], 6))
        add(('ln_g', l), chunked(inp['ln_g'][l], 6))
        add(('ln_b', l), chunked(inp['ln_b'][l], 6))
    add('g_kv', chunked(inp['g_kv'], 8))
    add('g_mem', chunked(inp['g_mem'], 8))
    add('g_final', chunked(inp['g_final'], 8))
    bf = np.zeros((128, 1), np.float32)
    bf[:12, 0] = np.asarray(inp['b_f'], np.float32)
    add('b_f', bf)
    add('eps_rms', np.full((128, 1), 1e-6, np.float32))
    add('eps_ln', np.full((128, 1), 1e-5, np.float32))
    vec = np.concatenate(cols, axis=1)
    return np.ascontiguousarray(vec), index


def _consts():
    k = np.arange(128)[:, None]
    q = np.arange(128)[None, :]
    mask = np.where(k <= q, 0.0, -30000.0).astype(np.float32)
    ident = np.eye(128, dtype=np.float32)
    id12 = np.zeros((128, 12), np.float32)
    id12[:12, :12] = np.eye(12, dtype=np.float32)
    sel = np.zeros((128, 12, 128), np.float32)
    for pc in range(3):
        for h in range(12):
            sel[12 * pc + h, h, :] = 1.0
    cst = np.concatenate([mask, ident, id12, sel.reshape(128, 12 * 128)], axis=1)
    return np.ascontiguousarray(cst)


_VEC_INDEX_CACHE = {}


def build(vec_index, nv, stop=None):
    nc = bass.Bass("TRN2", target_bir_lowering=False)
    dr = {}

    def din(name, shape):
        dr[name] = nc.dram_tensor(name, list(shape), F32, kind="ExternalInput").ap()
        return dr[name]

    xT_d = din("xT", [D, S])
    memT_d = din("memT", [D, MEM])
    vec_d = din("vec", [128, nv])
    NCST = 128 + 128 + 12 + 12 * 128
    cst_d = din("cst", [128, NCST])
    w_in_a_d = din("w_in_a", [2, D, 1792])
    w_kvf_d = din("w_kvf", [D, 1548])
    w_in_b_d = din("w_in_b", [2, D, D])
    w_mem_kv_d = din("w_mem_kv", [4, D, 512])
    w_out_d = din("w_out", [4, D, D])
    w_up_d = din("w_up", [4, D, 2 * DFF])
    w_down_d = din("w_down", [4, DFF, D])
    outT_d = nc.dram_tensor("outT", [D, S], F32, kind="ExternalOutput").ap()
    wsrc = dict(w_in_a=w_in_a_d, w_kvf=w_kvf_d, w_in_b=w_in_b_d, w_mem_kv=w_mem_kv_d, w_out=w_out_d,
                w_up=w_up_d, w_down=w_down_d)
    wbf = {k: nc.dram_tensor(k + "_bf", list(v.shape), BF16).ap() for k, v in wsrc.items()}

    p = Prog()
    WU = {}

    def wsel(d, name, l):
        return d[name] if l is None else d[name][l]

    cast_chain = [None]

    def cast_w(name, l=None, after=()):
        u = p.unit(f"WU_{name}_{l}")
        WU[(name, l)] = u
        src = wsel(wsrc, name, l)
        dst = wsel(wbf, name, l)
        nrow, ncol = src.shape[-2], src.shape[-1]
        step = 2048
        first = True
        for c0 in range(0, ncol, step):
            c1 = min(ncol, c0 + step)
            rows = 128 if (c1 - c0) > 1024 else 256
            for r0 in range(0, nrow, rows):
                r1 = min(nrow, r0 + rows)
                rd = list(after) if first else []
                if first and cast_chain[0] is not None:
                    rd.append(cast_chain[0])
                first = False
                p.op('pool', lambda e, o=dst[r0:r1, c0:c1], i=src[r0:r1, c0:c1]: e.dma_start(out=o, in_=i, max_dma_last_dim=2048),
                     r=rd, w=[u], dma=True)
        cast_chain[0] = u

    with nc.sbuf_tensor("arena", [128, ARENA], U8) as arena_t:
        arena = arena_t[:]

        def view(off, nbytes, dt, pat=None, **kw):
            a = arena[:, off:off + nbytes].bitcast(dt)
            if pat is not None:
                a = a.rearrange(pat, **kw)
            return a

        OX = 0
        OB1 = OX + 65536
        OB2 = OB1 + 32768
        OPH = OB2 + 32768
        OMISC = OPH + 57344
        assert OMISC == 188416

        def mk(name, off, nbytes, dt, pat=None, **kw):
            return p.unit(name, off, off + nbytes, view(off, nbytes, dt, pat, **kw))

        X = [[mk(f"X{c}_{n}", OX + c * 8192 + n * 2048, 2048, F32) for n in range(NT)]
             for c in range(KC)]
        Xc = [view(OX + c * 8192, 8192, F32) for c in range(KC)]
        B1 = [[mk(f"B1_{c}_{n}", OB1 + c * 4096 + n * 1024, 1024, BF16) for n in range(NT)]
              for c in range(KC)]
        B2 = [[mk(f"B2_{c}_{n}", OB2 + c * 4096 + n * 1024, 1024, BF16) for n in range(NT)]
              for c in range(KC)]
        B2c = [view(OB2 + c * 4096, 4096, BF16) for c in range(KC)]
        OU = OB2 + 16384
        SQ = [mk(f"SQ{i}", OU + i * 1024, 1024, BF16) for i in range(2)]
        SD = mk("SD", OU + 2048, 2048, F32)
        RSTD = mk("RSTD", OU + 4096, 2048, F32)
        SQ4 = [mk(f"SQ4_{i}", OU + 4096 + i * 1024, 1024, BF16) for i in range(4)]
        RS4 = [mk(f"RS4_{i}", OU + 8192 + i * 2048, 2048, F32) for i in range(4)]
        TG = [mk(f"TG{i}", OU + i * 2048, 2048, F32) for i in range(3)]
        TV = [mk(f"TV{i}", OU + 6144 + i * 2048, 2048, F32) for i in range(3)]
        HG = [mk(f"HG{i}", OU + 12288 + i * 8, 8, F32) for i in range(2)]
        HV = [mk(f"HV{i}", OU + 12304 + i * 8, 8, F32) for i in range(2)]
        ACTB = [[B2[g][n] for n in range(NT)] for g in range(4)]
        QTA = [mk(f"QTA{j}", OB2 + j * 2048, 1024, BF16) for j in range(6)]
        QTB = [mk(f"QTB{j}", OB2 + j * 2048 + 1024, 1024, BF16) for j in range(6)]
        QTM = [mk(f"QTM{j}", OB2 + 12288 + j * 1024, 1024, BF16) for j in range(2)]
        PTB = [mk(f"PTB{i}", OU + 6144 + i * 1024, 1024, BF16) for i in range(4)]
        RDENB = mk("RDENB", OU + 10240, 2048, F32)
        DGB = [mk(f"DGB{i}", OU + 12288 + i * 512, 512, F32) for i in range(2)]
        LF = mk("LF", OB2, 8192, F32)
        HI = mk("HIp", OB2, 4096, BF16)
        MID = mk("MIDp", OB2 + 4096, 4096, BF16)
        ONESF = mk("ONESF", OB2 + 8192, 8192, F32)
        CC = mk("CC", OB2 + 16384, 8192, F32)
        LO = mk("LOp", OB2 + 24576, 4096, BF16)
        DIAG = [mk(f"DIAG{i}", OPH + i * 7936, 7936, BF16, "p (k m) -> p k m", k=31)
                for i in range(2)]
        oa = OPH + 15872
        Y = [mk(f"Y{j}", oa + j * 2048, 2048, F32) for j in range(6)]
        YB = [mk(f"YB{j}", oa + 12288 + j * 1024, 1024, BF16) for j in range(6)]
        YSQ = [mk(f"YSQ{j}", oa + 18432 + j * 1024, 1024, BF16) for j in range(6)]
        oa += 24576
        MU = mk("MU", oa, 2048, F32)
        T2 = mk("T2", oa + 2048, 2048, F32)
        RSTL = mk("RSTL", oa + 4096, 2048, F32)
        SG = [mk(f"SG{i}", oa + 6144 + i * 2048, 2048, F32) for i in range(2)]
        PTA = [mk(f"PTA{i}", oa + 10240 + i * 1024, 1024, BF16) for i in range(4)]
        RDENA = mk("RDENA", oa + 14336, 2048, F32)
        IDENT = mk("IDENT", oa + 16384, 512, F32)
        assert oa + 16896 <= OMISC
        MEMX = mk("MEMX", OPH, 8192, F32, "p (c m) -> p c m", c=8)
        KT = [[mk(f"KT{j}_{n}", OPH + j * 4096 + n * 1024, 1024, BF16) for n in range(NT)]
              for j in range(6)]
        KTc = [view(OPH + j * 4096, 4096, BF16) for j in range(6)]
        OV = OPH + 24576
        V = [mk(f"V{kb}", OV + kb * 1536, 1536, BF16) for kb in range(16)]
        Vall = view(OV, 24576, BF16, "p (kb c) -> p kb c", kb=16)
        CQ = mk("CQ", OV + 24576, 4096, BF16)
        SEL = mk("SEL", OV + 28672, 3072, BF16, "p (h m) -> p h m", h=12)
        NEGCK = mk("NEGCK", OV + 31744, 768, F32)
        MASK = mk("MASK", OV + 32512, 256, BF16)
        assert OV + 32768 <= OMISC
        om = OMISC
        VEC = mk("VEC", om, nv * 4, F32)
        om += ((nv * 4 + 31) // 32) * 32
        MEMN = mk("MEMN", om, 4096, BF16, "p (c m) -> p c m", c=8)
        om += 4096
        MEMKT = mk("MEMKT", om, 1024, BF16, "p (c m) -> p c m", c=2)
        om += 1024
        MEMV = mk("MEMV", om, 1024, BF16, "p (b c) -> p b c", b=2)
        om += 1024
        ONES = mk("ONES", om, 256, BF16)
        om += 256
        ID12 = mk("ID12", om, 64, F32)
        om += 64
        ONESH = [mk(f"ONESH{i}", om + i * 256, 256, BF16) for i in range(2)]
        om += 512
        NSLOT = (ARENA - om) // 2048
        assert NSLOT >= 6, NSLOT
        slot_units = [mk(f"WS{i}", om + i * 2048, 2048, BF16) for i in range(NSLOT)]
        slots = Ring(slot_units)

        pst = [nc.psum_tensor("psbig", [128, 4096], F32)]
        psbig = pst[0].__enter__()[:]
        PSU = [p.unit(f"PS{i}", ap=psbig[:, i * 512:(i + 1) * 512]) for i in range(8)]
        psg = Ring(PSU[0:4])
        pss3 = Ring(PSU[0:3])
        psacc = Ring(PSU[3:8])
        psall = Ring(PSU)
        ps6 = Ring(PSU[0:6])

        def vcol(key, i=0, n=1, rows=128):
            b = vec_index[key] + i
            return VEC.ap[0:rows, b:b + n]

        def load_w(dram_ap, shape_pat=None, **kw):
            s = slots.next()
            dst = s.ap if shape_pat is None else s.ap.rearrange(shape_pat, **kw)
            return s, dst

        def wjob_k(wd, c0, m):
            s = slots.next()
            dst = s.ap.rearrange("p (k m) -> p k m", k=8)
            if m < 64:
                src = wsel(wsrc, *wd).rearrange("(k p) n -> p k n", p=128)[:, :, c0:c0 + m]
                p.op('pool', lambda e, d=dst[:, :, 0:m], s_=src: e.dma_start(out=d, in_=s_),
                     w=[s], dma=True)
                return s, dst
            src = wsel(wbf, *wd).rearrange("(k p) n -> p k n", p=128)[:, :, c0:c0 + m]
            p.op('sp', lambda e, d=dst[:, :, 0:m], s_=src: e.dma_start(out=d, in_=s_),
                 r=[WU[wd]], w=[s], dma=True)
            return s, dst

        def linear(groups, rhs_fn, tiles, cb, N=TW, ring=None):
            for gi, grp in enumerate(groups):
                sl = [wjob_k(wd, c0, m) + (m,) for (wd, c0, m) in grp]
                for n in tiles:
                    pss = []
                    for (su, sap, m) in sl:
                        ps = (ring or psg).next()
                        for kc in range(KC):
                            rap, ru = rhs_fn(kc, n)
                            p.op('pe', lambda e, o=ps.ap[0:m, 0:N], a=sap[:, kc, 0:m], b=rap,
                                 st=(kc == 0), sp=(kc == KC - 1):
                                 e.matmul(o, lhsT=a, rhs=b, start=st, stop=sp),
                                 r=[su, ru], w=[ps])
                        pss.append(ps)
                    cb(gi, n, pss)

        def rms_tile(srcs, dsts, gkey, N, eps_key='eps_rms', inv=1.0 / D):
            ps = psg.next()
            for c in range(KC):
                sap, su = srcs[c]
                sq = SQ[c % 2]
                p.op('act', lambda e, o=sq.ap[:, 0:N], i=sap: e.activation(out=o, in_=i, func=AF.Square),
                     r=[su], w=[sq])
                p.op('pe', lambda e, o=ps.ap[:, 0:N], b=sq.ap[:, 0:N], st=(c == 0), sp=(c == KC - 1):
                     e.matmul(o, lhsT=ONES.ap, rhs=b, start=st, stop=sp), r=[sq, ONES], w=[ps])
            p.op('act', lambda e, o=SD.ap[:, 0:N], i=ps.ap[:, 0:N]:
                 e.activation(out=o, in_=i, func=AF.Ln, bias=vcol(eps_key), scale=inv),
                 r=[ps, VEC], w=[SD])
            p.op('act', lambda e, o=RSTD.ap[:, 0:N], i=SD.ap[:, 0:N]:
                 e.activation(out=o, in_=i, func=AF.Exp, scale=-0.5),
                 r=[SD], w=[RSTD])
            for c in range(KC):
                sap, su = srcs[c]
                dap, du = dsts[c]
                p.op('dve', lambda e, o=dap, i=sap, g=vcol(gkey, c), rs=RSTD.ap[:, 0:N]:
                     e.scalar_tensor_tensor(out=o, in0=i, scalar=g, in1=rs, op0=ALU.mult, op1=ALU.mult),
                     r=[su, RSTD, VEC], w=[du])

        fused_stats = [False]

        def rmsnorm_x(dst, gkey, c_outer=False):
            if fused_stats[0]:
                fused_stats[0] = False
                rms_apply(dst, gkey, c_outer)
                return
            rms_stats()
            rms_apply(dst, gkey, c_outer)

        def rms_lnexp(n, ps):
            p.op('act', lambda e, o=RS4[n].ap, i=ps.ap:
                 e.activation(out=o, in_=i, func=AF.Ln, bias=vcol('eps_rms'), scale=1.0 / D),
                 r=[ps, VEC], w=[RS4[n]])
            p.op('act', lambda e, o=RS4[n].ap: e.activation(out=o, in_=o, func=AF.Exp, scale=-0.5),
                 r=[RS4[n]], w=[RS4[n]])

        def rms_apply(dst, gkey, c_outer):
            order = [(n, c) for n in range(NT) for c in range(KC)]
            if c_outer:
                order = [(n, c) for c in range(KC) for n in range(NT)]
            for (n, c) in order:
                p.op('dve', lambda e, o=dst[c][n].ap, i=X[c][n].ap, g=vcol(gkey, c), rs=RS4[n].ap:
                     e.scalar_tensor_tensor(out=o, in0=i, scalar=g, in1=rs, op0=ALU.mult, op1=ALU.mult),
                     r=[X[c][n], RS4[n], VEC], w=[dst[c][n]])

        def rms_stats():
            pss = []
            k = 0
            for n in range(NT):
                ps = psg.next()
                pss.append(ps)
                for c in range(KC):
                    sq = SQ4[k % 4]
                    k += 1
                    if c % 2 == 0:
                        p.op('act', lambda e, o=sq.ap, i=X[c][n].ap: e.activation(out=o, in_=i, func=AF.Square),
                             r=[X[c][n]], w=[sq])
                    else:
                        p.op('dve', lambda e, o=sq.ap, i=X[c][n].ap: e.tensor_tensor(out=o, in0=i, in1=i, op=ALU.mult),
                             r=[X[c][n]], w=[sq])
                    p.op('pe', lambda e, o=ps.ap, b=sq.ap, st=(c == 0), sp=(c == KC - 1):
                         e.matmul(o, lhsT=ONES.ap, rhs=b, start=st, stop=sp), r=[sq, ONES], w=[ps])
            for n in range(NT):
                rms_lnexp(n, pss[n])

        def attn_tile(n, pairs, PT, RDEN, DG, depth=2):
            items = []
            for pi, pd in enumerate(pairs):
                nb = len(pd['blocks'])
                for hh in range(2):
                    for bi, blk in enumerate(pd['blocks']):
                        items.append((pi, hh, bi, blk, bi == 0, bi == nb - 1))
            acc = {}

            def scores(it):
                pi, hh, bi, (kb, c0, diag), first, last = it
                pd = pairs[pi]
                pr = slice(64 * hh, 64 * hh + 64)
                sps = pss3.next()
                kap, ku = pd['kt'](hh, kb)
                if pd['h0'] is None:
                    qt_unit = pd['q']
                    p.op('pe', lambda e, o=sps.ap[:, c0:TW], a=kap, b=qt_unit.ap[pr, c0:TW]:
                         e.matmul(o, lhsT=a, rhs=b, start=True, stop=True),
                         r=[ku, qt_unit], w=[sps])
                    bias = None
                else:
                    h = pd['h0'] + hh
                    qt_unit = pd['q'][hh]
                    p.op('pe', lambda e, o=sps.ap[:, c0:TW], a=kap, b=qt_unit.ap[:, c0:TW]:
                         e.matmul(o, lhsT=a, rhs=b, start=True, stop=False),
                         r=[ku, qt_unit], w=[sps])
                    p.op('pe', lambda e, o=sps.ap[:, c0:TW], a=SEL.ap[:, h, :],
                         b=CQ.ap[:, n * TW + c0:(n + 1) * TW]:
                         e.matmul(o, lhsT=a, rhs=b, start=False, stop=True),
                         r=[SEL, CQ], w=[sps])
                    bias = NEGCK.ap[:, kb * 12 + h:kb * 12 + h + 1]
                pt = PT.next()
                if diag:
                    dg = DG.next()
                    p.op('dve', lambda e, o=dg.ap, a=sps.ap[:, c0:c0 + 128]:
                         e.tensor_tensor(out=o, in0=a, in1=MASK.ap, op=ALU.add),
                         r=[sps, MASK], w=[dg])
                    p.op('act', lambda e, o=pt.ap[:, c0:c0 + 128], i=dg.ap, b=bias:
                         e.activation(out=o, in_=i, func=AF.Exp, bias=b),
                         r=[dg, NEGCK], w=[pt])
                    if c0 + 128 < TW:
                        p.op('act', lambda e, o=pt.ap[:, c0 + 128:TW], i=sps.ap[:, c0 + 128:TW], b=bias:
                             e.activation(out=o, in_=i, func=AF.Exp, bias=b),
                             r=[sps, NEGCK], w=[pt])
                elif bias is not None:
                    p.op('act', lambda e, o=pt.ap[:, c0:TW], i=sps.ap[:, c0:TW], b=bias:
                         e.activation(out=o, in_=i, func=AF.Exp, bias=b),
                         r=[sps, NEGCK], w=[pt])
                else:
                    p.op('act', lambda e, o=pt.ap[:, c0:TW], i=sps.ap[:, c0:TW]:
                         e.activation(out=o, in_=i, func=AF.Exp),
                         r=[sps], w=[pt])
                return pt

            def pv(it, pt):
                pi, hh, bi, (kb, c0, diag), first, last = it
                pd = pairs[pi]
                pr = slice(64 * hh, 64 * hh + 64)
                fox = pd['h0'] is not None
                if pi not in acc:
                    acc[pi] = dict(den=psacc.next())
                a_ = acc[pi]
                den = a_['den']
                vap, vu = pd['v'](hh, kb)
                out_unit = pd['out']
                if fox:
                    if hh not in a_:
                        a_[hh] = psacc.next()
                    num = a_[hh]
                    p.op('pe', lambda e, o=num.ap[:, c0:TW], a=vap, b=pt.ap[:, c0:TW], st=first, sp=last:
                         e.matmul(o, lhsT=a, rhs=b, start=st, stop=sp), r=[vu, pt], w=[num])
                    p.op('pe', lambda e, o=den.ap[:, c0:TW], a=ONESH[hh].ap, b=pt.ap[:, c0:TW],
                         st=(first and hh == 0), sp=(last and hh == 1):
                         e.matmul(o, lhsT=a, rhs=b, start=st, stop=sp), r=[ONESH[hh], pt], w=[den])
                    if hh == 1 and last:
                        na, nb_ = a_[0], a_[1]
                        p.op('dve', lambda e, d=den: e.reciprocal(out=RDEN.ap, in_=d.ap), r=[den], w=[RDEN])
                        p.op('dve', lambda e, o=out_unit.ap[0:64, :], nm=na: e.tensor_tensor(out=o, in0=nm.ap[0:64, :], in1=RDEN.ap[0:64, :], op=ALU.mult),
                             r=[na, RDEN], w=[out_unit])
                        p.op('dve', lambda e, o=out_unit.ap[64:128, :], nm=nb_: e.tensor_tensor(out=o, in0=nm.ap[64:128, :], in1=RDEN.ap[64:128, :], op=ALU.mult),
                             r=[nb_, RDEN], w=[out_unit])
                else:
                    if 'num' not in a_:
                        a_['num'] = psacc.next()
                    num = a_['num']
                    p.op('pe', lambda e, o=num.ap[pr, c0:TW], a=vap, b=pt.ap[:, c0:TW], st=first, sp=last:
                         e.matmul(o, lhsT=a, rhs=b, start=st, stop=sp), r=[vu, pt], w=[num])
                    p.op('pe', lambda e, o=den.ap[pr, c0:TW], a=ONES.ap[:, 0:64], b=pt.ap[:, c0:TW], st=first, sp=last:
                         e.matmul(o, lhsT=a, rhs=b, start=st, stop=sp), r=[ONES, pt], w=[den])
                    if hh == 1 and last:
                        p.op('dve', lambda e, d=den: e.reciprocal(out=RDEN.ap, in_=d.ap), r=[den], w=[RDEN])
                        p.op('dve', lambda e, o=out_unit.ap, nm=num: e.tensor_tensor(out=o, in0=nm.ap, in1=RDEN.ap, op=ALU.mult),
                             r=[num, RDEN], w=[out_unit])

            pend = []
            for it in items:
                pend.append((it, scores(it)))
                if len(pend) > depth:
                    pv(*pend.pop(0))
            while pend:
                pv(*pend.pop(0))

        def mem_pairs(qsrc, outs):
            return [dict(q=qsrc[jp],
                         kt=(lambda hh, kb, jp=jp: (MEMKT.ap[64 * hh:64 * hh + 64, jp, kb * 128:(kb + 1) * 128], MEMKT)),
                         v=(lambda hh, kb, jp=jp: (MEMV.ap[:, kb, (2 * jp + hh) * 64:(2 * jp + hh + 1) * 64], MEMV)),
                         blocks=[(0, 0, False), (1, 0, False)], h0=None, out=outs[jp]) for jp in range(2)]

        def mem_kv(l):
            wd = ('w_mem_kv', l)

            def cbk(gi, n, pss):
                p.op('act', lambda e, o=MEMKT.ap[:, gi, :], i=pss[0].ap[:, 0:MEM]:
                     e.activation(out=o, in_=i, func=AF.Copy), r=[pss[0]], w=[MEMKT])
            linear([[(wd, j * 128, 128)] for j in range(2)],
                   lambda kc, n: (MEMN.ap[:, kc, :], MEMN), [0], cbk, N=MEM)
            for cj in range(2):
                su, sap = wjob_k(wd, 256 + cj * 128, 128)
                ps = psg.next()
                for mb in range(2):
                    for kc in range(KC):
                        p.op('pe', lambda e, o=ps.ap[:, mb * 128:(mb + 1) * 128],
                             a=MEMN.ap[:, kc, mb * 128:(mb + 1) * 128], b=sap[:, kc, :],
                             st=(kc == 0), sp=(kc == KC - 1):
                             e.matmul(o, lhsT=a, rhs=b, start=st, stop=sp), r=[su, MEMN], w=[ps])
                p.op('act', lambda e, o=MEMV.ap[:, :, cj * 128:(cj + 1) * 128],
                     i=ps.ap[:, 0:256].rearrange("p (b c) -> p b c", b=2):
                     e.activation(out=o, in_=i, func=AF.Copy), r=[ps], w=[MEMV])

        def wout_residual(l, src):
            def cb(gi, n, pss):
                p.op('dve', lambda e, o=X[gi][n].ap, i=pss[0].ap:
                     e.tensor_tensor(out=o, in0=o, in1=i, op=ALU.add), r=[pss[0], X[gi][n]], w=[X[gi][n]])
            linear([[(('w_out', l), dc * 128, 128)] for dc in range(KC)],
                   lambda kc, n: (src[kc][n].ap, src[kc][n]), range(NT), cb)

        def ffn(l):
            rmsnorm_x(B1, ('g_ffn', l))
            wu = ('w_up', l)
            wdn = wbf['w_down'][l]
            groups = [list(range(g, min(g + 4, NFF))) for g in range(0, NFF, 4)]
            cnt = [0]
            pend = [None]
            for grp in groups:
                for gi, f in enumerate(grp):
                    sg_u, sg_ap = wjob_k(wu, f * 128, 128)
                    sv_u, sv_ap = wjob_k(wu, DFF + f * 128, 128)
                    wg = vec_index[('w_dw_f', l)] + f * 3
                    wv = vec_index[('w_dw_f', l)] + (NFF + f) * 3
                    bg = vec_index[('b_dw_f', l)] + f
                    bv = vec_index[('b_dw_f', l)] + NFF + f
                    for n in range(NT):
                        banks = (n, 4 + n)
                        for (su, sap, bk) in ((sg_u, sg_ap, banks[0]), (sv_u, sv_ap, banks[1])):
                            for kc in range(KC):
                                p.op('pe', lambda e, o=PSU[bk].ap, a=sap[:, kc, :], b=B1[kc][n].ap,
                                     st=(kc == 0), sp=(kc == KC - 1):
                                     e.matmul(o, lhsT=a, rhs=b, start=st, stop=sp),
                                     r=[su, B1[kc][n]], w=[PSU[bk]])
                        k = cnt[0]
                        cnt[0] += 1
                        tg = TG[k % 3]
                        tv = TV[k % 3]
                        paths = ((banks[0], wg, bg, tg), (banks[1], wv, bv, tv))
                        for (bk, wb, bb, t) in paths:
                            p.op('act', lambda e, o=t.ap, i=PSU[bk].ap, s_=VEC.ap[:, wb + 2:wb + 3], b_=VEC.ap[:, bb:bb + 1]:
                                 e.activation(out=o, in_=i, func=AF.Identity, bias=b_, scale=s_),
                                 r=[PSU[bk], VEC], w=[t])
                        if pend[0] is not None:
                            pend[0]()
                        for tap, sh in ((1, 1), (0, 2)):
                            for (bk, wb, bb, t) in paths:
                                if n == 0:
                                    p.op('dve', lambda e, o=t.ap[:, sh:TW], i=PSU[bk].ap[:, 0:TW - sh], s_=VEC.ap[:, wb + tap:wb + tap + 1]:
                                         e.scalar_tensor_tensor(out=o, in0=i, scalar=s_, in1=o, op0=ALU.mult, op1=ALU.add),
                                         r=[PSU[bk], t, VEC], w=[t])
                                else:
                                    p.op('dve', lambda e, o=t.ap, i=psbig[:, bk * 512 - sh:bk * 512 - sh + TW], s_=VEC.ap[:, wb + tap:wb + tap + 1]:
                                         e.scalar_tensor_tensor(out=o, in0=i, scalar=s_, in1=o, op0=ALU.mult, op1=ALU.add),
                                         r=[PSU[bk], PSU[bk - 1], t, VEC], w=[t])

                        def tail(tg=tg, tv=tv, dst=ACTB[gi][n]):
                            p.op('act', lambda e, o=tg.ap: e.activation(out=o, in_=o, func=AF.Silu), r=[tg], w=[tg])
                            p.op('dve', lambda e, o=dst.ap, a=tg.ap, b=tv.ap:
                                 e.tensor_tensor(out=o, in0=a, in1=b, op=ALU.mult), r=[tg, tv], w=[dst])
                        pend[0] = tail
                pend[0]()
                pend[0] = None
                dsl = []
                for f in grp:
                    s = slots.next()
                    p.op('sp', lambda e, d=s.ap, s_=wdn[f * 128:(f + 1) * 128, :]: e.dma_start(out=d, in_=s_),
                         r=[WU[('w_down', l)]], w=[s], dma=True)
                    dsl.append(s)
                last_grp = (grp is groups[-1])
                ring = ps6 if last_grp else psall
                sqk = 0
                for n in range(NT):
                    stat = PSU[6 + n % 2]
                    pend_q = []
                    for dc in range(KC):
                        ps = ring.next()
                        for gi, s in enumerate(dsl):
                            p.op('pe', lambda e, o=ps.ap, a=s.ap[:, dc * 128:(dc + 1) * 128], b=ACTB[gi][n].ap,
                                 st=(gi == 0), sp=(gi == len(dsl) - 1):
                                 e.matmul(o, lhsT=a, rhs=b, start=st, stop=sp), r=[s, ACTB[gi][n]], w=[ps])
                        while len(pend_q) > 1:
                            pend_q.pop(0)()
                        p.op('dve', lambda e, o=X[dc][n].ap, i=ps.ap:
                             e.tensor_tensor(out=o, in0=o, in1=i, op=ALU.add), r=[ps, X[dc][n]], w=[X[dc][n]])
                        if last_grp:
                            sq = SQ4[sqk % 4]
                            sqk += 1
                            p.op('act', lambda e, o=sq.ap, i=X[dc][n].ap: e.activation(out=o, in_=i, func=AF.Square),
                                 r=[X[dc][n]], w=[sq])

                            def pend_stat(sq=sq, dc=dc, stat=stat):
                                p.op('pe', lambda e, o=stat.ap, b=sq.ap, st=(dc == 0), sp=(dc == KC - 1):
                                     e.matmul(o, lhsT=ONES.ap, rhs=b, start=st, stop=sp), r=[sq, ONES], w=[stat])
                            pend_q.append(pend_stat)
                    while pend_q:
                        pend_q.pop(0)()
                    if last_grp:
                        rms_lnexp(n, stat)
                if last_grp:
                    fused_stats[0] = True

        def mixer_a(l, hook):
            rmsnorm_x(B1, ('g_mix', l))
            wd = ('w_in_a', l)
            bgl = vec_index[('b_glu', l)]
            sgc = [0]

            def cb_glu(gi, n, pss):
                sg = SG[sgc[0] % 2]
                sgc[0] += 1
                p.op('act', lambda e, o=sg.ap, i=pss[1].ap, b=VEC.ap[:, bgl + 6 + gi:bgl + 7 + gi]:
                     e.activation(out=o, in_=i, func=AF.Sigmoid, bias=b), r=[pss[1], VEC], w=[sg])
                p.op('dve', lambda e, o=B2[gi][n].ap, i=pss[0].ap, s_=VEC.ap[:, bgl + gi:bgl + gi + 1], b=sg.ap:
                     e.scalar_tensor_tensor(out=o, in0=i, scalar=s_, in1=b, op0=ALU.add, op1=ALU.mult),
                     r=[pss[0], sg, VEC], w=[B2[gi][n]])
            linear([[(wd, j * 128, 128), (wd, 768 + j * 128, 128)] for j in range(6)],
                   lambda kc, n: (B1[kc][n].ap, B1[kc][n]), range(NT), cb_glu)
            hook([B2[5][NT - 1]])
            mem_kv(l)

            def cb_q(gi, n, pss):
                p.op('act', lambda e, o=B2[6 + gi][n].ap, i=pss[0].ap:
                     e.activation(out=o, in_=i, func=AF.Copy, scale=0.125), r=[pss[0]], w=[B2[6 + gi][n]])
            linear([[(wd, 1536 + j * 128, 128)] for j in range(2)],
                   lambda kc, n: (B1[kc][n].ap, B1[kc][n]), range(NT), cb_q)

            PT = Ring(PTA)
            wb = vec_index[('w_dw_a', l)]
            dgc = [0]
            allp = []
            for n in range(NT):
                allp += mem_pairs([B2[6][n], B2[7][n]], [B1[6][n], B1[7][n]])
            attn_tile(0, allp, PT, RDENA, None)
            def build_diag(j_, slot):
                dgu = DIAG[slot % 2]
                p.op('dve', lambda e, o=dgu.ap, i0=IDENT.ap.unsqueeze(1).to_broadcast([128, 31, 128]),
                     i1=VEC.ap[:, wb + j_ * 31:wb + j_ * 31 + 31].unsqueeze(2).to_broadcast([128, 31, 128]):
                     e.tensor_tensor(out=o, in0=i0, in1=i1, op=ALU.mult),
                     r=[IDENT, VEC], w=[dgu])
            build_diag(0, 0)
            for n in range(NT):
                for j in range(6):
                    dg = DIAG[dgc[0] % 2]
                    dgc[0] += 1
                    nxt_emitted = [False]

                    def emit_next(j=j):
                        if not nxt_emitted[0] and not (n == NT - 1 and j == 5):
                            build_diag((j + 1) % 6, dgc[0])
                        nxt_emitted[0] = True
                    ps = psg.next()
                    order = [30] + list(range(30))
                    for ki, k in enumerate(order):
                        off = n * TW - 30 + k
                        o0 = 0
                        if off < 0:
                            o0 = -off
                            off = 0
                        ln_ = TW - o0
                        ru = [B2[j][n]] + ([B2[j][n - 1]] if n > 0 else [])
                        p.op('pe', lambda e, o=ps.ap[:, o0:TW], a=dg.ap[:, k, :], b=B2c[j][:, off:off + ln_],
                             st=(ki == 0), sp=(ki == 30):
                             e.matmul(o, lhsT=a, rhs=b, start=st, stop=sp), r=[dg] + ru, w=[ps])
                    emit_next()
                    bd = vcol(('b_dw_a', l), j)
                    p.op('act', lambda e, o=Y[j].ap, i=ps.ap, b=bd: e.activation(out=o, in_=i, func=AF.Identity, bias=b),
                         r=[ps, VEC], w=[Y[j]])
                    p.op('act', lambda e, o=YB[j].ap, i=ps.ap, b=bd: e.activation(out=o, in_=i, func=AF.Identity, bias=b),
                         r=[ps, VEC], w=[YB[j]])
                    p.op('act', lambda e, o=YSQ[j].ap, i=ps.ap, b=bd: e.activation(out=o, in_=i, func=AF.Square, bias=b),
                         r=[ps, VEC], w=[YSQ[j]])
                s1 = psg.next()
                s2 = psg.next()
                for j in range(6):
                    p.op('pe', lambda e, b=YB[j].ap, st=(j == 0), sp=(j == 5):
                         e.matmul(s1.ap, lhsT=ONES.ap, rhs=b, start=st, stop=sp), r=[YB[j], ONES], w=[s1])
                for j in range(6):
                    p.op('pe', lambda e, b=YSQ[j].ap, st=(j == 0), sp=(j == 5):
                         e.matmul(s2.ap, lhsT=ONES.ap, rhs=b, start=st, stop=sp), r=[YSQ[j], ONES], w=[s2])
                p.op('dve', lambda e: e.tensor_scalar(out=MU.ap, in0=s1.ap, scalar1=1.0 / 768, scalar2=None, op0=ALU.mult),
                     r=[s1], w=[MU])
                p.op('dve', lambda e: e.tensor_tensor(out=T2.ap, in0=MU.ap, in1=MU.ap, op=ALU.mult), r=[MU], w=[T2])
                p.op('dve', lambda e: e.scalar_tensor_tensor(out=T2.ap, in0=s2.ap, scalar=1.0 / 768, in1=T2.ap,
                                                            op0=ALU.mult, op1=ALU.subtract), r=[s2, T2], w=[T2])
                p.op('act', lambda e: e.activation(out=T2.ap, in_=T2.ap, func=AF.Ln, bias=vcol('eps_ln')),
                     r=[T2, VEC], w=[T2])
                p.op('act', lambda e: e.activation(out=RSTL.ap, in_=T2.ap, func=AF.Exp, scale=-0.5), r=[T2], w=[RSTL])
                for j in range(6):
                    p.op('dve', lambda e, o=Y[j].ap: e.tensor_tensor(out=o, in0=o, in1=MU.ap, op=ALU.subtract),
                         r=[Y[j], MU], w=[Y[j]])
                    p.op('dve', lambda e, o=Y[j].ap: e.tensor_tensor(out=o, in0=o, in1=RSTL.ap, op=ALU.mult),
                         r=[Y[j], RSTL], w=[Y[j]])
                    p.op('act', lambda e, o=B1[j][n].ap, i=Y[j].ap, s_=vcol(('ln_g', l), j), b=vcol(('ln_b', l), j):
                         e.activation(out=o, in_=i, func=AF.Silu, bias=b, scale=s_),
                         r=[Y[j], VEC], w=[B1[j][n]])
            wout_residual(l, B1)

        def kv_pre():
            p.op('pool', lambda e: e.dma_start(out=SEL.ap, in_=cst_d[:, 268:268 + 1536].rearrange("p (h m) -> p h m", h=12)),
                 w=[SEL], dma=True)
            p.op('pool', lambda e: e.dma_start(out=MASK.ap, in_=cst_d[:, 0:128]), w=[MASK], dma=True)
            rmsnorm_x(B1, 'g_kv')

            def cb_f(gi, n, pss):
                p.op('act', lambda e, o=LF.ap[0:12, n * TW:(n + 1) * TW], i=pss[0].ap[0:12, :]:
                     e.activation(out=o, in_=i, func=AF.Sigmoid, bias=vcol('b_f', rows=12)),
                     r=[pss[0], VEC], w=[LF])
                p.op('act', lambda e, o=LF.ap[0:12, n * TW:(n + 1) * TW]:
                     e.activation(out=o, in_=o, func=AF.Ln), r=[LF], w=[LF])
            linear([[(('w_kvf', None), 1536, 12)]], lambda kc, n: (B1[kc][n].ap, B1[kc][n]), range(NT), cb_f)
            p.op('dve', lambda e: e.memset(ONESF.ap[0:12, :], 1.0), w=[ONESF])
            p.op('dve', lambda e: e.tensor_tensor_scan(out=CC.ap[0:12, :], data0=ONESF.ap[0:12, :], data1=LF.ap[0:12, :],
                                                       initial=0.0, op0=ALU.mult, op1=ALU.add),
                 r=[ONESF, LF], w=[CC])
            ps = psg.next()
            for kb in range(16):
                p.op('pe', lambda e, o=ps.ap[:, kb * 12:(kb + 1) * 12], i=CC.ap[0:12, kb * 128:(kb + 1) * 128]:
                     e.transpose(o, i, ID12.ap[0:12, 0:12]), r=[CC, ID12], w=[ps])
            p.op('act', lambda e: e.activation(out=NEGCK.ap, in_=ps.ap[:, 0:192], func=AF.Copy, scale=-1.0),
                 r=[ps], w=[NEGCK])
            R1 = ONESF
            p.op('dve', lambda e: e.tensor_copy(out=HI.ap[0:12, :], in_=CC.ap[0:12, :]), r=[CC], w=[HI])
            p.op('dve', lambda e: e.tensor_tensor(out=R1.ap[0:12, :], in0=CC.ap[0:12, :], in1=HI.ap[0:12, :], op=ALU.subtract),
                 r=[CC, HI], w=[R1])
            p.op('dve', lambda e: e.tensor_copy(out=MID.ap[0:12, :], in_=R1.ap[0:12, :]), r=[R1], w=[MID])
            p.op('dve', lambda e: e.tensor_tensor(out=R1.ap[0:12, :], in0=R1.ap[0:12, :], in1=MID.ap[0:12, :], op=ALU.subtract),
                 r=[R1, MID], w=[R1])
            p.op('dve', lambda e: e.tensor_copy(out=LO.ap[0:12, :], in_=R1.ap[0:12, :]), r=[R1], w=[LO])
            p.op('dve', lambda e: e.memset(CQ.ap, 0.0), w=[CQ])
            for pc, pu in enumerate((HI, MID, LO)):
                p.op('pool', lambda e, o=CQ.ap[12 * pc:12 * pc + 12, :], i=pu.ap[0:12, :]:
                     e.dma_start(out=o, in_=i), r=[pu], w=[CQ], dma=True)
            def cb_k(gi, n, pss):
                p.op('act', lambda e, o=KT[gi][n].ap, i=pss[0].ap: e.activation(out=o, in_=i, func=AF.Copy),
                     r=[pss[0]], w=[KT[gi][n]])
            linear([[(('w_kvf', None), j * 128, 128)] for j in range(6)],
                   lambda kc, n: (B1[kc][n].ap, B1[kc][n]), range(NT), cb_k)
            for cj in range(6):
                su, sap = wjob_k(('w_kvf', None), 768 + cj * 128, 128)
                for n in range(NT):
                    ps = psg.next()
                    for tb in range(4):
                        for kc in range(KC):
                            p.op('pe', lambda e, o=ps.ap[:, tb * 128:(tb + 1) * 128],
                                 a=B1[kc][n].ap[:, tb * 128:(tb + 1) * 128], b=sap[:, kc, :],
                                 st=(kc == 0), sp=(kc == KC - 1):
                                 e.matmul(o, lhsT=a, rhs=b, start=st, stop=sp), r=[su, B1[kc][n]], w=[ps])
                    p.op('act', lambda e, o=Vall[:, 4 * n:4 * n + 4, cj * 128:(cj + 1) * 128],
                         i=ps.ap.rearrange("p (b c) -> p b c", b=4):
                         e.activation(out=o, in_=i, func=AF.Copy), r=[ps], w=[V[4 * n + t] for t in range(4)])


        def mixer_b(l, hook):
            rmsnorm_x(B1, ('g_mix', l))
            mem_kv(l)
            wd = ('w_in_b', l - 2)
            PT = Ring(PTB)
            DG = Ring(DGB)
            for j in range(6):
                p.op('dve', lambda e, j=j: e.memset(QTA[j].ap, 0.0), w=[QTA[j]])
                p.op('dve', lambda e, j=j: e.memset(QTB[j].ap, 0.0), w=[QTB[j]])
            for n in range(NT):
                def cb_q(gi, n_, pss):
                    if gi < 6:
                        p.op('act', lambda e, o=QTA[gi].ap[0:64, :], i=pss[0].ap[0:64, :]:
                             e.activation(out=o, in_=i, func=AF.Copy, scale=0.125), r=[pss[0]], w=[QTA[gi]])
                        p.op('act', lambda e, o=QTB[gi].ap[64:128, :], i=pss[0].ap[64:128, :]:
                             e.activation(out=o, in_=i, func=AF.Copy, scale=0.125), r=[pss[0]], w=[QTB[gi]])
                    else:
                        p.op('act', lambda e, o=QTM[gi - 6].ap, i=pss[0].ap:
                             e.activation(out=o, in_=i, func=AF.Copy, scale=0.125), r=[pss[0]], w=[QTM[gi - 6]])
                linear([[(wd, c * 128, 128)] for c in range(KC)],
                       lambda kc, n_: (B1[kc][n_].ap, B1[kc][n_]), [n], cb_q, ring=pss3)
                if n == 0:
                    hook([QTM[1]])

                def blocks(n=n):
                    out = []
                    for kb in range(4 * n + 4):
                        i = kb - 4 * n
                        out.append((kb, max(0, i) * 128, i >= 0))
                    return out
                pairs = [dict(q=(QTA[j], QTB[j]),
                              kt=(lambda hh, kb, j=j: (KTc[j][:, kb * 128:(kb + 1) * 128], KT[j][kb // 4])),
                              v=(lambda hh, kb, j=j: (Vall[:, kb, j * 128:(j + 1) * 128], V[kb])),
                              blocks=blocks(), h0=2 * j, out=B1[j][n]) for j in range(6)]
                pairs += mem_pairs([QTM[0], QTM[1]], [B1[6][n], B1[7][n]])
                attn_tile(n, pairs, PT, RDENB, DG)
            wout_residual(l, B1)

        p.op('sp', lambda e: e.dma_start(out=VEC.ap, in_=vec_d), w=[VEC], dma=True)
        p.op('sp', lambda e: e.dma_start(out=MEMX.ap, in_=memT_d.rearrange("(c p) m -> p c m", p=128)),
             w=[MEMX], dma=True)
        Xall = view(OX, 65536, F32, "p (c t) -> p c t", c=KC)
        for n in range(NT):
            p.op('sp', lambda e, o=Xall[:, :, n * TW:(n + 1) * TW],
                 i=xT_d.rearrange("(c p) t -> p c t", p=128)[:, :, n * TW:(n + 1) * TW]: e.dma_start(out=o, in_=i),
                 w=[X[c][n] for c in range(KC)], dma=True)
        p.op('sp', lambda e: e.dma_start(out=IDENT.ap, in_=cst_d[:, 128:256]), w=[IDENT], dma=True)
        p.op('sp', lambda e: e.dma_start(out=ID12.ap[:, 0:12], in_=cst_d[:, 256:268]), w=[ID12], dma=True)
        p.op('dve', lambda e: e.memset(ONES.ap, 1.0), w=[ONES])
        for i in range(2):
            p.op('dve', lambda e, i=i: e.memset(ONESH[i].ap[:, 64 * i:64 * i + 64], 1.0), w=[ONESH[i]])
            p.op('dve', lambda e, i=i: e.memset(ONESH[i].ap[:, 64 * (1 - i):64 * (1 - i) + 64], 0.0), w=[ONESH[i]])
        rms_tile([(MEMX.ap[:, c, :], MEMX) for c in range(KC)],
                 [(MEMN.ap[:, c, :], MEMN) for c in range(KC)], 'g_mem', MEM)

        precast = {
            'A0': [('w_out', 0), ('w_up', 0), ('w_down', 0), ('w_mem_kv', 1), ('w_in_a', 1), ('w_out', 1)],
            'A1': [('w_up', 1), ('w_down', 1), ('w_kvf', None), ('w_mem_kv', 2), ('w_in_b', 0), ('w_out', 2)],
            'B2': [('w_up', 2), ('w_down', 2), ('w_mem_kv', 3), ('w_in_b', 1), ('w_out', 3)],
            'B3': [('w_up', 3), ('w_down', 3)],
        }

        def mk_hook(name):
            def hook(after):
                for i, wn in enumerate(precast.get(name, [])):
                    cast_w(*wn, after=(after if i == 0 else ()))
            return hook
        for wn in [('w_in_a', 0), ('w_mem_kv', 0)]:
            cast_w(*wn)
        stages = []
        for l in range(2):
            stages.append((f"A{l}", lambda l=l: mixer_a(l, mk_hook(f"A{l}"))))
            stages.append((f"F{l}", lambda l=l: ffn(l)))
        stages.append(("KV", kv_pre))
        for l in range(2, 4):
            stages.append((f"B{l}", lambda l=l: mixer_b(l, mk_hook(f"B{l}"))))
            stages.append((f"F{l}", lambda l=l: ffn(l)))
        final_norm = True
        for name, fn in stages:
            if stop is not None and stop == 'init':
                final_norm = False
                break
            fn()
            if stop is not None and name == stop:
                final_norm = False
                break
        if final_norm:
            rmsnorm_x(X, 'g_final', c_outer=True)
        OUT = [p.unit(f"OUT{c}") for c in range(KC)]
        for c in range(KC):
            p.op('sp', lambda e, i=Xc[c], o=outT_d[c * 128:(c + 1) * 128, :]: e.dma_start(out=o, in_=i),
                 r=X[c], w=[OUT[c]], dma=True)
        p.op('sp', None, r=OUT)
        p.finalize()

        import contextlib
        with contextlib.ExitStack() as es:
            sems = {e: es.enter_context(nc.semaphore(f"s_{e}")) for e in ENGS}
            dsems = {e: [es.enter_context(nc.semaphore(f"d_{e}{i}")) for i in range(Prog.R)]
                     for e in ('sp', 'pool')}
            block = es.enter_context(nc.Block())

            @block.tensor
            def _(h):
                p.emit('pe', h, sems, dsems)

            @block.scalar
            def _(h):
                p.emit('act', h, sems, dsems)

            @block.vector
            def _(h):
                p.emit('dve', h, sems, dsems)

            @block.gpsimd
            def _(h):
                p.emit('pool', h, sems, dsems)

            @block.sync
            def _(h):
                p.emit('sp', h, sems, dsems)
        for t in reversed(pst):
            t.__exit__(None, None, None)
    return nc, p


_CACHE = {}


def kernel(x, mem, g_mix, w_in_a, b_glu, w_dw_a, b_dw_a, ln_g, ln_b, g_kv, w_kvf, b_f,
           w_in_b, g_mem, w_mem_kv, w_out, g_ffn, w_up, w_dw_f, b_dw_f, w_down, g_final,
           _stop=None, _ncores=8):
    inp = dict(g_mix=g_mix, b_glu=b_glu, w_dw_a=w_dw_a, b_dw_a=b_dw_a, ln_g=ln_g, ln_b=ln_b,
               g_kv=g_kv, b_f=b_f, g_mem=g_mem, g_ffn=g_ffn, w_dw_f=w_dw_f, b_dw_f=b_dw_f,
               g_final=g_final)
    inp = {k: np.asarray(v, np.float32) for k, v in inp.items()}
    vec, vindex = _vec_layout(inp)
    cst = _consts()
    key = (_stop,)
    if key not in _CACHE:
        _CACHE[key] = build(vindex, vec.shape[1], stop=_stop)[0]
    nc = _CACHE[key]
    x = np.asarray(x, np.float32)
    mem = np.asarray(mem, np.float32)
    shared = dict(
        vec=vec, cst=cst,
        w_in_a=np.ascontiguousarray(np.asarray(w_in_a, np.float32)),
        w_kvf=np.ascontiguousarray(np.asarray(w_kvf, np.float32)),
        w_in_b=np.ascontiguousarray(np.asarray(w_in_b, np.float32)),
        w_mem_kv=np.ascontiguousarray(np.asarray(w_mem_kv, np.float32)),
        w_out=np.ascontiguousarray(np.asarray(w_out, np.float32)),
        w_up=np.ascontiguousarray(np.asarray(w_up, np.float32)),
        w_down=np.ascontiguousarray(np.asarray(w_down, np.float32)),
    )
    in_maps = []
    for b in range(_ncores):
        m = dict(shared)
        m["xT"] = np.ascontiguousarray(x[b].T)
        m["memT"] = np.ascontiguousarray(mem[b].T)
        in_maps.append(m)
    res = run_bass_kernel_spmd(nc, in_maps, core_ids=list(range(_ncores)))
    out = np.stack([np.ascontiguousarray(np.asarray(r["outT"], np.float32).T) for r in res.results], axis=0)
    return out
```

```python
import os
import numpy as np
import concourse.bass as bass
import concourse.mybir as mybir
from concourse.bass_utils import run_bass_kernel_spmd

F32 = mybir.dt.float32
BF16 = mybir.dt.bfloat16
U8 = mybir.dt.uint8
AF = mybir.ActivationFunctionType
ALU = mybir.AluOpType

D = 1024
S = 2048
NT = 4
TW = 512
KC = 8
DFF = 2816
NFF = 22
NH = 12
MEM = 256
ARENA = 212832
SAME_ENG_SYNC = True
ENGS = ['pe', 'act', 'dve', 'pool', 'sp']


class Unit:
    __slots__ = ('name', 'lo', 'hi', 'lastw', 'readers', 'alias', 'ap', 'gen')

    def __init__(self, name, lo=None, hi=None, ap=None):
        self.name = name
        self.lo = lo
        self.hi = hi
        self.ap = ap
        self.lastw = None
        self.readers = []
        self.alias = []
        self.gen = 0


class Op:
    __slots__ = ('eng', 'fn', 'idx', 'waits', 'dwaits', 'is_dma', 'signal', 'sigval',
                 'dslot', 'dval', 'desc')

    def __init__(self, eng, fn, is_dma):
        self.eng = eng
        self.fn = fn
        self.is_dma = is_dma
        self.waits = []
        self.dwaits = []
        self.signal = False
        self.sigval = 0
        self.dslot = 0
        self.dval = 0


class Prog:
    R = 8

    def __init__(self):
        self.streams = {e: [] for e in ENGS}
        self.sb_units = []
        self.known = {e: {f: -1 for f in ENGS} for e in ENGS}
        self.kdma = {e: {} for e in ENGS}
        self.dmaops = {e: [] for e in ENGS}

    def unit(self, name, lo=None, hi=None, ap=None):
        u = Unit(name, lo, hi, ap)
        if lo is not None:
            for v in self.sb_units:
                if v.lo < hi and lo < v.hi:
                    v.alias.append(u)
                    u.alias.append(v)
            self.sb_units.append(u)
        return u

    def op(self, eng, fn, r=(), w=(), dma=False):
        o = Op(eng, fn, dma)
        o.idx = len(self.streams[eng])
        o.desc = ([u.name for u in r], [u.name for u in w])
        need = {}
        dneed = []
        seen = set()

        def add(d, kind):
            if d is None or id(d) in seen:
                return
            if d.is_dma:
                seen.add(id(d))
                dneed.append(d)
                return
            if d.eng == eng:
                if eng == 'pe' or kind == 'war' or not SAME_ENG_SYNC:
                    return
            if need.get(d.eng, -1) < d.idx:
                need[d.eng] = d.idx

        for u in r:
            add(u.lastw, 'raw')
            for v in u.alias:
                add(v.lastw, 'raw')
        for u in w:
            for v in [u] + u.alias:
                add(v.lastw, 'waw')
                for rd in v.readers:
                    add(rd, 'war')
        if dma:
            k = len(self.dmaops[eng])
            o.dslot = k % self.R
            o.dval = 16 * (k // self.R + 1)
            if k >= self.R:
                dneed.append(self.dmaops[eng][k - self.R])
            self.dmaops[eng].append(o)
        for f, idx in need.items():
            if idx > self.known[eng][f]:
                self.known[eng][f] = idx
                dep = self.streams[f][idx]
                dep.signal = True
                o.waits.append(dep)
        for d in dneed:
            key = (d.eng, d.dslot)
            if self.kdma[eng].get(key, 0) < d.dval:
                self.kdma[eng][key] = d.dval
                o.dwaits.append(d)
        for u in r:
            u.readers.append(o)
        for u in w:
            u.lastw = o
            u.readers = []
        self.streams[eng].append(o)
        return o

    def finalize(self):
        for e in ENGS:
            c = 0
            for o in self.streams[e]:
                if o.signal and not o.is_dma:
                    c += 1
                    o.sigval = c

    def emit(self, e, h, sems, dsems):
        for o in self.streams[e]:
            for d in o.waits:
                h.wait_ge(sems[d.eng], d.sigval)
            for d in o.dwaits:
                h.wait_ge(dsems[d.eng][d.dslot], d.dval)
            if o.fn is None:
                continue
            ins = o.fn(h)
            if o.is_dma:
                ins.then_inc(dsems[e][o.dslot], 16)
            elif o.signal:
                ins.then_inc(sems[e], 1)


class Ring:
    def __init__(self, units):
        self.units = units
        self.i = 0

    def next(self):
        u = self.units[self.i % len(self.units)]
        self.i += 1
        u.gen += 1
        return u


def _vec_layout(inp):
    cols = []
    index = {}

    def add(name, arr):
        arr = np.asarray(arr, np.float32)
        index[name] = sum(c.shape[1] for c in cols)
        cols.append(arr)

    def chunked(v, n):
        return np.ascontiguousarray(np.asarray(v, np.float32).reshape(n, 128).T)

    for l in range(4):
        add(('g_mix', l), chunked(inp['g_mix'][l], 8))
        add(('g_ffn', l), chunked(inp['g_ffn'][l], 8))
        wf = np.asarray(inp['w_dw_f'][l], np.float32)
        add(('w_dw_f', l), np.ascontiguousarray(
            wf.T.reshape(44, 128, 3).transpose(1, 0, 2).reshape(128, 132)))
        add(('b_dw_f', l), chunked(inp['b_dw_f'][l], 44))
    for l in range(2):
        add(('b_glu', l), chunked(inp['b_glu'][l], 12))
        wa = np.asarray(inp['w_dw_a'][l], np.float32)
        add(('w_dw_a', l), np.ascontiguousarray(
            wa.T.reshape(6, 128, 31).transpose(1, 0, 2).reshape(128, 186)))
        add(('b_dw_a', l), chunked(inp['b_dw_a'][l], 6))
        add(('ln_g', l), chunked(inp['ln_g'][l], 6))
        add(('ln_b', l), chunked(inp['ln_b'][l], 6))
    add('g_kv', chunked(inp['g_kv'], 8))
    add('g_mem', chunked(inp['g_mem'], 8))
    add('g_final', chunked(inp['g_final'], 8))
    bf = np.zeros((128, 1), np.float32)
    bf[:12, 0] = np.asarray(inp['b_f'], np.float32)
    add('b_f', bf)
    add('eps_rms', np.full((128, 1), 1e-6, np.float32))
    add('eps_ln', np.full((128, 1), 1e-5, np.float32))
    vec = np.concatenate(cols, axis=1)
    return np.ascontiguousarray(vec), index


def _consts():
    k = np.arange(128)[:, None]
    q = np.arange(128)[None, :]
    mask = np.where(k <= q, 0.0, -30000.0).astype(np.float32)
    ident = np.eye(128, dtype=np.float32)
    id12 = np.zeros((128, 12), np.float32)
    id12[:12, :12] = np.eye(12, dtype=np.float32)
    sel = np.zeros((128, 12, 128), np.float32)
    for pc in range(3):
        for h in range(12):
            sel[12 * pc + h, h, :] = 1.0
    cst = np.concatenate([mask, ident, id12, sel.reshape(128, 12 * 128)], axis=1)
    return np.ascontiguousarray(cst)


_VEC_INDEX_CACHE = {}


def build(vec_index, nv, stop=None):
    nc = bass.Bass("TRN2", target_bir_lowering=False)
    dr = {}

    def din(name, shape):
        dr[name] = nc.dram_tensor(name, list(shape), F32, kind="ExternalInput").ap()
        return dr[name]

    xT_d = din("xT", [D, S])
    memT_d = din("memT", [D, MEM])
    vec_d = din("vec", [128, nv])
    NCST = 128 + 128 + 12 + 12 * 128
    cst_d = din("cst", [128, NCST])
    w_in_a_d = din("w_in_a", [2, D, 1792])
    w_kvf_d = din("w_kvf", [D, 1548])
    w_in_b_d = din("w_in_b", [2, D, D])
    w_mem_kv_d = din("w_mem_kv", [4, D, 512])
    w_out_d = din("w_out", [4, D, D])
    w_up_d = din("w_up", [4, D, 2 * DFF])
    w_down_d = din("w_down", [4, DFF, D])
    outT_d = nc.dram_tensor("outT", [D, S], F32, kind="ExternalOutput").ap()
    wsrc = dict(w_in_a=w_in_a_d, w_kvf=w_kvf_d, w_in_b=w_in_b_d, w_mem_kv=w_mem_kv_d, w_out=w_out_d,
                w_up=w_up_d, w_down=w_down_d)
    wbf = {k: nc.dram_tensor(k + "_bf", list(v.shape), BF16).ap() for k, v in wsrc.items()}

    p = Prog()
    WU = {}

    def wsel(d, name, l):
        return d[name] if l is None else d[name][l]

    cast_chain = [None]

    def cast_w(name, l=None, after=(), step=2048):
        src = wsel(wsrc, name, l)
        dst = wsel(wbf, name, l)
        nrow, ncol = src.shape[-2], src.shape[-1]
        blocks = []
        first = True
        for c0 in range(0, ncol, step):
            c1 = min(ncol, c0 + step)
            u = p.unit(f"WU_{name}_{l}_{c0}")
            blocks.append((c0, c1, u))
            rows = 128 if (c1 - c0) > 1024 else 256
            for r0 in range(0, nrow, rows):
                r1 = min(nrow, r0 + rows)
                rd = list(after) if first else []
                if cast_chain[0] is not None and (first or r0 == 0):
                    rd.append(cast_chain[0])
                first = False
                p.op('pool', lambda e, o=dst[r0:r1, c0:c1], i=src[r0:r1, c0:c1]: e.dma_start(out=o, in_=i, max_dma_last_dim=2048),
                     r=rd, w=[u], dma=True)
            cast_chain[0] = u
        WU[(name, l)] = blocks

    def wunits(wd, c0=None, c1=None):
        return [u for (a, b, u) in WU[wd] if c0 is None or (a < c1 and c0 < b)]

    with nc.sbuf_tensor("arena", [128, ARENA], U8) as arena_t:
        arena = arena_t[:]

        def view(off, nbytes, dt, pat=None, **kw):
            a = arena[:, off:off + nbytes].bitcast(dt)
            if pat is not None:
                a = a.rearrange(pat, **kw)
            return a

        OX = 0
        OB1 = OX + 65536
        OB2 = OB1 + 32768
        OPH = OB2 + 32768
        OMISC = OPH + 57344
        assert OMISC == 188416

        def mk(name, off, nbytes, dt, pat=None, **kw):
            return p.unit(name, off, off + nbytes, view(off, nbytes, dt, pat, **kw))

        X = [[mk(f"X{c}_{n}", OX + c * 8192 + n * 2048, 2048, F32) for n in range(NT)]
             for c in range(KC)]
        Xc = [view(OX + c * 8192, 8192, F32) for c in range(KC)]
        B1 = [[mk(f"B1_{c}_{n}", OB1 + c * 4096 + n * 1024, 1024, BF16) for n in range(NT)]
              for c in range(KC)]
        B2 = [[mk(f"B2_{c}_{n}", OB2 + c * 4096 + n * 1024, 1024, BF16) for n in range(NT)]
              for c in range(KC)]
        B2c = [view(OB2 + c * 4096, 4096, BF16) for c in range(KC)]
        OU = OB2 + 16384
        SQ = [mk(f"SQ{i}", OU + i * 1024, 1024, BF16) for i in range(2)]
        SD = mk("SD", OU + 2048, 2048, F32)
        RSTD = mk("RSTD", OU + 4096, 2048, F32)
        SQ4 = [mk(f"SQ4_{i}", OU + 4096 + i * 1024, 1024, BF16) for i in range(4)]
        RS4 = [mk(f"RS4_{i}", OU + 8192 + i * 2048, 2048, F32) for i in range(4)]
        TG = [mk(f"TG{i}", OU + i * 2048, 2048, F32) for i in range(3)]
        TV = [mk(f"TV{i}", OU + 6144 + i * 2048, 2048, F32) for i in range(3)]
        HG = [mk(f"HG{i}", OU + 12288 + i * 8, 8, F32) for i in range(2)]
        HV = [mk(f"HV{i}", OU + 12304 + i * 8, 8, F32) for i in range(2)]
        ACTB = [[B2[g][n] for n in range(NT)] for g in range(4)]
        QTA = [mk(f"QTA{j}", OB2 + j * 2048, 1024, BF16) for j in range(6)]
        QTB = [mk(f"QTB{j}", OB2 + j * 2048 + 1024, 1024, BF16) for j in range(6)]
        QTM = [mk(f"QTM{j}", OB2 + 12288 + j * 1024, 1024, BF16) for j in range(2)]
        PTB = [mk(f"PTB{i}", OU + 6144 + i * 1024, 1024, BF16) for i in range(4)]
        RDENB = mk("RDENB", OU + 10240, 2048, F32)
        DGB = [mk(f"DGB{i}", OU + 12288 + i * 512, 512, F32) for i in range(2)]
        LF = mk("LF", OB2, 8192, F32)
        HI = mk("HIp", OB2, 4096, BF16)
        MID = mk("MIDp", OB2 + 4096, 4096, BF16)
        ONESF = mk("ONESF", OB2 + 8192, 8192, F32)
        CC = mk("CC", OB2 + 16384, 8192, F32)
        LO = mk("LOp", OB2 + 24576, 4096, BF16)
        DIAG = [mk(f"DIAG{i}", OPH + i * 7936, 7936, BF16, "p (k m) -> p k m", k=31)
                for i in range(2)]
        oa = OPH + 15872
        Y = [mk(f"Y{j}", oa + j * 2048, 2048, F32) for j in range(6)]
        YB = [mk(f"YB{j}", oa + 12288 + j * 1024, 1024, BF16) for j in range(6)]
        YSQ = [mk(f"YSQ{j}", oa + 18432 + j * 1024, 1024, BF16) for j in range(6)]
        oa += 24576
        MU = mk("MU", oa, 2048, F32)
        T2 = mk("T2", oa + 2048, 2048, F32)
        RSTL = mk("RSTL", oa + 4096, 2048, F32)
        SG = [mk(f"SG{i}", oa + 6144 + i * 2048, 2048, F32) for i in range(2)]
        PTA = [mk(f"PTA{i}", oa + 10240 + i * 1024, 1024, BF16) for i in range(4)]
        RDENA = mk("RDENA", oa + 14336, 2048, F32)
        IDENT = mk("IDENT", oa + 16384, 512, F32)
        assert oa + 16896 <= OMISC
        MEMX = mk("MEMX", OPH, 8192, F32, "p (c m) -> p c m", c=8)
        KT = [[mk(f"KT{j}_{n}", OPH + j * 4096 + n * 1024, 1024, BF16) for n in range(NT)]
              for j in range(6)]
        KTc = [view(OPH + j * 4096, 4096, BF16) for j in range(6)]
        OV = OPH + 24576
        V = [mk(f"V{kb}", OV + kb * 1536, 1536, BF16) for kb in range(16)]
        Vall = view(OV, 24576, BF16, "p (kb c) -> p kb c", kb=16)
        CQ = mk("CQ", OV + 24576, 4096, BF16)
        SEL = mk("SEL", OV + 28672, 3072, BF16, "p (h m) -> p h m", h=12)
        NEGCK = mk("NEGCK", OV + 31744, 768, F32)
        MASK = mk("MASK", OV + 32512, 256, BF16)
        assert OV + 32768 <= OMISC
        om = OMISC
        VEC = mk("VEC", om, nv * 4, F32)
        om += ((nv * 4 + 31) // 32) * 32
        MEMN = mk("MEMN", om, 4096, BF16, "p (c m) -> p c m", c=8)
        om += 4096
        MEMKT = mk("MEMKT", om, 1024, BF16, "p (c m) -> p c m", c=2)
        om += 1024
        MEMV = mk("MEMV", om, 1024, BF16, "p (b c) -> p b c", b=2)
        om += 1024
        ONES = mk("ONES", om, 256, BF16)
        om += 256
        ID12 = mk("ID12", om, 64, F32)
        om += 64
        ONESH = [mk(f"ONESH{i}", om + i * 256, 256, BF16) for i in range(2)]
        om += 512
        NSLOT = (ARENA - om) // 2048
        assert NSLOT >= 6, NSLOT
        slot_units = [mk(f"WS{i}", om + i * 2048, 2048, BF16) for i in range(NSLOT)]
        slots = Ring(slot_units)

        pst = [nc.psum_tensor("psbig", [128, 4096], F32)]
        psbig = pst[0].__enter__()[:]
        PSU = [p.unit(f"PS{i}", ap=psbig[:, i * 512:(i + 1) * 512]) for i in range(8)]
        psg = Ring(PSU[0:4])
        pss3 = Ring(PSU[0:3])
        psacc = Ring(PSU[3:8])
        psall = Ring(PSU)
        ps6 = Ring(PSU[0:6])

        def vcol(key, i=0, n=1, rows=128):
            b = vec_index[key] + i
            return VEC.ap[0:rows, b:b + n]

        def load_w(dram_ap, shape_pat=None, **kw):
            s = slots.next()
            dst = s.ap if shape_pat is None else s.ap.rearrange(shape_pat, **kw)
            return s, dst

        def wjob_k(wd, c0, m):
            s = slots.next()
            dst = s.ap.rearrange("p (k m) -> p k m", k=8)
            if m < 64:
                src = wsel(wsrc, *wd).rearrange("(k p) n -> p k n", p=128)[:, :, c0:c0 + m]
                p.op('pool', lambda e, d=dst[:, :, 0:m], s_=src: e.dma_start(out=d, in_=s_),
                     w=[s], dma=True)
                return s, dst
            src = wsel(wbf, *wd).rearrange("(k p) n -> p k n", p=128)[:, :, c0:c0 + m]
            p.op('sp', lambda e, d=dst[:, :, 0:m], s_=src: e.dma_start(out=d, in_=s_),
                 r=wunits(wd, c0, c0 + m), w=[s], dma=True)
            return s, dst

        def linear(groups, rhs_fn, tiles, cb, N=TW, ring=None):
            for gi, grp in enumerate(groups):
                sl = [wjob_k(wd, c0, m) + (m,) for (wd, c0, m) in grp]
                for n in tiles:
                    pss = []
                    for (su, sap, m) in sl:
                        ps = (ring or psg).next()
                        for kc in range(KC):
                            rap, ru = rhs_fn(kc, n)
                            p.op('pe', lambda e, o=ps.ap[0:m, 0:N], a=sap[:, kc, 0:m], b=rap,
                                 st=(kc == 0), sp=(kc == KC - 1):
                                 e.matmul(o, lhsT=a, rhs=b, start=st, stop=sp),
                                 r=[su, ru], w=[ps])
                        pss.append(ps)
                    cb(gi, n, pss)

        def rms_tile(srcs, dsts, gkey, N, eps_key='eps_rms', inv=1.0 / D):
            ps = psg.next()
            for c in range(KC):
                sap, su = srcs[c]
                sq = SQ[c % 2]
                p.op('act', lambda e, o=sq.ap[:, 0:N], i=sap: e.activation(out=o, in_=i, func=AF.Square),
                     r=[su], w=[sq])
                p.op('pe', lambda e, o=ps.ap[:, 0:N], b=sq.ap[:, 0:N], st=(c == 0), sp=(c == KC - 1):
                     e.matmul(o, lhsT=ONES.ap, rhs=b, start=st, stop=sp), r=[sq, ONES], w=[ps])
            p.op('act', lambda e, o=SD.ap[:, 0:N], i=ps.ap[:, 0:N]:
                 e.activation(out=o, in_=i, func=AF.Ln, bias=vcol(eps_key), scale=inv),
                 r=[ps, VEC], w=[SD])
            p.op('act', lambda e, o=RSTD.ap[:, 0:N], i=SD.ap[:, 0:N]:
                 e.activation(out=o, in_=i, func=AF.Exp, scale=-0.5),
                 r=[SD], w=[RSTD])
            for c in range(KC):
                sap, su = srcs[c]
                dap, du = dsts[c]
                p.op('dve', lambda e, o=dap, i=sap, g=vcol(gkey, c), rs=RSTD.ap[:, 0:N]:
                     e.scalar_tensor_tensor(out=o, in0=i, scalar=g, in1=rs, op0=ALU.mult, op1=ALU.mult),
                     r=[su, RSTD, VEC], w=[du])

        fused_stats = [False]

        def rmsnorm_x(dst, gkey, c_outer=False):
            if fused_stats[0]:
                fused_stats[0] = False
                rms_apply(dst, gkey, c_outer)
                return
            rms_stats()
            rms_apply(dst, gkey, c_outer)

        def rms_lnexp(n, ps):
            p.op('act', lambda e, o=RS4[n].ap, i=ps.ap:
                 e.activation(out=o, in_=i, func=AF.Ln, bias=vcol('eps_rms'), scale=1.0 / D),
                 r=[ps, VEC], w=[RS4[n]])
            p.op('act', lambda e, o=RS4[n].ap: e.activation(out=o, in_=o, func=AF.Exp, scale=-0.5),
                 r=[RS4[n]], w=[RS4[n]])

        def rms_apply(dst, gkey, c_outer):
            order = [(n, c) for n in range(NT) for c in range(KC)]
            if c_outer:
                order = [(n, c) for c in range(KC) for n in range(NT)]
            for (n, c) in order:
                p.op('dve', lambda e, o=dst[c][n].ap, i=X[c][n].ap, g=vcol(gkey, c), rs=RS4[n].ap:
                     e.scalar_tensor_tensor(out=o, in0=i, scalar=g, in1=rs, op0=ALU.mult, op1=ALU.mult),
                     r=[X[c][n], RS4[n], VEC], w=[dst[c][n]])

        def rms_stats():
            pss = []
            k = 0
            for n in range(NT):
                ps = psg.next()
                pss.append(ps)
                for c in range(KC):
                    sq = SQ4[k % 4]
                    k += 1
                    if c % 2 == 0:
                        p.op('act', lambda e, o=sq.ap, i=X[c][n].ap: e.activation(out=o, in_=i, func=AF.Square),
                             r=[X[c][n]], w=[sq])
                    else:
                        p.op('dve', lambda e, o=sq.ap, i=X[c][n].ap: e.tensor_tensor(out=o, in0=i, in1=i, op=ALU.mult),
                             r=[X[c][n]], w=[sq])
                    p.op('pe', lambda e, o=ps.ap, b=sq.ap, st=(c == 0), sp=(c == KC - 1):
                         e.matmul(o, lhsT=ONES.ap, rhs=b, start=st, stop=sp), r=[sq, ONES], w=[ps])
            for n in range(NT):
                rms_lnexp(n, pss[n])

        def attn_tile(n, pairs, PT, RDEN, DG, depth=2):
            items = []
            for pi, pd in enumerate(pairs):
                nb = len(pd['blocks'])
                for hh in range(2):
                    for bi, blk in enumerate(pd['blocks']):
                        items.append((pi, hh, bi, blk, bi == 0, bi == nb - 1))
            acc = {}

            def scores(it):
                pi, hh, bi, (kb, c0, diag), first, last = it
                pd = pairs[pi]
                pr = slice(64 * hh, 64 * hh + 64)
                sps = pss3.next()
                kap, ku = pd['kt'](hh, kb)
                if pd['h0'] is None:
                    qt_unit = pd['q']
                    p.op('pe', lambda e, o=sps.ap[:, c0:TW], a=kap, b=qt_unit.ap[pr, c0:TW]:
                         e.matmul(o, lhsT=a, rhs=b, start=True, stop=True),
                         r=[ku, qt_unit], w=[sps])
                    bias = None
                else:
                    h = pd['h0'] + hh
                    qt_unit = pd['q'][hh]
                    p.op('pe', lambda e, o=sps.ap[:, c0:TW], a=kap, b=qt_unit.ap[:, c0:TW]:
                         e.matmul(o, lhsT=a, rhs=b, start=True, stop=False),
                         r=[ku, qt_unit], w=[sps])
                    p.op('pe', lambda e, o=sps.ap[:, c0:TW], a=SEL.ap[:, h, :],
                         b=CQ.ap[:, n * TW + c0:(n + 1) * TW]:
                         e.matmul(o, lhsT=a, rhs=b, start=False, stop=True),
                         r=[SEL, CQ], w=[sps])
                    bias = NEGCK.ap[:, kb * 12 + h:kb * 12 + h + 1]
                pt = PT.next()
                if diag:
                    dg = DG.next()
                    p.op('dve', lambda e, o=dg.ap, a=sps.ap[:, c0:c0 + 128]:
                         e.tensor_tensor(out=o, in0=a, in1=MASK.ap, op=ALU.add),
                         r=[sps, MASK], w=[dg])
                    p.op('act', lambda e, o=pt.ap[:, c0:c0 + 128], i=dg.ap, b=bias:
                         e.activation(out=o, in_=i, func=AF.Exp, bias=b),
                         r=[dg, NEGCK], w=[pt])
                    if c0 + 128 < TW:
                        p.op('act', lambda e, o=pt.ap[:, c0 + 128:TW], i=sps.ap[:, c0 + 128:TW], b=bias:
                             e.activation(out=o, in_=i, func=AF.Exp, bias=b),
                             r=[sps, NEGCK], w=[pt])
                elif bias is not None:
                    p.op('act', lambda e, o=pt.ap[:, c0:TW], i=sps.ap[:, c0:TW], b=bias:
                         e.activation(out=o, in_=i, func=AF.Exp, bias=b),
                         r=[sps, NEGCK], w=[pt])
                else:
                    p.op('act', lambda e, o=pt.ap[:, c0:TW], i=sps.ap[:, c0:TW]:
                         e.activation(out=o, in_=i, func=AF.Exp),
                         r=[sps], w=[pt])
                return pt

            def pv(it, pt):
                pi, hh, bi, (kb, c0, diag), first, last = it
                pd = pairs[pi]
                pr = slice(64 * hh, 64 * hh + 64)
                fox = pd['h0'] is not None
                if pi not in acc:
                    acc[pi] = dict(den=psacc.next())
                a_ = acc[pi]
                den = a_['den']
                vap, vu = pd['v'](hh, kb)
                out_unit = pd['out']
                if fox:
                    if hh not in a_:
                        a_[hh] = psacc.next()
                    num = a_[hh]
                    p.op('pe', lambda e, o=num.ap[:, c0:TW], a=vap, b=pt.ap[:, c0:TW], st=first, sp=last:
                         e.matmul(o, lhsT=a, rhs=b, start=st, stop=sp), r=[vu, pt], w=[num])
                    p.op('pe', lambda e, o=den.ap[:, c0:TW], a=ONESH[hh].ap, b=pt.ap[:, c0:TW],
                         st=(first and hh == 0), sp=(last and hh == 1):
                         e.matmul(o, lhsT=a, rhs=b, start=st, stop=sp), r=[ONESH[hh], pt], w=[den])
                    if hh == 1 and last:
                        na, nb_ = a_[0], a_[1]
                        p.op('dve', lambda e, d=den: e.reciprocal(out=RDEN.ap, in_=d.ap), r=[den], w=[RDEN])
                        p.op('dve', lambda e, o=out_unit.ap[0:64, :], nm=na: e.tensor_tensor(out=o, in0=nm.ap[0:64, :], in1=RDEN.ap[0:64, :], op=ALU.mult),
                             r=[na, RDEN], w=[out_unit])
                        p.op('dve', lambda e, o=out_unit.ap[64:128, :], nm=nb_: e.tensor_tensor(out=o, in0=nm.ap[64:128, :], in1=RDEN.ap[64:128, :], op=ALU.mult),
                             r=[nb_, RDEN], w=[out_unit])
                else:
                    if 'num' not in a_:
                        a_['num'] = psacc.next()
                    num = a_['num']
                    p.op('pe', lambda e, o=num.ap[pr, c0:TW], a=vap, b=pt.ap[:, c0:TW], st=first, sp=last:
                         e.matmul(o, lhsT=a, rhs=b, start=st, stop=sp), r=[vu, pt], w=[num])
                    p.op('pe', lambda e, o=den.ap[pr, c0:TW], a=ONES.ap[:, 0:64], b=pt.ap[:, c0:TW], st=first, sp=last:
                         e.matmul(o, lhsT=a, rhs=b, start=st, stop=sp), r=[ONES, pt], w=[den])
                    if hh == 1 and last:
                        p.op('dve', lambda e, d=den: e.reciprocal(out=RDEN.ap, in_=d.ap), r=[den], w=[RDEN])
                        p.op('dve', lambda e, o=out_unit.ap, nm=num: e.tensor_tensor(out=o, in0=nm.ap, in1=RDEN.ap, op=ALU.mult),
                             r=[num, RDEN], w=[out_unit])

            pend = []
            for it in items:
                pend.append((it, scores(it)))
                if len(pend) > depth:
                    pv(*pend.pop(0))
            while pend:
                pv(*pend.pop(0))

        def mem_pairs(qsrc, outs):
            return [dict(q=qsrc[jp],
                         kt=(lambda hh, kb, jp=jp: (MEMKT.ap[64 * hh:64 * hh + 64, jp, kb * 128:(kb + 1) * 128], MEMKT)),
                         v=(lambda hh, kb, jp=jp: (MEMV.ap[:, kb, (2 * jp + hh) * 64:(2 * jp + hh + 1) * 64], MEMV)),
                         blocks=[(0, 0, False), (1, 0, False)], h0=None, out=outs[jp]) for jp in range(2)]

        def mem_kv(l):
            wd = ('w_mem_kv', l)

            def cbk(gi, n, pss):
                p.op('act', lambda e, o=MEMKT.ap[:, gi, :], i=pss[0].ap[:, 0:MEM]:
                     e.activation(out=o, in_=i, func=AF.Copy), r=[pss[0]], w=[MEMKT])
            linear([[(wd, j * 128, 128)] for j in range(2)],
                   lambda kc, n: (MEMN.ap[:, kc, :], MEMN), [0], cbk, N=MEM)
            for cj in range(2):
                su, sap = wjob_k(wd, 256 + cj * 128, 128)
                ps = psg.next()
                for mb in range(2):
                    for kc in range(KC):
                        p.op('pe', lambda e, o=ps.ap[:, mb * 128:(mb + 1) * 128],
                             a=MEMN.ap[:, kc, mb * 128:(mb + 1) * 128], b=sap[:, kc, :],
                             st=(kc == 0), sp=(kc == KC - 1):
                             e.matmul(o, lhsT=a, rhs=b, start=st, stop=sp), r=[su, MEMN], w=[ps])
                p.op('act', lambda e, o=MEMV.ap[:, :, cj * 128:(cj + 1) * 128],
                     i=ps.ap[:, 0:256].rearrange("p (b c) -> p b c", b=2):
                     e.activation(out=o, in_=i, func=AF.Copy), r=[ps], w=[MEMV])

        def wout_residual(l, src):
            def cb(gi, n, pss):
                p.op('dve', lambda e, o=X[gi][n].ap, i=pss[0].ap:
                     e.tensor_tensor(out=o, in0=o, in1=i, op=ALU.add), r=[pss[0], X[gi][n]], w=[X[gi][n]])
            linear([[(('w_out', l), dc * 128, 128)] for dc in range(KC)],
                   lambda kc, n: (src[kc][n].ap, src[kc][n]), range(NT), cb)

        def ffn(l):
            rmsnorm_x(B1, ('g_ffn', l))
            wu = ('w_up', l)
            wdn = wbf['w_down'][l]
            groups = [list(range(g, min(g + 4, NFF))) for g in range(0, NFF, 4)]
            cnt = [0]
            pend = [None]
            for grp in groups:
                for gi, f in enumerate(grp):
                    sg_u, sg_ap = wjob_k(wu, f * 128, 128)
                    sv_u, sv_ap = wjob_k(wu, DFF + f * 128, 128)
                    wg = vec_index[('w_dw_f', l)] + f * 3
                    wv = vec_index[('w_dw_f', l)] + (NFF + f) * 3
                    bg = vec_index[('b_dw_f', l)] + f
                    bv = vec_index[('b_dw_f', l)] + NFF + f
                    for n in range(NT):
                        banks = (n, 4 + n)
                        for (su, sap, bk) in ((sg_u, sg_ap, banks[0]), (sv_u, sv_ap, banks[1])):
                            for kc in range(KC):
                                p.op('pe', lambda e, o=PSU[bk].ap, a=sap[:, kc, :], b=B1[kc][n].ap,
                                     st=(kc == 0), sp=(kc == KC - 1):
                                     e.matmul(o, lhsT=a, rhs=b, start=st, stop=sp),
                                     r=[su, B1[kc][n]], w=[PSU[bk]])
                        k = cnt[0]
                        cnt[0] += 1
                        tg = TG[k % 3]
                        tv = TV[k % 3]
                        paths = ((banks[0], wg, bg, tg), (banks[1], wv, bv, tv))
                        for (bk, wb, bb, t) in paths:
                            p.op('act', lambda e, o=t.ap, i=PSU[bk].ap, s_=VEC.ap[:, wb + 2:wb + 3], b_=VEC.ap[:, bb:bb + 1]:
                                 e.activation(out=o, in_=i, func=AF.Identity, bias=b_, scale=s_),
                                 r=[PSU[bk], VEC], w=[t])
                        if pend[0] is not None:
                            pend[0]()
                        for tap, sh in ((1, 1), (0, 2)):
                            for (bk, wb, bb, t) in paths:
                                if n == 0:
                                    p.op('dve', lambda e, o=t.ap[:, sh:TW], i=PSU[bk].ap[:, 0:TW - sh], s_=VEC.ap[:, wb + tap:wb + tap + 1]:
                                         e.scalar_tensor_tensor(out=o, in0=i, scalar=s_, in1=o, op0=ALU.mult, op1=ALU.add),
                                         r=[PSU[bk], t, VEC], w=[t])
                                else:
                                    p.op('dve', lambda e, o=t.ap, i=psbig[:, bk * 512 - sh:bk * 512 - sh + TW], s_=VEC.ap[:, wb + tap:wb + tap + 1]:
                                         e.scalar_tensor_tensor(out=o, in0=i, scalar=s_, in1=o, op0=ALU.mult, op1=ALU.add),
                                         r=[PSU[bk], PSU[bk - 1], t, VEC], w=[t])

                        def tail(tg=tg, tv=tv, dst=ACTB[gi][n]):
                            p.op('act', lambda e, o=tg.ap: e.activation(out=o, in_=o, func=AF.Silu), r=[tg], w=[tg])
                            p.op('dve', lambda e, o=dst.ap, a=tg.ap, b=tv.ap:
                                 e.tensor_tensor(out=o, in0=a, in1=b, op=ALU.mult), r=[tg, tv], w=[dst])
                        pend[0] = tail
                pend[0]()
                pend[0] = None
                dsl = []
                for f in grp:
                    s = slots.next()
                    p.op('sp', lambda e, d=s.ap, s_=wdn[f * 128:(f + 1) * 128, :]: e.dma_start(out=d, in_=s_),
                         r=wunits(('w_down', l)), w=[s], dma=True)
                    dsl.append(s)
                last_grp = (grp is groups[-1])
                ring = ps6 if last_grp else psall
                sqk = 0
                for n in range(NT):
                    stat = PSU[6 + n % 2]
                    pend_q = []
                    for dc in range(KC):
                        ps = ring.next()
                        for gi, s in enumerate(dsl):
                            p.op('pe', lambda e, o=ps.ap, a=s.ap[:, dc * 128:(dc + 1) * 128], b=ACTB[gi][n].ap,
                                 st=(gi == 0), sp=(gi == len(dsl) - 1):
                                 e.matmul(o, lhsT=a, rhs=b, start=st, stop=sp), r=[s, ACTB[gi][n]], w=[ps])
                        while len(pend_q) > 1:
                            pend_q.pop(0)()
                        p.op('dve', lambda e, o=X[dc][n].ap, i=ps.ap:
                             e.tensor_tensor(out=o, in0=o, in1=i, op=ALU.add), r=[ps, X[dc][n]], w=[X[dc][n]])
                        if last_grp:
                            sq = SQ4[sqk % 4]
                            sqk += 1
                            p.op('act', lambda e, o=sq.ap, i=X[dc][n].ap: e.activation(out=o, in_=i, func=AF.Square),
                                 r=[X[dc][n]], w=[sq])

                            def pend_stat(sq=sq, dc=dc, stat=stat):
                                p.op('pe', lambda e, o=stat.ap, b=sq.ap, st=(dc == 0), sp=(dc == KC - 1):
                                     e.matmul(o, lhsT=ONES.ap, rhs=b, start=st, stop=sp), r=[sq, ONES], w=[stat])
                            pend_q.append(pend_stat)
                    while pend_q:
                        pend_q.pop(0)()
                    if last_grp:
                        rms_lnexp(n, stat)
                if last_grp:
                    fused_stats[0] = True

        def mixer_a(l, hook):
            rmsnorm_x(B1, ('g_mix', l))
            wd = ('w_in_a', l)
            bgl = vec_index[('b_glu', l)]
            sgc = [0]

            def cb_glu(gi, n, pss):
                sg = SG[sgc[0] % 2]
                sgc[0] += 1
                p.op('act', lambda e, o=sg.ap, i=pss[1].ap, b=VEC.ap[:, bgl + 6 + gi:bgl + 7 + gi]:
                     e.activation(out=o, in_=i, func=AF.Sigmoid, bias=b), r=[pss[1], VEC], w=[sg])
                p.op('dve', lambda e, o=B2[gi][n].ap, i=pss[0].ap, s_=VEC.ap[:, bgl + gi:bgl + gi + 1], b=sg.ap:
                     e.scalar_tensor_tensor(out=o, in0=i, scalar=s_, in1=b, op0=ALU.add, op1=ALU.mult),
                     r=[pss[0], sg, VEC], w=[B2[gi][n]])
            linear([[(wd, j * 128, 128), (wd, 768 + j * 128, 128)] for j in range(6)],
                   lambda kc, n: (B1[kc][n].ap, B1[kc][n]), range(NT), cb_glu)
            hook([B2[5][NT - 1]])
            mem_kv(l)

            def cb_q(gi, n, pss):
                p.op('act', lambda e, o=B2[6 + gi][n].ap, i=pss[0].ap:
                     e.activation(out=o, in_=i, func=AF.Copy, scale=0.125), r=[pss[0]], w=[B2[6 + gi][n]])
            linear([[(wd, 1536 + j * 128, 128)] for j in range(2)],
                   lambda kc, n: (B1[kc][n].ap, B1[kc][n]), range(NT), cb_q)

            PT = Ring(PTA)
            wb = vec_index[('w_dw_a', l)]
            dgc = [0]
            allp = []
            for n in range(NT):
                allp += mem_pairs([B2[6][n], B2[7][n]], [B1[6][n], B1[7][n]])
            attn_tile(0, allp, PT, RDENA, None)
            def build_diag(j_, slot):
                dgu = DIAG[slot % 2]
                p.op('dve', lambda e, o=dgu.ap, i0=IDENT.ap.unsqueeze(1).to_broadcast([128, 31, 128]),
                     i1=VEC.ap[:, wb + j_ * 31:wb + j_ * 31 + 31].unsqueeze(2).to_broadcast([128, 31, 128]):
                     e.tensor_tensor(out=o, in0=i0, in1=i1, op=ALU.mult),
                     r=[IDENT, VEC], w=[dgu])
            build_diag(0, 0)
            for n in range(NT):
                for j in range(6):
                    dg = DIAG[dgc[0] % 2]
                    dgc[0] += 1
                    nxt_emitted = [False]

                    def emit_next(j=j):
                        if not nxt_emitted[0] and not (n == NT - 1 and j == 5):
                            build_diag((j + 1) % 6, dgc[0])
                        nxt_emitted[0] = True
                    ps = psg.next()
                    order = [30] + list(range(30))
                    for ki, k in enumerate(order):
                        off = n * TW - 30 + k
                        o0 = 0
                        if off < 0:
                            o0 = -off
                            off = 0
                        ln_ = TW - o0
                        ru = [B2[j][n]] + ([B2[j][n - 1]] if n > 0 else [])
                        p.op('pe', lambda e, o=ps.ap[:, o0:TW], a=dg.ap[:, k, :], b=B2c[j][:, off:off + ln_],
                             st=(ki == 0), sp=(ki == 30):
                             e.matmul(o, lhsT=a, rhs=b, start=st, stop=sp), r=[dg] + ru, w=[ps])
                    emit_next()
                    bd = vcol(('b_dw_a', l), j)
                    p.op('act', lambda e, o=Y[j].ap, i=ps.ap, b=bd: e.activation(out=o, in_=i, func=AF.Identity, bias=b),
                         r=[ps, VEC], w=[Y[j]])
                    p.op('act', lambda e, o=YB[j].ap, i=ps.ap, b=bd: e.activation(out=o, in_=i, func=AF.Identity, bias=b),
                         r=[ps, VEC], w=[YB[j]])
                    p.op('act', lambda e, o=YSQ[j].ap, i=ps.ap, b=bd: e.activation(out=o, in_=i, func=AF.Square, bias=b),
                         r=[ps, VEC], w=[YSQ[j]])
                s1 = psg.next()
                s2 = psg.next()
                for j in range(6):
                    p.op('pe', lambda e, b=YB[j].ap, st=(j == 0), sp=(j == 5):
                         e.matmul(s1.ap, lhsT=ONES.ap, rhs=b, start=st, stop=sp), r=[YB[j], ONES], w=[s1])
                for j in range(6):
                    p.op('pe', lambda e, b=YSQ[j].ap, st=(j == 0), sp=(j == 5):
                         e.matmul(s2.ap, lhsT=ONES.ap, rhs=b, start=st, stop=sp), r=[YSQ[j], ONES], w=[s2])
                p.op('dve', lambda e: e.tensor_scalar(out=MU.ap, in0=s1.ap, scalar1=1.0 / 768, scalar2=None, op0=ALU.mult),
                     r=[s1], w=[MU])
                p.op('dve', lambda e: e.tensor_tensor(out=T2.ap, in0=MU.ap, in1=MU.ap, op=ALU.mult), r=[MU], w=[T2])
                p.op('dve', lambda e: e.scalar_tensor_tensor(out=T2.ap, in0=s2.ap, scalar=1.0 / 768, in1=T2.ap,
                                                            op0=ALU.mult, op1=ALU.subtract), r=[s2, T2], w=[T2])
                p.op('act', lambda e: e.activation(out=T2.ap, in_=T2.ap, func=AF.Ln, bias=vcol('eps_ln')),
                     r=[T2, VEC], w=[T2])
                p.op('act', lambda e: e.activation(out=RSTL.ap, in_=T2.ap, func=AF.Exp, scale=-0.5), r=[T2], w=[RSTL])
                for j in range(6):
                    p.op('dve', lambda e, o=Y[j].ap: e.tensor_tensor(out=o, in0=o, in1=MU.ap, op=ALU.subtract),
                         r=[Y[j], MU], w=[Y[j]])
                    p.op('dve', lambda e, o=Y[j].ap: e.tensor_tensor(out=o, in0=o, in1=RSTL.ap, op=ALU.mult),
                         r=[Y[j], RSTL], w=[Y[j]])
                    p.op('act', lambda e, o=B1[j][n].ap, i=Y[j].ap, s_=vcol(('ln_g', l), j), b=vcol(('ln_b', l), j):
                         e.activation(out=o, in_=i, func=AF.Silu, bias=b, scale=s_),
                         r=[Y[j], VEC], w=[B1[j][n]])
            wout_residual(l, B1)

        def kv_pre():
            p.op('pool', lambda e: e.dma_start(out=SEL.ap, in_=cst_d[:, 268:268 + 1536].rearrange("p (h m) -> p h m", h=12)),
                 w=[SEL], dma=True)
            p.op('pool', lambda e: e.dma_start(out=MASK.ap, in_=cst_d[:, 0:128]), w=[MASK], dma=True)
            rmsnorm_x(B1, 'g_kv')

            def cb_f(gi, n, pss):
                p.op('act', lambda e, o=LF.ap[0:12, n * TW:(n + 1) * TW], i=pss[0].ap[0:12, :]:
                     e.activation(out=o, in_=i, func=AF.Sigmoid, bias=vcol('b_f', rows=12)),
                     r=[pss[0], VEC], w=[LF])
                p.op('act', lambda e, o=LF.ap[0:12, n * TW:(n + 1) * TW]:
                     e.activation(out=o, in_=o, func=AF.Ln), r=[LF], w=[LF])
            linear([[(('w_kvf', None), 1536, 12)]], lambda kc, n: (B1[kc][n].ap, B1[kc][n]), range(NT), cb_f)
            p.op('dve', lambda e: e.memset(ONESF.ap[0:12, :], 1.0), w=[ONESF])
            p.op('dve', lambda e: e.tensor_tensor_scan(out=CC.ap[0:12, :], data0=ONESF.ap[0:12, :], data1=LF.ap[0:12, :],
                                                       initial=0.0, op0=ALU.mult, op1=ALU.add),
                 r=[ONESF, LF], w=[CC])
            ps = psg.next()
            for kb in range(16):
                p.op('pe', lambda e, o=ps.ap[:, kb * 12:(kb + 1) * 12], i=CC.ap[0:12, kb * 128:(kb + 1) * 128]:
                     e.transpose(o, i, ID12.ap[0:12, 0:12]), r=[CC, ID12], w=[ps])
            p.op('act', lambda e: e.activation(out=NEGCK.ap, in_=ps.ap[:, 0:192], func=AF.Copy, scale=-1.0),
                 r=[ps], w=[NEGCK])
            R1 = ONESF
            p.op('dve', lambda e: e.tensor_copy(out=HI.ap[0:12, :], in_=CC.ap[0:12, :]), r=[CC], w=[HI])
            p.op('dve', lambda e: e.tensor_tensor(out=R1.ap[0:12, :], in0=CC.ap[0:12, :], in1=HI.ap[0:12, :], op=ALU.subtract),
                 r=[CC, HI], w=[R1])
            p.op('dve', lambda e: e.tensor_copy(out=MID.ap[0:12, :], in_=R1.ap[0:12, :]), r=[R1], w=[MID])
            p.op('dve', lambda e: e.tensor_tensor(out=R1.ap[0:12, :], in0=R1.ap[0:12, :], in1=MID.ap[0:12, :], op=ALU.subtract),
                 r=[R1, MID], w=[R1])
            p.op('dve', lambda e: e.tensor_copy(out=LO.ap[0:12, :], in_=R1.ap[0:12, :]), r=[R1], w=[LO])
            p.op('dve', lambda e: e.memset(CQ.ap, 0.0), w=[CQ])
            for pc, pu in enumerate((HI, MID, LO)):
                p.op('pool', lambda e, o=CQ.ap[12 * pc:12 * pc + 12, :], i=pu.ap[0:12, :]:
                     e.dma_start(out=o, in_=i), r=[pu], w=[CQ], dma=True)
            def cb_k(gi, n, pss):
                p.op('act', lambda e, o=KT[gi][n].ap, i=pss[0].ap: e.activation(out=o, in_=i, func=AF.Copy),
                     r=[pss[0]], w=[KT[gi][n]])
            linear([[(('w_kvf', None), j * 128, 128)] for j in range(6)],
                   lambda kc, n: (B1[kc][n].ap, B1[kc][n]), range(NT), cb_k)
            for cj in range(6):
                su, sap = wjob_k(('w_kvf', None), 768 + cj * 128, 128)
                for n in range(NT):
                    ps = psg.next()
                    for tb in range(4):
                        for kc in range(KC):
                            p.op('pe', lambda e, o=ps.ap[:, tb * 128:(tb + 1) * 128],
                                 a=B1[kc][n].ap[:, tb * 128:(tb + 1) * 128], b=sap[:, kc, :],
                                 st=(kc == 0), sp=(kc == KC - 1):
                                 e.matmul(o, lhsT=a, rhs=b, start=st, stop=sp), r=[su, B1[kc][n]], w=[ps])
                    p.op('act', lambda e, o=Vall[:, 4 * n:4 * n + 4, cj * 128:(cj + 1) * 128],
                         i=ps.ap.rearrange("p (b c) -> p b c", b=4):
                         e.activation(out=o, in_=i, func=AF.Copy), r=[ps], w=[V[4 * n + t] for t in range(4)])


        def mixer_b(l, hook):
            rmsnorm_x(B1, ('g_mix', l))
            mem_kv(l)
            wd = ('w_in_b', l - 2)
            PT = Ring(PTB)
            DG = Ring(DGB)
            for j in range(6):
                p.op('dve', lambda e, j=j: e.memset(QTA[j].ap, 0.0), w=[QTA[j]])
                p.op('dve', lambda e, j=j: e.memset(QTB[j].ap, 0.0), w=[QTB[j]])
            for n in range(NT):
                def cb_q(gi, n_, pss):
                    if gi < 6:
                        p.op('act', lambda e, o=QTA[gi].ap[0:64, :], i=pss[0].ap[0:64, :]:
                             e.activation(out=o, in_=i, func=AF.Copy, scale=0.125), r=[pss[0]], w=[QTA[gi]])
                        p.op('act', lambda e, o=QTB[gi].ap[64:128, :], i=pss[0].ap[64:128, :]:
                             e.activation(out=o, in_=i, func=AF.Copy, scale=0.125), r=[pss[0]], w=[QTB[gi]])
                    else:
                        p.op('act', lambda e, o=QTM[gi - 6].ap, i=pss[0].ap:
                             e.activation(out=o, in_=i, func=AF.Copy, scale=0.125), r=[pss[0]], w=[QTM[gi - 6]])
                linear([[(wd, c * 128, 128)] for c in range(KC)],
                       lambda kc, n_: (B1[kc][n_].ap, B1[kc][n_]), [n], cb_q, ring=pss3)
                if n == 0:
                    hook([QTM[1]])

                def blocks(n=n):
                    out = []
                    for kb in range(4 * n + 4):
                        i = kb - 4 * n
                        out.append((kb, max(0, i) * 128, i >= 0))
                    return out
                pairs = [dict(q=(QTA[j], QTB[j]),
                              kt=(lambda hh, kb, j=j: (KTc[j][:, kb * 128:(kb + 1) * 128], KT[j][kb // 4])),
                              v=(lambda hh, kb, j=j: (Vall[:, kb, j * 128:(j + 1) * 128], V[kb])),
                              blocks=blocks(), h0=2 * j, out=B1[j][n]) for j in range(6)]
                pairs += mem_pairs([QTM[0], QTM[1]], [B1[6][n], B1[7][n]])
                attn_tile(n, pairs, PT, RDENB, DG)
            wout_residual(l, B1)

        p.op('sp', lambda e: e.dma_start(out=VEC.ap, in_=vec_d), w=[VEC], dma=True)
        p.op('sp', lambda e: e.dma_start(out=MEMX.ap, in_=memT_d.rearrange("(c p) m -> p c m", p=128)),
             w=[MEMX], dma=True)
        Xall = view(OX, 65536, F32, "p (c t) -> p c t", c=KC)
        for n in range(NT):
            p.op('sp', lambda e, o=Xall[:, :, n * TW:(n + 1) * TW],
                 i=xT_d.rearrange("(c p) t -> p c t", p=128)[:, :, n * TW:(n + 1) * TW]: e.dma_start(out=o, in_=i),
                 w=[X[c][n] for c in range(KC)], dma=True)
        p.op('sp', lambda e: e.dma_start(out=IDENT.ap, in_=cst_d[:, 128:256]), w=[IDENT], dma=True)
        p.op('sp', lambda e: e.dma_start(out=ID12.ap[:, 0:12], in_=cst_d[:, 256:268]), w=[ID12], dma=True)
        p.op('dve', lambda e: e.memset(ONES.ap, 1.0), w=[ONES])
        for i in range(2):
            p.op('dve', lambda e, i=i: e.memset(ONESH[i].ap[:, 64 * i:64 * i + 64], 1.0), w=[ONESH[i]])
            p.op('dve', lambda e, i=i: e.memset(ONESH[i].ap[:, 64 * (1 - i):64 * (1 - i) + 64], 0.0), w=[ONESH[i]])
        rms_tile([(MEMX.ap[:, c, :], MEMX) for c in range(KC)],
                 [(MEMN.ap[:, c, :], MEMN) for c in range(KC)], 'g_mem', MEM)

        precast = {
            'A0': [('w_out', 0), ('w_up', 0), ('w_down', 0), ('w_mem_kv', 1), ('w_in_a', 1), ('w_out', 1)],
            'A1': [('w_up', 1), ('w_down', 1), ('w_kvf', None), ('w_mem_kv', 2), ('w_in_b', 0), ('w_out', 2)],
            'B2': [('w_up', 2), ('w_down', 2), ('w_mem_kv', 3), ('w_in_b', 1), ('w_out', 3)],
            'B3': [('w_up', 3), ('w_down', 3)],
        }

        def mk_hook(name):
            def hook(after):
                for i, wn in enumerate(precast.get(name, [])):
                    cast_w(*wn, after=(after if i == 0 else ()))
            return hook
        cast_w('w_in_a', 0, step=1024)
        cast_w('w_mem_kv', 0)
        stages = []
        for l in range(2):
            stages.append((f"A{l}", lambda l=l: mixer_a(l, mk_hook(f"A{l}"))))
            stages.append((f"F{l}", lambda l=l: ffn(l)))
        stages.append(("KV", kv_pre))
        for l in range(2, 4):
            stages.append((f"B{l}", lambda l=l: mixer_b(l, mk_hook(f"B{l}"))))
            stages.append((f"F{l}", lambda l=l: ffn(l)))
        final_norm = True
        for name, fn in stages:
            if stop is not None and stop == 'init':
                final_norm = False
                break
            fn()
            if stop is not None and name == stop:
                final_norm = False
                break
        if final_norm:
            rmsnorm_x(X, 'g_final', c_outer=True)
        OUT = [p.unit(f"OUT{c}") for c in range(KC)]
        for c in range(KC):
            p.op('sp', lambda e, i=Xc[c], o=outT_d[c * 128:(c + 1) * 128, :]: e.dma_start(out=o, in_=i),
                 r=X[c], w=[OUT[c]], dma=True)
        p.op('sp', None, r=OUT)
        p.finalize()

        import contextlib
        with contextlib.ExitStack() as es:
            sems = {e: es.enter_context(nc.semaphore(f"s_{e}")) for e in ENGS}
            dsems = {e: [es.enter_context(nc.semaphore(f"d_{e}{i}")) for i in range(Prog.R)]
                     for e in ('sp', 'pool')}
            block = es.enter_context(nc.Block())

            @block.tensor
            def _(h):
                p.emit('pe', h, sems, dsems)

            @block.scalar
            def _(h):
                p.emit('act', h, sems, dsems)

            @block.vector
            def _(h):
                p.emit('dve', h, sems, dsems)

            @block.gpsimd
            def _(h):
                p.emit('pool', h, sems, dsems)

            @block.sync
            def _(h):
                p.emit('sp', h, sems, dsems)
        for t in reversed(pst):
            t.__exit__(None, None, None)
    return nc, p


_CACHE = {}


def kernel(x, mem, g_mix, w_in_a, b_glu, w_dw_a, b_dw_a, ln_g, ln_b, g_kv, w_kvf, b_f,
           w_in_b, g_mem, w_mem_kv, w_out, g_ffn, w_up, w_dw_f, b_dw_f, w_down, g_final,
           _stop=None, _ncores=8):
    inp = dict(g_mix=g_mix, b_glu=b_glu, w_dw_a=w_dw_a, b_dw_a=b_dw_a, ln_g=ln_g, ln_b=ln_b,
               g_kv=g_kv, b_f=b_f, g_mem=g_mem, g_ffn=g_ffn, w_dw_f=w_dw_f, b_dw_f=b_dw_f,
               g_final=g_final)
    inp = {k: np.asarray(v, np.float32) for k, v in inp.items()}
    vec, vindex = _vec_layout(inp)
    cst = _consts()
    key = (_stop,)
    if key not in _CACHE:
        _CACHE[key] = build(vindex, vec.shape[1], stop=_stop)[0]
    nc = _CACHE[key]
    x = np.asarray(x, np.float32)
    mem = np.asarray(mem, np.float32)
    shared = dict(
        vec=vec, cst=cst,
        w_in_a=np.ascontiguousarray(np.asarray(w_in_a, np.float32)),
        w_kvf=np.ascontiguousarray(np.asarray(w_kvf, np.float32)),
        w_in_b=np.ascontiguousarray(np.asarray(w_in_b, np.float32)),
        w_mem_kv=np.ascontiguousarray(np.asarray(w_mem_kv, np.float32)),
        w_out=np.ascontiguousarray(np.asarray(w_out, np.float32)),
        w_up=np.ascontiguousarray(np.asarray(w_up, np.float32)),
        w_down=np.ascontiguousarray(np.asarray(w_down, np.float32)),
    )
    in_maps = []
    for b in range(_ncores):
        m = dict(shared)
        m["xT"] = np.ascontiguousarray(x[b].T)
        m["memT"] = np.ascontiguousarray(mem[b].T)
        in_maps.append(m)
    res = run_bass_kernel_spmd(nc, in_maps, core_ids=list(range(_ncores)))
    out = np.stack([np.ascontiguousarray(np.asarray(r["outT"], np.float32).T) for r in res.results], axis=0)
    return out
```

```python
import os
import numpy as np
import concourse.bass as bass
import concourse.mybir as mybir
from concourse.bass_utils import run_bass_kernel_spmd

F32 = mybir.dt.float32
BF16 = mybir.dt.bfloat16
U8 = mybir.dt.uint8
AF = mybir.ActivationFunctionType
ALU = mybir.AluOpType

D = 1024
S = 2048
NT = 4
TW = 512
KC = 8
DFF = 2816
NFF = 22
NH = 12
MEM = 256
ARENA = 212832
SAME_ENG_SYNC = True
ENGS = ['pe', 'act', 'dve', 'pool', 'sp']


class Unit:
    __slots__ = ('name', 'lo', 'hi', 'lastw', 'readers', 'alias', 'ap', 'gen')

    def __init__(self, name, lo=None, hi=None, ap=None):
        self.name = name
        self.lo = lo
        self.hi = hi
        self.ap = ap
        self.lastw = None
        self.readers = []
        self.alias = []
        self.gen = 0


class Op:
    __slots__ = ('eng', 'fn', 'idx', 'waits', 'dwaits', 'is_dma', 'signal', 'sigval',
                 'dslot', 'dval', 'desc')

    def __init__(self, eng, fn, is_dma):
        self.eng = eng
        self.fn = fn
        self.is_dma = is_dma
        self.waits = []
        self.dwaits = []
        self.signal = False
        self.sigval = 0
        self.dslot = 0
        self.dval = 0


class Prog:
    R = 8

    def __init__(self):
        self.streams = {e: [] for e in ENGS}
        self.sb_units = []
        self.known = {e: {f: -1 for f in ENGS} for e in ENGS}
        self.kdma = {e: {} for e in ENGS}
        self.dmaops = {e: [] for e in ENGS}

    def unit(self, name, lo=None, hi=None, ap=None):
        u = Unit(name, lo, hi, ap)
        if lo is not None:
            for v in self.sb_units:
                if v.lo < hi and lo < v.hi:
                    v.alias.append(u)
                    u.alias.append(v)
            self.sb_units.append(u)
        return u

    def op(self, eng, fn, r=(), w=(), dma=False):
        o = Op(eng, fn, dma)
        o.idx = len(self.streams[eng])
        o.desc = ([u.name for u in r], [u.name for u in w])
        need = {}
        dneed = []
        seen = set()

        def add(d, kind):
            if d is None or id(d) in seen:
                return
            if d.is_dma:
                seen.add(id(d))
                dneed.append(d)
                return
            if d.eng == eng:
                if eng == 'pe' or kind == 'war' or not SAME_ENG_SYNC:
                    return
            if need.get(d.eng, -1) < d.idx:
                need[d.eng] = d.idx

        for u in r:
            add(u.lastw, 'raw')
            for v in u.alias:
                add(v.lastw, 'raw')
        for u in w:
            for v in [u] + u.alias:
                add(v.lastw, 'waw')
                for rd in v.readers:
                    add(rd, 'war')
        if dma:
            k = len(self.dmaops[eng])
            o.dslot = k % self.R
            o.dval = 16 * (k // self.R + 1)
            if k >= self.R:
                dneed.append(self.dmaops[eng][k - self.R])
            self.dmaops[eng].append(o)
        for f, idx in need.items():
            if idx > self.known[eng][f]:
                self.known[eng][f] = idx
                dep = self.streams[f][idx]
                dep.signal = True
                o.waits.append(dep)
        for d in dneed:
            key = (d.eng, d.dslot)
            if self.kdma[eng].get(key, 0) < d.dval:
                self.kdma[eng][key] = d.dval
                o.dwaits.append(d)
        for u in r:
            u.readers.append(o)
        for u in w:
            u.lastw = o
            u.readers = []
        self.streams[eng].append(o)
        return o

    def finalize(self):
        for e in ENGS:
            c = 0
            for o in self.streams[e]:
                if o.signal and not o.is_dma:
                    c += 1
                    o.sigval = c

    def emit(self, e, h, sems, dsems):
        for o in self.streams[e]:
            for d in o.waits:
                h.wait_ge(sems[d.eng], d.sigval)
            for d in o.dwaits:
                h.wait_ge(dsems[d.eng][d.dslot], d.dval)
            if o.fn is None:
                continue
            ins = o.fn(h)
            if o.is_dma:
                ins.then_inc(dsems[e][o.dslot], 16)
            elif o.signal:
                ins.then_inc(sems[e], 1)


class Ring:
    def __init__(self, units):
        self.units = units
        self.i = 0

    def next(self):
        u = self.units[self.i % len(self.units)]
        self.i += 1
        u.gen += 1
        return u


def _vec_layout(inp):
    cols = []
    index = {}

    def add(name, arr):
        arr = np.asarray(arr, np.float32)
        index[name] = sum(c.shape[1] for c in cols)
        cols.append(arr)

    def chunked(v, n):
        return np.ascontiguousarray(np.asarray(v, np.float32).reshape(n, 128).T)

    for l in range(4):
        add(('g_mix', l), chunked(inp['g_mix'][l], 8))
        add(('g_ffn', l), chunked(inp['g_ffn'][l], 8))
        wf = np.asarray(inp['w_dw_f'][l], np.float32)
        add(('w_dw_f', l), np.ascontiguousarray(
            wf.T.reshape(44, 128, 3).transpose(1, 0, 2).reshape(128, 132)))
        add(('b_dw_f', l), chunked(inp['b_dw_f'][l], 44))
    for l in range(2):
        add(('b_glu', l), chunked(inp['b_glu'][l], 12))
        wa = np.asarray(inp['w_dw_a'][l], np.float32)
        add(('w_dw_a', l), np.ascontiguousarray(
            wa.T.reshape(6, 128, 31).transpose(1, 0, 2).reshape(128, 186)))
        add(('b_dw_a', l), chunked(inp['b_dw_a'][l], 6))
        add(('ln_g', l), chunked(inp['ln_g'][l], 6))
        add(('ln_b', l), chunked(inp['ln_b'][l], 6))
    add('g_kv', chunked(inp['g_kv'], 8))
    add('g_mem', chunked(inp['g_mem'], 8))
    add('g_final', chunked(inp['g_final'], 8))
    bf = np.zeros((128, 1), np.float32)
    bf[:12, 0] = np.asarray(inp['b_f'], np.float32)
    add('b_f', bf)
    add('eps_rms', np.full((128, 1), 1e-6, np.float32))
    add('eps_ln', np.full((128, 1), 1e-5, np.float32))
    vec = np.concatenate(cols, axis=1)
    return np.ascontiguousarray(vec), index


def _consts():
    k = np.arange(128)[:, None]
    q = np.arange(128)[None, :]
    mask = np.where(k <= q, 0.0, -30000.0).astype(np.float32)
    ident = np.eye(128, dtype=np.float32)
    id12 = np.zeros((128, 12), np.float32)
    id12[:12, :12] = np.eye(12, dtype=np.float32)
    sel = np.zeros((128, 12, 128), np.float32)
    for pc in range(3):
        for h in range(12):
            sel[12 * pc + h, h, :] = 1.0
    cst = np.concatenate([mask, ident, id12, sel.reshape(128, 12 * 128)], axis=1)
    return np.ascontiguousarray(cst)


_VEC_INDEX_CACHE = {}


def build(vec_index, nv, stop=None):
    nc = bass.Bass("TRN2", target_bir_lowering=False)
    dr = {}

    def din(name, shape):
        dr[name] = nc.dram_tensor(name, list(shape), F32, kind="ExternalInput").ap()
        return dr[name]

    xT_d = din("xT", [D, S])
    memT_d = din("memT", [D, MEM])
    vec_d = din("vec", [128, nv])
    NCST = 128 + 128 + 12 + 12 * 128
    cst_d = din("cst", [128, NCST])
    w_in_a_d = din("w_in_a", [2, D, 1792])
    w_kvf_d = din("w_kvf", [D, 1548])
    w_in_b_d = din("w_in_b", [2, D, D])
    w_mem_kv_d = din("w_mem_kv", [4, D, 512])
    w_out_d = din("w_out", [4, D, D])
    w_up_d = din("w_up", [4, D, 2 * DFF])
    w_down_d = din("w_down", [4, DFF, D])
    outT_d = nc.dram_tensor("outT", [D, S], F32, kind="ExternalOutput").ap()
    wsrc = dict(w_in_a=w_in_a_d, w_kvf=w_kvf_d, w_in_b=w_in_b_d, w_mem_kv=w_mem_kv_d, w_out=w_out_d,
                w_up=w_up_d, w_down=w_down_d)
    wbf = {k: nc.dram_tensor(k + "_bf", list(v.shape), BF16).ap() for k, v in wsrc.items()}

    p = Prog()
    WU = {}

    def wsel(d, name, l):
        return d[name] if l is None else d[name][l]

    cast_chain = [None]

    def cast_w(name, l=None, after=(), step=2048):
        src = wsel(wsrc, name, l)
        dst = wsel(wbf, name, l)
        nrow, ncol = src.shape[-2], src.shape[-1]
        blocks = []
        first = True
        for c0 in range(0, ncol, step):
            c1 = min(ncol, c0 + step)
            u = p.unit(f"WU_{name}_{l}_{c0}")
            blocks.append((c0, c1, u))
            rows = 128 if (c1 - c0) > 1024 else 256
            for r0 in range(0, nrow, rows):
                r1 = min(nrow, r0 + rows)
                rd = list(after) if first else []
                if cast_chain[0] is not None and (first or r0 == 0):
                    rd.append(cast_chain[0])
                first = False
                p.op('pool', lambda e, o=dst[r0:r1, c0:c1], i=src[r0:r1, c0:c1]: e.dma_start(out=o, in_=i, max_dma_last_dim=2048),
                     r=rd, w=[u], dma=True)
            cast_chain[0] = u
        WU[(name, l)] = blocks

    def wunits(wd, c0=None, c1=None):
        return [u for (a, b, u) in WU[wd] if c0 is None or (a < c1 and c0 < b)]

    with nc.sbuf_tensor("arena", [128, ARENA], U8) as arena_t:
        arena = arena_t[:]

        def view(off, nbytes, dt, pat=None, **kw):
            a = arena[:, off:off + nbytes].bitcast(dt)
            if pat is not None:
                a = a.rearrange(pat, **kw)
            return a

        OX = 0
        OB1 = OX + 65536
        OB2 = OB1 + 32768
        OPH = OB2 + 32768
        OMISC = OPH + 57344
        assert OMISC == 188416

        def mk(name, off, nbytes, dt, pat=None, **kw):
            return p.unit(name, off, off + nbytes, view(off, nbytes, dt, pat, **kw))

        X = [[mk(f"X{c}_{n}", OX + c * 8192 + n * 2048, 2048, F32) for n in range(NT)]
             for c in range(KC)]
        Xc = [view(OX + c * 8192, 8192, F32) for c in range(KC)]
        B1 = [[mk(f"B1_{c}_{n}", OB1 + c * 4096 + n * 1024, 1024, BF16) for n in range(NT)]
              for c in range(KC)]
        B2 = [[mk(f"B2_{c}_{n}", OB2 + c * 4096 + n * 1024, 1024, BF16) for n in range(NT)]
              for c in range(KC)]
        B2c = [view(OB2 + c * 4096, 4096, BF16) for c in range(KC)]
        OU = OB2 + 16384
        SQ = [mk(f"SQ{i}", OU + i * 1024, 1024, BF16) for i in range(2)]
        SD = mk("SD", OU + 2048, 2048, F32)
        RSTD = mk("RSTD", OU + 4096, 2048, F32)
        SQ4 = [mk(f"SQ4_{i}", OU + 4096 + i * 1024, 1024, BF16) for i in range(4)]
        RS4 = [mk(f"RS4_{i}", OU + 8192 + i * 2048, 2048, F32) for i in range(4)]
        TG = [mk(f"TG{i}", OU + i * 2048, 2048, F32) for i in range(3)]
        TV = [mk(f"TV{i}", OU + 6144 + i * 2048, 2048, F32) for i in range(3)]
        HG = [mk(f"HG{i}", OU + 12288 + i * 8, 8, F32) for i in range(2)]
        HV = [mk(f"HV{i}", OU + 12304 + i * 8, 8, F32) for i in range(2)]
        ACTB = [[B2[g][n] for n in range(NT)] for g in range(4)]
        QTA = [mk(f"QTA{j}", OB2 + j * 2048, 1024, BF16) for j in range(6)]
        QTB = [mk(f"QTB{j}", OB2 + j * 2048 + 1024, 1024, BF16) for j in range(6)]
        QTM = [mk(f"QTM{j}", OB2 + 12288 + j * 1024, 1024, BF16) for j in range(2)]
        PTB = [mk(f"PTB{i}", OU + 6144 + i * 1024, 1024, BF16) for i in range(4)]
        RDENB = mk("RDENB", OU + 10240, 2048, F32)
        DGB = [mk(f"DGB{i}", OU + 12288 + i * 512, 512, F32) for i in range(2)]
        LF = mk("LF", OB2, 8192, F32)
        HI = mk("HIp", OB2, 4096, BF16)
        MID = mk("MIDp", OB2 + 4096, 4096, BF16)
        ONESF = mk("ONESF", OB2 + 8192, 8192, F32)
        CC = mk("CC", OB2 + 16384, 8192, F32)
        LO = mk("LOp", OB2 + 24576, 4096, BF16)
        DIAG = [mk(f"DIAG{i}", OPH + i * 7936, 7936, BF16, "p (k m) -> p k m", k=31)
                for i in range(2)]
        oa = OPH + 15872
        Y = [mk(f"Y{j}", oa + j * 2048, 2048, F32) for j in range(6)]
        YB = [mk(f"YB{j}", oa + 12288 + j * 1024, 1024, BF16) for j in range(6)]
        YSQ = [mk(f"YSQ{j}", oa + 18432 + j * 1024, 1024, BF16) for j in range(6)]
        oa += 24576
        MU = mk("MU", oa, 2048, F32)
        T2 = mk("T2", oa + 2048, 2048, F32)
        RSTL = mk("RSTL", oa + 4096, 2048, F32)
        SG = [mk(f"SG{i}", oa + 6144 + i * 2048, 2048, F32) for i in range(2)]
        PTA = [mk(f"PTA{i}", oa + 10240 + i * 1024, 1024, BF16) for i in range(4)]
        RDENA = mk("RDENA", oa + 14336, 2048, F32)
        IDENT = mk("IDENT", oa + 16384, 512, F32)
        assert oa + 16896 <= OMISC
        MEMX = mk("MEMX", OPH, 8192, F32, "p (c m) -> p c m", c=8)
        KT = [[mk(f"KT{j}_{n}", OPH + j * 4096 + n * 1024, 1024, BF16) for n in range(NT)]
              for j in range(6)]
        KTc = [view(OPH + j * 4096, 4096, BF16) for j in range(6)]
        OV = OPH + 24576
        V = [mk(f"V{kb}", OV + kb * 1536, 1536, BF16) for kb in range(16)]
        Vall = view(OV, 24576, BF16, "p (kb c) -> p kb c", kb=16)
        CQ = mk("CQ", OV + 24576, 4096, BF16)
        SEL = mk("SEL", OV + 28672, 3072, BF16, "p (h m) -> p h m", h=12)
        NEGCK = mk("NEGCK", OV + 31744, 768, F32)
        MASK = mk("MASK", OV + 32512, 256, BF16)
        assert OV + 32768 <= OMISC
        om = OMISC
        VEC = mk("VEC", om, nv * 4, F32)
        om += ((nv * 4 + 31) // 32) * 32
        MEMN = mk("MEMN", om, 4096, BF16, "p (c m) -> p c m", c=8)
        om += 4096
        MEMKT = mk("MEMKT", om, 1024, BF16, "p (c m) -> p c m", c=2)
        om += 1024
        MEMV = mk("MEMV", om, 1024, BF16, "p (b c) -> p b c", b=2)
        om += 1024
        ONES = mk("ONES", om, 256, BF16)
        om += 256
        ID12 = mk("ID12", om, 64, F32)
        om += 64
        ONESH = [mk(f"ONESH{i}", om + i * 256, 256, BF16) for i in range(2)]
        om += 512
        NSLOT = (ARENA - om) // 2048
        assert NSLOT >= 6, NSLOT
        slot_units = [mk(f"WS{i}", om + i * 2048, 2048, BF16) for i in range(NSLOT)]
        slots = Ring(slot_units)

        pst = [nc.psum_tensor("psbig", [128, 4096], F32)]
        psbig = pst[0].__enter__()[:]
        PSU = [p.unit(f"PS{i}", ap=psbig[:, i * 512:(i + 1) * 512]) for i in range(8)]
        psg = Ring(PSU[0:4])
        pss3 = Ring(PSU[0:3])
        psacc = Ring(PSU[3:8])
        psall = Ring(PSU)
        ps6 = Ring(PSU[0:6])

        def vcol(key, i=0, n=1, rows=128):
            b = vec_index[key] + i
            return VEC.ap[0:rows, b:b + n]

        def load_w(dram_ap, shape_pat=None, **kw):
            s = slots.next()
            dst = s.ap if shape_pat is None else s.ap.rearrange(shape_pat, **kw)
            return s, dst

        def wjob_k(wd, c0, m):
            s = slots.next()
            dst = s.ap.rearrange("p (k m) -> p k m", k=8)
            if m < 64:
                src = wsel(wsrc, *wd).rearrange("(k p) n -> p k n", p=128)[:, :, c0:c0 + m]
                p.op('pool', lambda e, d=dst[:, :, 0:m], s_=src: e.dma_start(out=d, in_=s_),
                     w=[s], dma=True)
                return s, dst
            src = wsel(wbf, *wd).rearrange("(k p) n -> p k n", p=128)[:, :, c0:c0 + m]
            p.op('sp', lambda e, d=dst[:, :, 0:m], s_=src: e.dma_start(out=d, in_=s_),
                 r=wunits(wd, c0, c0 + m), w=[s], dma=True)
            return s, dst

        def linear(groups, rhs_fn, tiles, cb, N=TW, ring=None):
            for gi, grp in enumerate(groups):
                sl = [wjob_k(wd, c0, m) + (m,) for (wd, c0, m) in grp]
                for n in tiles:
                    pss = []
                    for (su, sap, m) in sl:
                        ps = (ring or psg).next()
                        for kc in range(KC):
                            rap, ru = rhs_fn(kc, n)
                            p.op('pe', lambda e, o=ps.ap[0:m, 0:N], a=sap[:, kc, 0:m], b=rap,
                                 st=(kc == 0), sp=(kc == KC - 1):
                                 e.matmul(o, lhsT=a, rhs=b, start=st, stop=sp),
                                 r=[su, ru], w=[ps])
                        pss.append(ps)
                    cb(gi, n, pss)

        def rms_tile(srcs, dsts, gkey, N, eps_key='eps_rms', inv=1.0 / D):
            ps = psg.next()
            for c in range(KC):
                sap, su = srcs[c]
                sq = SQ[c % 2]
                p.op('act', lambda e, o=sq.ap[:, 0:N], i=sap: e.activation(out=o, in_=i, func=AF.Square),
                     r=[su], w=[sq])
                p.op('pe', lambda e, o=ps.ap[:, 0:N], b=sq.ap[:, 0:N], st=(c == 0), sp=(c == KC - 1):
                     e.matmul(o, lhsT=ONES.ap, rhs=b, start=st, stop=sp), r=[sq, ONES], w=[ps])
            p.op('act', lambda e, o=SD.ap[:, 0:N], i=ps.ap[:, 0:N]:
                 e.activation(out=o, in_=i, func=AF.Ln, bias=vcol(eps_key), scale=inv),
                 r=[ps, VEC], w=[SD])
            p.op('act', lambda e, o=RSTD.ap[:, 0:N], i=SD.ap[:, 0:N]:
                 e.activation(out=o, in_=i, func=AF.Exp, scale=-0.5),
                 r=[SD], w=[RSTD])
            for c in range(KC):
                sap, su = srcs[c]
                dap, du = dsts[c]
                p.op('dve', lambda e, o=dap, i=sap, g=vcol(gkey, c), rs=RSTD.ap[:, 0:N]:
                     e.scalar_tensor_tensor(out=o, in0=i, scalar=g, in1=rs, op0=ALU.mult, op1=ALU.mult),
                     r=[su, RSTD, VEC], w=[du])

        fused_stats = [False]

        def rmsnorm_x(dst, gkey, c_outer=False):
            if fused_stats[0]:
                fused_stats[0] = False
                rms_apply(dst, gkey, c_outer)
                return
            rms_stats()
            rms_apply(dst, gkey, c_outer)

        def rms_lnexp(n, ps):
            p.op('act', lambda e, o=RS4[n].ap, i=ps.ap:
                 e.activation(out=o, in_=i, func=AF.Ln, bias=vcol('eps_rms'), scale=1.0 / D),
                 r=[ps, VEC], w=[RS4[n]])
            p.op('act', lambda e, o=RS4[n].ap: e.activation(out=o, in_=o, func=AF.Exp, scale=-0.5),
                 r=[RS4[n]], w=[RS4[n]])

        def rms_apply(dst, gkey, c_outer):
            order = [(n, c) for n in range(NT) for c in range(KC)]
            if c_outer:
                order = [(n, c) for c in range(KC) for n in range(NT)]
            for (n, c) in order:
                p.op('dve', lambda e, o=dst[c][n].ap, i=X[c][n].ap, g=vcol(gkey, c), rs=RS4[n].ap:
                     e.scalar_tensor_tensor(out=o, in0=i, scalar=g, in1=rs, op0=ALU.mult, op1=ALU.mult),
                     r=[X[c][n], RS4[n], VEC], w=[dst[c][n]])

        def rms_stats():
            pss = []
            k = 0
            for n in range(NT):
                ps = psg.next()
                pss.append(ps)
                for c in range(KC):
                    sq = SQ4[k % 4]
                    k += 1
                    if c % 2 == 0:
                        p.op('act', lambda e, o=sq.ap, i=X[c][n].ap: e.activation(out=o, in_=i, func=AF.Square),
                             r=[X[c][n]], w=[sq])
                    else:
                        p.op('dve', lambda e, o=sq.ap, i=X[c][n].ap: e.tensor_tensor(out=o, in0=i, in1=i, op=ALU.mult),
                             r=[X[c][n]], w=[sq])
                    p.op('pe', lambda e, o=ps.ap, b=sq.ap, st=(c == 0), sp=(c == KC - 1):
                         e.matmul(o, lhsT=ONES.ap, rhs=b, start=st, stop=sp), r=[sq, ONES], w=[ps])
            for n in range(NT):
                rms_lnexp(n, pss[n])

        def attn_tile(n, pairs, PT, RDEN, DG, depth=2):
            items = []
            for pi, pd in enumerate(pairs):
                nb = len(pd['blocks'])
                for hh in range(2):
                    for bi, blk in enumerate(pd['blocks']):
                        items.append((pi, hh, bi, blk, bi == 0, bi == nb - 1))
            acc = {}

            def scores(it):
                pi, hh, bi, (kb, c0, diag), first, last = it
                pd = pairs[pi]
                pr = slice(64 * hh, 64 * hh + 64)
                sps = pss3.next()
                kap, ku = pd['kt'](hh, kb)
                if pd['h0'] is None:
                    qt_unit = pd['q']
                    p.op('pe', lambda e, o=sps.ap[:, c0:TW], a=kap, b=qt_unit.ap[pr, c0:TW]:
                         e.matmul(o, lhsT=a, rhs=b, start=True, stop=True),
                         r=[ku, qt_unit], w=[sps])
                    bias = None
                else:
                    h = pd['h0'] + hh
                    qt_unit = pd['q'][hh]
                    p.op('pe', lambda e, o=sps.ap[:, c0:TW], a=kap, b=qt_unit.ap[:, c0:TW]:
                         e.matmul(o, lhsT=a, rhs=b, start=True, stop=False),
                         r=[ku, qt_unit], w=[sps])
                    p.op('pe', lambda e, o=sps.ap[:, c0:TW], a=SEL.ap[:, h, :],
                         b=CQ.ap[:, n * TW + c0:(n + 1) * TW]:
                         e.matmul(o, lhsT=a, rhs=b, start=False, stop=True),
                         r=[SEL, CQ], w=[sps])
                    bias = NEGCK.ap[:, kb * 12 + h:kb * 12 + h + 1]
                pt = PT.next()
                if diag:
                    dg = DG.next()
                    p.op('dve', lambda e, o=dg.ap, a=sps.ap[:, c0:c0 + 128]:
                         e.tensor_tensor(out=o, in0=a, in1=MASK.ap, op=ALU.add),
                         r=[sps, MASK], w=[dg])
                    p.op('act', lambda e, o=pt.ap[:, c0:c0 + 128], i=dg.ap, b=bias:
                         e.activation(out=o, in_=i, func=AF.Exp, bias=b),
                         r=[dg, NEGCK], w=[pt])
                    if c0 + 128 < TW:
                        p.op('act', lambda e, o=pt.ap[:, c0 + 128:TW], i=sps.ap[:, c0 + 128:TW], b=bias:
                             e.activation(out=o, in_=i, func=AF.Exp, bias=b),
                             r=[sps, NEGCK], w=[pt])
                elif bias is not None:
                    p.op('act', lambda e, o=pt.ap[:, c0:TW], i=sps.ap[:, c0:TW], b=bias:
                         e.activation(out=o, in_=i, func=AF.Exp, bias=b),
                         r=[sps, NEGCK], w=[pt])
                else:
                    p.op('act', lambda e, o=pt.ap[:, c0:TW], i=sps.ap[:, c0:TW]:
                         e.activation(out=o, in_=i, func=AF.Exp),
                         r=[sps], w=[pt])
                return pt

            def pv(it, pt):
                pi, hh, bi, (kb, c0, diag), first, last = it
                pd = pairs[pi]
                pr = slice(64 * hh, 64 * hh + 64)
                fox = pd['h0'] is not None
                if pi not in acc:
                    acc[pi] = dict(den=psacc.next())
                a_ = acc[pi]
                den = a_['den']
                vap, vu = pd['v'](hh, kb)
                out_unit = pd['out']
                if fox:
                    if hh not in a_:
                        a_[hh] = psacc.next()
                    num = a_[hh]
                    p.op('pe', lambda e, o=num.ap[:, c0:TW], a=vap, b=pt.ap[:, c0:TW], st=first, sp=last:
                         e.matmul(o, lhsT=a, rhs=b, start=st, stop=sp), r=[vu, pt], w=[num])
                    p.op('pe', lambda e, o=den.ap[:, c0:TW], a=ONESH[hh].ap, b=pt.ap[:, c0:TW],
                         st=(first and hh == 0), sp=(last and hh == 1):
                         e.matmul(o, lhsT=a, rhs=b, start=st, stop=sp), r=[ONESH[hh], pt], w=[den])
                    if hh == 1 and last:
                        na, nb_ = a_[0], a_[1]
                        p.op('dve', lambda e, d=den: e.reciprocal(out=RDEN.ap, in_=d.ap), r=[den], w=[RDEN])
                        p.op('dve', lambda e, o=out_unit.ap[0:64, :], nm=na: e.tensor_tensor(out=o, in0=nm.ap[0:64, :], in1=RDEN.ap[0:64, :], op=ALU.mult),
                             r=[na, RDEN], w=[out_unit])
                        p.op('dve', lambda e, o=out_unit.ap[64:128, :], nm=nb_: e.tensor_tensor(out=o, in0=nm.ap[64:128, :], in1=RDEN.ap[64:128, :], op=ALU.mult),
                             r=[nb_, RDEN], w=[out_unit])
                else:
                    if 'num' not in a_:
                        a_['num'] = psacc.next()
                    num = a_['num']
                    p.op('pe', lambda e, o=num.ap[pr, c0:TW], a=vap, b=pt.ap[:, c0:TW], st=first, sp=last:
                         e.matmul(o, lhsT=a, rhs=b, start=st, stop=sp), r=[vu, pt], w=[num])
                    p.op('pe', lambda e, o=den.ap[pr, c0:TW], a=ONES.ap[:, 0:64], b=pt.ap[:, c0:TW], st=first, sp=last:
                         e.matmul(o, lhsT=a, rhs=b, start=st, stop=sp), r=[ONES, pt], w=[den])
                    if hh == 1 and last:
                        p.op('dve', lambda e, d=den: e.reciprocal(out=RDEN.ap, in_=d.ap), r=[den], w=[RDEN])
                        p.op('dve', lambda e, o=out_unit.ap, nm=num: e.tensor_tensor(out=o, in0=nm.ap, in1=RDEN.ap, op=ALU.mult),
                             r=[num, RDEN], w=[out_unit])

            pend = []
            for it in items:
                pend.append((it, scores(it)))
                if len(pend) > depth:
                    pv(*pend.pop(0))
            while pend:
                pv(*pend.pop(0))

        def mem_pairs(qsrc, outs):
            return [dict(q=qsrc[jp],
                         kt=(lambda hh, kb, jp=jp: (MEMKT.ap[64 * hh:64 * hh + 64, jp, kb * 128:(kb + 1) * 128], MEMKT)),
                         v=(lambda hh, kb, jp=jp: (MEMV.ap[:, kb, (2 * jp + hh) * 64:(2 * jp + hh + 1) * 64], MEMV)),
                         blocks=[(0, 0, False), (1, 0, False)], h0=None, out=outs[jp]) for jp in range(2)]

        def mem_kv(l):
            wd = ('w_mem_kv', l)

            def cbk(gi, n, pss):
                p.op('act', lambda e, o=MEMKT.ap[:, gi, :], i=pss[0].ap[:, 0:MEM]:
                     e.activation(out=o, in_=i, func=AF.Copy), r=[pss[0]], w=[MEMKT])
            linear([[(wd, j * 128, 128)] for j in range(2)],
                   lambda kc, n: (MEMN.ap[:, kc, :], MEMN), [0], cbk, N=MEM)
            for cj in range(2):
                su, sap = wjob_k(wd, 256 + cj * 128, 128)
                ps = psg.next()
                for mb in range(2):
                    for kc in range(KC):
                        p.op('pe', lambda e, o=ps.ap[:, mb * 128:(mb + 1) * 128],
                             a=MEMN.ap[:, kc, mb * 128:(mb + 1) * 128], b=sap[:, kc, :],
                             st=(kc == 0), sp=(kc == KC - 1):
                             e.matmul(o, lhsT=a, rhs=b, start=st, stop=sp), r=[su, MEMN], w=[ps])
                p.op('act', lambda e, o=MEMV.ap[:, :, cj * 128:(cj + 1) * 128],
                     i=ps.ap[:, 0:256].rearrange("p (b c) -> p b c", b=2):
                     e.activation(out=o, in_=i, func=AF.Copy), r=[ps], w=[MEMV])

        def wout_residual(l, src):
            def cb(gi, n, pss):
                p.op('dve', lambda e, o=X[gi][n].ap, i=pss[0].ap:
                     e.tensor_tensor(out=o, in0=o, in1=i, op=ALU.add), r=[pss[0], X[gi][n]], w=[X[gi][n]])
            linear([[(('w_out', l), dc * 128, 128)] for dc in range(KC)],
                   lambda kc, n: (src[kc][n].ap, src[kc][n]), range(NT), cb)

        def ffn(l):
            rmsnorm_x(B1, ('g_ffn', l))
            wu = ('w_up', l)
            wdn = wbf['w_down'][l]
            groups = [list(range(g, min(g + 4, NFF))) for g in range(0, NFF, 4)]
            cnt = [0]
            pend = [None]
            for grp in groups:
                for gi, f in enumerate(grp):
                    sg_u, sg_ap = wjob_k(wu, f * 128, 128)
                    sv_u, sv_ap = wjob_k(wu, DFF + f * 128, 128)
                    wg = vec_index[('w_dw_f', l)] + f * 3
                    wv = vec_index[('w_dw_f', l)] + (NFF + f) * 3
                    bg = vec_index[('b_dw_f', l)] + f
                    bv = vec_index[('b_dw_f', l)] + NFF + f
                    for n in range(NT):
                        banks = (n, 4 + n)
                        for (su, sap, bk) in ((sg_u, sg_ap, banks[0]), (sv_u, sv_ap, banks[1])):
                            for kc in range(KC):
                                p.op('pe', lambda e, o=PSU[bk].ap, a=sap[:, kc, :], b=B1[kc][n].ap,
                                     st=(kc == 0), sp=(kc == KC - 1):
                                     e.matmul(o, lhsT=a, rhs=b, start=st, stop=sp),
                                     r=[su, B1[kc][n]], w=[PSU[bk]])
                        k = cnt[0]
                        cnt[0] += 1
                        tg = TG[k % 3]
                        tv = TV[k % 3]
                        paths = ((banks[0], wg, bg, tg), (banks[1], wv, bv, tv))
                        for (bk, wb, bb, t) in paths:
                            p.op('act', lambda e, o=t.ap, i=PSU[bk].ap, s_=VEC.ap[:, wb + 2:wb + 3], b_=VEC.ap[:, bb:bb + 1]:
                                 e.activation(out=o, in_=i, func=AF.Identity, bias=b_, scale=s_),
                                 r=[PSU[bk], VEC], w=[t])
                        if pend[0] is not None:
                            pend[0]()
                        for tap, sh in ((1, 1), (0, 2)):
                            for (bk, wb, bb, t) in paths:
                                if n == 0:
                                    p.op('dve', lambda e, o=t.ap[:, sh:TW], i=PSU[bk].ap[:, 0:TW - sh], s_=VEC.ap[:, wb + tap:wb + tap + 1]:
                                         e.scalar_tensor_tensor(out=o, in0=i, scalar=s_, in1=o, op0=ALU.mult, op1=ALU.add),
                                         r=[PSU[bk], t, VEC], w=[t])
                                else:
                                    p.op('dve', lambda e, o=t.ap, i=psbig[:, bk * 512 - sh:bk * 512 - sh + TW], s_=VEC.ap[:, wb + tap:wb + tap + 1]:
                                         e.scalar_tensor_tensor(out=o, in0=i, scalar=s_, in1=o, op0=ALU.mult, op1=ALU.add),
                                         r=[PSU[bk], PSU[bk - 1], t, VEC], w=[t])

                        def tail(tg=tg, tv=tv, dst=ACTB[gi][n]):
                            p.op('act', lambda e, o=tg.ap: e.activation(out=o, in_=o, func=AF.Silu), r=[tg], w=[tg])
                            p.op('dve', lambda e, o=dst.ap, a=tg.ap, b=tv.ap:
                                 e.tensor_tensor(out=o, in0=a, in1=b, op=ALU.mult), r=[tg, tv], w=[dst])
                        pend[0] = tail
                pend[0]()
                pend[0] = None
                dsl = []
                for f in grp:
                    s = slots.next()
                    p.op('sp', lambda e, d=s.ap, s_=wdn[f * 128:(f + 1) * 128, :]: e.dma_start(out=d, in_=s_),
                         r=wunits(('w_down', l)), w=[s], dma=True)
                    dsl.append(s)
                last_grp = (grp is groups[-1])
                ring = ps6 if last_grp else psall
                sqk = 0
                for n in range(NT):
                    stat = PSU[6 + n % 2]
                    pend_q = []
                    for dc in range(KC):
                        ps = ring.next()
                        for gi, s in enumerate(dsl):
                            p.op('pe', lambda e, o=ps.ap, a=s.ap[:, dc * 128:(dc + 1) * 128], b=ACTB[gi][n].ap,
                                 st=(gi == 0), sp=(gi == len(dsl) - 1):
                                 e.matmul(o, lhsT=a, rhs=b, start=st, stop=sp), r=[s, ACTB[gi][n]], w=[ps])
                        while len(pend_q) > 1:
                            pend_q.pop(0)()
                        p.op('dve', lambda e, o=X[dc][n].ap, i=ps.ap:
                             e.tensor_tensor(out=o, in0=o, in1=i, op=ALU.add), r=[ps, X[dc][n]], w=[X[dc][n]])
                        if last_grp:
                            sq = SQ4[sqk % 4]
                            sqk += 1
                            p.op('act', lambda e, o=sq.ap, i=X[dc][n].ap: e.activation(out=o, in_=i, func=AF.Square),
                                 r=[X[dc][n]], w=[sq])

                            def pend_stat(sq=sq, dc=dc, stat=stat):
                                p.op('pe', lambda e, o=stat.ap, b=sq.ap, st=(dc == 0), sp=(dc == KC - 1):
                                     e.matmul(o, lhsT=ONES.ap, rhs=b, start=st, stop=sp), r=[sq, ONES], w=[stat])
                            pend_q.append(pend_stat)
                    while pend_q:
                        pend_q.pop(0)()
                    if last_grp:
                        rms_lnexp(n, stat)
                if last_grp:
                    fused_stats[0] = True

        def mixer_a(l, hook):
            rmsnorm_x(B1, ('g_mix', l))
            wd = ('w_in_a', l)
            bgl = vec_index[('b_glu', l)]
            sgc = [0]

            def cb_glu(gi, n, pss):
                sg = SG[sgc[0] % 2]
                sgc[0] += 1
                p.op('act', lambda e, o=sg.ap, i=pss[1].ap, b=VEC.ap[:, bgl + 6 + gi:bgl + 7 + gi]:
                     e.activation(out=o, in_=i, func=AF.Sigmoid, bias=b), r=[pss[1], VEC], w=[sg])
                p.op('dve', lambda e, o=B2[gi][n].ap, i=pss[0].ap, s_=VEC.ap[:, bgl + gi:bgl + gi + 1], b=sg.ap:
                     e.scalar_tensor_tensor(out=o, in0=i, scalar=s_, in1=b, op0=ALU.add, op1=ALU.mult),
                     r=[pss[0], sg, VEC], w=[B2[gi][n]])
            linear([[(wd, j * 128, 128), (wd, 768 + j * 128, 128)] for j in range(6)],
                   lambda kc, n: (B1[kc][n].ap, B1[kc][n]), range(NT), cb_glu)
            hook([B2[5][NT - 1]])
            mem_kv(l)

            def cb_q(gi, n, pss):
                p.op('act', lambda e, o=B2[6 + gi][n].ap, i=pss[0].ap:
                     e.activation(out=o, in_=i, func=AF.Copy, scale=0.125), r=[pss[0]], w=[B2[6 + gi][n]])
            linear([[(wd, 1536 + j * 128, 128)] for j in range(2)],
                   lambda kc, n: (B1[kc][n].ap, B1[kc][n]), range(NT), cb_q)

            PT = Ring(PTA)
            wb = vec_index[('w_dw_a', l)]
            dgc = [0]
            allp = []
            for n in range(NT):
                allp += mem_pairs([B2[6][n], B2[7][n]], [B1[6][n], B1[7][n]])
            attn_tile(0, allp, PT, RDENA, None)
            def build_diag(j_, slot):
                dgu = DIAG[slot % 2]
                p.op('dve', lambda e, o=dgu.ap, i0=IDENT.ap.unsqueeze(1).to_broadcast([128, 31, 128]),
                     i1=VEC.ap[:, wb + j_ * 31:wb + j_ * 31 + 31].unsqueeze(2).to_broadcast([128, 31, 128]):
                     e.tensor_tensor(out=o, in0=i0, in1=i1, op=ALU.mult),
                     r=[IDENT, VEC], w=[dgu])
            build_diag(0, 0)
            for n in range(NT):
                for j in range(6):
                    dg = DIAG[dgc[0] % 2]
                    dgc[0] += 1
                    nxt_emitted = [False]

                    def emit_next(j=j):
                        if not nxt_emitted[0] and not (n == NT - 1 and j == 5):
                            build_diag((j + 1) % 6, dgc[0])
                        nxt_emitted[0] = True
                    ps = psg.next()
                    order = [30] + list(range(30))
                    for ki, k in enumerate(order):
                        off = n * TW - 30 + k
                        o0 = 0
                        if off < 0:
                            o0 = -off
                            off = 0
                        ln_ = TW - o0
                        ru = [B2[j][n]] + ([B2[j][n - 1]] if n > 0 else [])
                        p.op('pe', lambda e, o=ps.ap[:, o0:TW], a=dg.ap[:, k, :], b=B2c[j][:, off:off + ln_],
                             st=(ki == 0), sp=(ki == 30):
                             e.matmul(o, lhsT=a, rhs=b, start=st, stop=sp), r=[dg] + ru, w=[ps])
                    emit_next()
                    bd = vcol(('b_dw_a', l), j)
                    p.op('act', lambda e, o=Y[j].ap, i=ps.ap, b=bd: e.activation(out=o, in_=i, func=AF.Identity, bias=b),
                         r=[ps, VEC], w=[Y[j]])
                    p.op('act', lambda e, o=YB[j].ap, i=ps.ap, b=bd: e.activation(out=o, in_=i, func=AF.Identity, bias=b),
                         r=[ps, VEC], w=[YB[j]])
                    p.op('act', lambda e, o=YSQ[j].ap, i=ps.ap, b=bd: e.activation(out=o, in_=i, func=AF.Square, bias=b),
                         r=[ps, VEC], w=[YSQ[j]])
                s1 = psg.next()
                s2 = psg.next()
                for j in range(6):
                    p.op('pe', lambda e, b=YB[j].ap, st=(j == 0), sp=(j == 5):
                         e.matmul(s1.ap, lhsT=ONES.ap, rhs=b, start=st, stop=sp), r=[YB[j], ONES], w=[s1])
                for j in range(6):
                    p.op('pe', lambda e, b=YSQ[j].ap, st=(j == 0), sp=(j == 5):
                         e.matmul(s2.ap, lhsT=ONES.ap, rhs=b, start=st, stop=sp), r=[YSQ[j], ONES], w=[s2])
                p.op('dve', lambda e: e.tensor_scalar(out=MU.ap, in0=s1.ap, scalar1=1.0 / 768, scalar2=None, op0=ALU.mult),
                     r=[s1], w=[MU])
                p.op('dve', lambda e: e.tensor_tensor(out=T2.ap, in0=MU.ap, in1=MU.ap, op=ALU.mult), r=[MU], w=[T2])
                p.op('dve', lambda e: e.scalar_tensor_tensor(out=T2.ap, in0=s2.ap, scalar=1.0 / 768, in1=T2.ap,
                                                            op0=ALU.mult, op1=ALU.subtract), r=[s2, T2], w=[T2])
                p.op('act', lambda e: e.activation(out=T2.ap, in_=T2.ap, func=AF.Ln, bias=vcol('eps_ln')),
                     r=[T2, VEC], w=[T2])
                p.op('act', lambda e: e.activation(out=RSTL.ap, in_=T2.ap, func=AF.Exp, scale=-0.5), r=[T2], w=[RSTL])
                for j in range(6):
                    p.op('dve', lambda e, o=Y[j].ap: e.tensor_tensor(out=o, in0=o, in1=MU.ap, op=ALU.subtract),
                         r=[Y[j], MU], w=[Y[j]])
                    p.op('dve', lambda e, o=Y[j].ap: e.tensor_tensor(out=o, in0=o, in1=RSTL.ap, op=ALU.mult),
                         r=[Y[j], RSTL], w=[Y[j]])
                    p.op('act', lambda e, o=B1[j][n].ap, i=Y[j].ap, s_=vcol(('ln_g', l), j), b=vcol(('ln_b', l), j):
                         e.activation(out=o, in_=i, func=AF.Silu, bias=b, scale=s_),
                         r=[Y[j], VEC], w=[B1[j][n]])
            wout_residual(l, B1)

        def kv_pre():
            p.op('pool', lambda e: e.dma_start(out=SEL.ap, in_=cst_d[:, 268:268 + 1536].rearrange("p (h m) -> p h m", h=12)),
                 w=[SEL], dma=True)
            p.op('pool', lambda e: e.dma_start(out=MASK.ap, in_=cst_d[:, 0:128]), w=[MASK], dma=True)
            rmsnorm_x(B1, 'g_kv')

            def cb_f(gi, n, pss):
                p.op('act', lambda e, o=LF.ap[0:12, n * TW:(n + 1) * TW], i=pss[0].ap[0:12, :]:
                     e.activation(out=o, in_=i, func=AF.Sigmoid, bias=vcol('b_f', rows=12)),
                     r=[pss[0], VEC], w=[LF])
                p.op('act', lambda e, o=LF.ap[0:12, n * TW:(n + 1) * TW]:
                     e.activation(out=o, in_=o, func=AF.Ln), r=[LF], w=[LF])
            linear([[(('w_kvf', None), 1536, 12)]], lambda kc, n: (B1[kc][n].ap, B1[kc][n]), range(NT), cb_f)
            p.op('dve', lambda e: e.memset(ONESF.ap[0:12, :], 1.0), w=[ONESF])
            p.op('dve', lambda e: e.tensor_tensor_scan(out=CC.ap[0:12, :], data0=ONESF.ap[0:12, :], data1=LF.ap[0:12, :],
                                                       initial=0.0, op0=ALU.mult, op1=ALU.add),
                 r=[ONESF, LF], w=[CC])
            ps = psg.next()
            for kb in range(16):
                p.op('pe', lambda e, o=ps.ap[:, kb * 12:(kb + 1) * 12], i=CC.ap[0:12, kb * 128:(kb + 1) * 128]:
                     e.transpose(o, i, ID12.ap[0:12, 0:12]), r=[CC, ID12], w=[ps])
            p.op('act', lambda e: e.activation(out=NEGCK.ap, in_=ps.ap[:, 0:192], func=AF.Copy, scale=-1.0),
                 r=[ps], w=[NEGCK])
            R1 = ONESF
            p.op('dve', lambda e: e.tensor_copy(out=HI.ap[0:12, :], in_=CC.ap[0:12, :]), r=[CC], w=[HI])
            p.op('dve', lambda e: e.tensor_tensor(out=R1.ap[0:12, :], in0=CC.ap[0:12, :], in1=HI.ap[0:12, :], op=ALU.subtract),
                 r=[CC, HI], w=[R1])
            p.op('dve', lambda e: e.tensor_copy(out=MID.ap[0:12, :], in_=R1.ap[0:12, :]), r=[R1], w=[MID])
            p.op('dve', lambda e: e.tensor_tensor(out=R1.ap[0:12, :], in0=R1.ap[0:12, :], in1=MID.ap[0:12, :], op=ALU.subtract),
                 r=[R1, MID], w=[R1])
            p.op('dve', lambda e: e.tensor_copy(out=LO.ap[0:12, :], in_=R1.ap[0:12, :]), r=[R1], w=[LO])
            p.op('dve', lambda e: e.memset(CQ.ap, 0.0), w=[CQ])
            for pc, pu in enumerate((HI, MID, LO)):
                p.op('pool', lambda e, o=CQ.ap[12 * pc:12 * pc + 12, :], i=pu.ap[0:12, :]:
                     e.dma_start(out=o, in_=i), r=[pu], w=[CQ], dma=True)
            def cb_k(gi, n, pss):
                p.op('act', lambda e, o=KT[gi][n].ap, i=pss[0].ap: e.activation(out=o, in_=i, func=AF.Copy),
                     r=[pss[0]], w=[KT[gi][n]])
            linear([[(('w_kvf', None), j * 128, 128)] for j in range(6)],
                   lambda kc, n: (B1[kc][n].ap, B1[kc][n]), range(NT), cb_k)
            for cj in range(6):
                su, sap = wjob_k(('w_kvf', None), 768 + cj * 128, 128)
                for n in range(NT):
                    ps = psg.next()
                    for tb in range(4):
                        for kc in range(KC):
                            p.op('pe', lambda e, o=ps.ap[:, tb * 128:(tb + 1) * 128],
                                 a=B1[kc][n].ap[:, tb * 128:(tb + 1) * 128], b=sap[:, kc, :],
                                 st=(kc == 0), sp=(kc == KC - 1):
                                 e.matmul(o, lhsT=a, rhs=b, start=st, stop=sp), r=[su, B1[kc][n]], w=[ps])
                    p.op('act', lambda e, o=Vall[:, 4 * n:4 * n + 4, cj * 128:(cj + 1) * 128],
                         i=ps.ap.rearrange("p (b c) -> p b c", b=4):
                         e.activation(out=o, in_=i, func=AF.Copy), r=[ps], w=[V[4 * n + t] for t in range(4)])


        def mixer_b(l, hook):
            for j in range(6):
                p.op('dve', lambda e, j=j: e.memset(QTA[j].ap, 0.0), w=[QTA[j]])
                p.op('dve', lambda e, j=j: e.memset(QTB[j].ap, 0.0), w=[QTB[j]])
            rmsnorm_x(B1, ('g_mix', l))
            mem_kv(l)
            wd = ('w_in_b', l - 2)
            PT = Ring(PTB)
            DG = Ring(DGB)
            for n in range(NT):
                def cb_q(gi, n_, pss):
                    if gi < 6:
                        p.op('act', lambda e, o=QTA[gi].ap[0:64, :], i=pss[0].ap[0:64, :]:
                             e.activation(out=o, in_=i, func=AF.Copy, scale=0.125), r=[pss[0]], w=[QTA[gi]])
                        p.op('act', lambda e, o=QTB[gi].ap[64:128, :], i=pss[0].ap[64:128, :]:
                             e.activation(out=o, in_=i, func=AF.Copy, scale=0.125), r=[pss[0]], w=[QTB[gi]])
                    else:
                        p.op('act', lambda e, o=QTM[gi - 6].ap, i=pss[0].ap:
                             e.activation(out=o, in_=i, func=AF.Copy, scale=0.125), r=[pss[0]], w=[QTM[gi - 6]])
                linear([[(wd, c * 128, 128)] for c in range(KC)],
                       lambda kc, n_: (B1[kc][n_].ap, B1[kc][n_]), [n], cb_q, ring=pss3)
                if n == 0:
                    hook([QTM[1]])

                def blocks(n=n):
                    out = []
                    for kb in range(4 * n + 4):
                        i = kb - 4 * n
                        out.append((kb, max(0, i) * 128, i >= 0))
                    return out
                pairs = [dict(q=(QTA[j], QTB[j]),
                              kt=(lambda hh, kb, j=j: (KTc[j][:, kb * 128:(kb + 1) * 128], KT[j][kb // 4])),
                              v=(lambda hh, kb, j=j: (Vall[:, kb, j * 128:(j + 1) * 128], V[kb])),
                              blocks=blocks(), h0=2 * j, out=B1[j][n]) for j in range(6)]
                pairs += mem_pairs([QTM[0], QTM[1]], [B1[6][n], B1[7][n]])
                attn_tile(n, pairs, PT, RDENB, DG)
            wout_residual(l, B1)

        p.op('sp', lambda e: e.dma_start(out=VEC.ap, in_=vec_d), w=[VEC], dma=True)
        p.op('sp', lambda e: e.dma_start(out=MEMX.ap, in_=memT_d.rearrange("(c p) m -> p c m", p=128)),
             w=[MEMX], dma=True)
        Xall = view(OX, 65536, F32, "p (c t) -> p c t", c=KC)
        for n in range(NT):
            p.op('sp', lambda e, o=Xall[:, :, n * TW:(n + 1) * TW],
                 i=xT_d.rearrange("(c p) t -> p c t", p=128)[:, :, n * TW:(n + 1) * TW]: e.dma_start(out=o, in_=i),
                 w=[X[c][n] for c in range(KC)], dma=True)
        p.op('sp', lambda e: e.dma_start(out=IDENT.ap, in_=cst_d[:, 128:256]), w=[IDENT], dma=True)
        p.op('sp', lambda e: e.dma_start(out=ID12.ap[:, 0:12], in_=cst_d[:, 256:268]), w=[ID12], dma=True)
        p.op('dve', lambda e: e.memset(ONES.ap, 1.0), w=[ONES])
        for i in range(2):
            p.op('dve', lambda e, i=i: e.memset(ONESH[i].ap[:, 64 * i:64 * i + 64], 1.0), w=[ONESH[i]])
            p.op('dve', lambda e, i=i: e.memset(ONESH[i].ap[:, 64 * (1 - i):64 * (1 - i) + 64], 0.0), w=[ONESH[i]])
        rms_tile([(MEMX.ap[:, c, :], MEMX) for c in range(KC)],
                 [(MEMN.ap[:, c, :], MEMN) for c in range(KC)], 'g_mem', MEM)

        precast = {
            'A0': [('w_out', 0), ('w_up', 0), ('w_down', 0), ('w_mem_kv', 1), ('w_in_a', 1), ('w_out', 1)],
            'A1': [('w_up', 1), ('w_down', 1), ('w_kvf', None), ('w_mem_kv', 2), ('w_in_b', 0), ('w_out', 2)],
            'B2': [('w_up', 2), ('w_down', 2), ('w_mem_kv', 3), ('w_in_b', 1), ('w_out', 3)],
            'B3': [('w_up', 3), ('w_down', 3)],
        }

        def mk_hook(name):
            def hook(after):
                for i, wn in enumerate(precast.get(name, [])):
                    cast_w(*wn, after=(after if i == 0 else ()))
            return hook
        cast_w('w_in_a', 0, step=1024)
        cast_w('w_mem_kv', 0)
        stages = []
        for l in range(2):
            stages.append((f"A{l}", lambda l=l: mixer_a(l, mk_hook(f"A{l}"))))
            stages.append((f"F{l}", lambda l=l: ffn(l)))
        stages.append(("KV", kv_pre))
        for l in range(2, 4):
            stages.append((f"B{l}", lambda l=l: mixer_b(l, mk_hook(f"B{l}"))))
            stages.append((f"F{l}", lambda l=l: ffn(l)))
        final_norm = True
        for name, fn in stages:
            if stop is not None and stop == 'init':
                final_norm = False
                break
            fn()
            if stop is not None and name == stop:
                final_norm = False
                break
        if final_norm:
            rmsnorm_x(X, 'g_final', c_outer=True)
        OUT = [p.unit(f"OUT{c}") for c in range(KC)]
        for c in range(KC):
            p.op('sp', lambda e, i=Xc[c], o=outT_d[c * 128:(c + 1) * 128, :]: e.dma_start(out=o, in_=i),
                 r=X[c], w=[OUT[c]], dma=True)
        p.op('sp', None, r=OUT)
        p.finalize()

        import contextlib
        with contextlib.ExitStack() as es:
            sems = {e: es.enter_context(nc.semaphore(f"s_{e}")) for e in ENGS}
            dsems = {e: [es.enter_context(nc.semaphore(f"d_{e}{i}")) for i in range(Prog.R)]
                     for e in ('sp', 'pool')}
            block = es.enter_context(nc.Block())

            @block.tensor
            def _(h):
                p.emit('pe', h, sems, dsems)

            @block.scalar
            def _(h):
                p.emit('act', h, sems, dsems)

            @block.vector
            def _(h):
                p.emit('dve', h, sems, dsems)

            @block.gpsimd
            def _(h):
                p.emit('pool', h, sems, dsems)

            @block.sync
            def _(h):
                p.emit('sp', h, sems, dsems)
        for t in reversed(pst):
            t.__exit__(None, None, None)
    return nc, p


_CACHE = {}


def kernel(x, mem, g_mix, w_in_a, b_glu, w_dw_a, b_dw_a, ln_g, ln_b, g_kv, w_kvf, b_f,
           w_in_b, g_mem, w_mem_kv, w_out, g_ffn, w_up, w_dw_f, b_dw_f, w_down, g_final,
           _stop=None, _ncores=8):
    inp = dict(g_mix=g_mix, b_glu=b_glu, w_dw_a=w_dw_a, b_dw_a=b_dw_a, ln_g=ln_g, ln_b=ln_b,
               g_kv=g_kv, b_f=b_f, g_mem=g_mem, g_ffn=g_ffn, w_dw_f=w_dw_f, b_dw_f=b_dw_f,
               g_final=g_final)
    inp = {k: np.asarray(v, np.float32) for k, v in inp.items()}
    vec, vindex = _vec_layout(inp)
    cst = _consts()
    key = (_stop,)
    if key not in _CACHE:
        _CACHE[key] = build(vindex, vec.shape[1], stop=_stop)[0]
    nc = _CACHE[key]
    x = np.asarray(x, np.float32)
    mem = np.asarray(mem, np.float32)
    shared = dict(
        vec=vec, cst=cst,
        w_in_a=np.ascontiguousarray(np.asarray(w_in_a, np.float32)),
        w_kvf=np.ascontiguousarray(np.asarray(w_kvf, np.float32)),
        w_in_b=np.ascontiguousarray(np.asarray(w_in_b, np.float32)),
        w_mem_kv=np.ascontiguousarray(np.asarray(w_mem_kv, np.float32)),
        w_out=np.ascontiguousarray(np.asarray(w_out, np.float32)),
        w_up=np.ascontiguousarray(np.asarray(w_up, np.float32)),
        w_down=np.ascontiguousarray(np.asarray(w_down, np.float32)),
    )
    in_maps = []
    for b in range(_ncores):
        m = dict(shared)
        m["xT"] = np.ascontiguousarray(x[b].T)
        m["memT"] = np.ascontiguousarray(mem[b].T)
        in_maps.append(m)
    res = run_bass_kernel_spmd(nc, in_maps, core_ids=list(range(_ncores)))
    out = np.stack([np.ascontiguousarray(np.asarray(r["outT"], np.float32).T) for r in res.results], axis=0)
    return out
```
